# Optimizing a Trainium2 kernel written in Bass

```python
import jax, jax.numpy as jnp
from jax import lax
import numpy as np

D_MODEL = 1024
BATCH = 4
SEQ = 8192
DEPTH = 1
DEC_BATCH = 128
DEC_SEQ = 1
PAST_LEN = 16384
PAGE_SIZE = 128

N_HEADS = 16
N_KV_HEADS = 4
HEAD_DIM = 64
GROUP = N_HEADS // N_KV_HEADS
WINDOW = 128
BLOCK = WINDOW
ROPE_THETA = 10000.0
C_CONV = D_MODEL
CONV_W = 31
D_FF = 4 * D_MODEL
D_PLE = 256
EPS = 1e-6
NEG = -1e30
Q_W = N_HEADS * HEAD_DIM
KV_W = N_KV_HEADS * HEAD_DIM
D_IN = Q_W + 2 * KV_W + 2 * C_CONV + 2 * D_MODEL

kernel_name = "hybrid_conformer_swa_sink_decoder_step"


def rmsnorm(x, g):
    xf = x.astype(jnp.float32)
    y = xf * lax.rsqrt(jnp.mean(xf * xf, axis=-1, keepdims=True) + EPS)
    return (y * g.astype(jnp.float32)).astype(x.dtype)


def layernorm(x, g, b):
    xf = x.astype(jnp.float32)
    mu = jnp.mean(xf, axis=-1, keepdims=True)
    var = jnp.mean(jnp.square(xf - mu), axis=-1, keepdims=True)
    y = (xf - mu) * lax.rsqrt(var + EPS) * g.astype(jnp.float32) + b.astype(jnp.float32)
    return y.astype(x.dtype)


def rope(x, pos):
    half = HEAD_DIM // 2
    inv = jnp.power(jnp.float32(ROPE_THETA), -jnp.arange(half, dtype=jnp.float32) / half)
    ang = pos.astype(jnp.float32)[:, None] * inv[None, :]
    cos = jnp.cos(ang)[None, :, None, :]
    sin = jnp.sin(ang)[None, :, None, :]
    xf = x.astype(jnp.float32)
    x1, x2 = xf[..., :half], xf[..., half:]
    return jnp.concatenate([x1 * cos - x2 * sin, x2 * cos + x1 * sin], axis=-1).astype(x.dtype)


def sink_attention(q, k, v, mask, sinks):
    s = jnp.einsum('bnqhgd,bnkhd->bnhgqk', q, k, preferred_element_type=jnp.float32) * (HEAD_DIM ** -0.5)
    s = jnp.where(mask[None, :, None, None], s, NEG)
    sk = sinks.astype(jnp.float32).reshape(N_KV_HEADS, GROUP)[None, None, :, :, None, None]
    m = jnp.maximum(jnp.max(s, axis=-1, keepdims=True), sk)
    e = jnp.exp(s - m)
    den = jnp.sum(e, axis=-1, keepdims=True) + jnp.exp(sk - m)
    pr = (e / den).astype(v.dtype)
    return jnp.einsum('bnhgqk,bnkhd->bnqhgd', pr, v)


def branch_inputs(x, pos, lp):
    B, T = x.shape[0], x.shape[1]
    u = rmsnorm(x, lp['ln1'])
    z = u @ lp['w_in']
    q, k, v, glu, gts = jnp.split(z, [Q_W, Q_W + KV_W, Q_W + 2 * KV_W, Q_W + 2 * KV_W + 2 * C_CONV], axis=-1)
    q = rope(rmsnorm(q.reshape(B, T, N_HEADS, HEAD_DIM), lp['q_norm']), pos)
    k = rope(rmsnorm(k.reshape(B, T, N_KV_HEADS, HEAD_DIM), lp['k_norm']), pos)
    v = v.reshape(B, T, N_KV_HEADS, HEAD_DIM)
    glu = glu + lp['b_glu']
    a = glu[..., :C_CONV] * jax.nn.sigmoid(glu[..., C_CONV:])
    return q, k, v, a, gts


def conv_branch(a_hist, lp):
    y = lax.conv_general_dilated(a_hist, lp['conv_dw'][:, None, :], (1,), 'VALID',
                                 dimension_numbers=('NWC', 'WIO', 'NWC'),
                                 feature_group_count=C_CONV) + lp['conv_dw_b']
    y = jax.nn.silu(layernorm(y, lp['conv_ln_g'], lp['conv_ln_b']))
    return y @ lp['w_conv_out'] + lp['b_conv_out']


def finish(x, attn_o, conv_o, gts, p, lp):
    g_attn, g_conv = jnp.split(gts, 2, axis=-1)
    mixed = jax.nn.sigmoid(g_attn) * (attn_o @ lp['w_o_attn']) + jax.nn.sigmoid(g_conv) * conv_o
    h = x + mixed @ lp['w_out']
    h = h + jnp.square(jax.nn.relu(rmsnorm(h, lp['ln2']) @ lp['w_ff1'])) @ lp['w_ff2']
    gate = jax.nn.sigmoid(rmsnorm(h, lp['ln_ple']) @ lp['w_ple_gate'])
    return h + gate * (p @ lp['w_ple'])


def prompt_layer(x, p, lp):
    B, S = x.shape[0], x.shape[1]
    pos = jnp.arange(S, dtype=jnp.int32)
    q, k, v, a, gts = branch_inputs(x, pos, lp)
    nb = S // BLOCK
    qb = q.reshape(B, nb, BLOCK, N_KV_HEADS, GROUP, HEAD_DIM)
    kb = k.reshape(B, nb, BLOCK, N_KV_HEADS, HEAD_DIM)
    vb = v.reshape(B, nb, BLOCK, N_KV_HEADS, HEAD_DIM)
    shift = ((0, 0), (1, 0), (0, 0), (0, 0), (0, 0))
    kc = jnp.concatenate([jnp.pad(kb[:, :-1], shift), kb], axis=2)
    vc = jnp.concatenate([jnp.pad(vb[:, :-1], shift), vb], axis=2)
    i = jnp.arange(BLOCK)[:, None]
    j = jnp.arange(2 * BLOCK)[None, :]
    rel = i - j + BLOCK
    band = (rel >= 0) & (rel < WINDOW)
    mask = band[None] & ((jnp.arange(nb)[:, None, None] > 0) | (j[None] >= BLOCK))
    attn_o = sink_attention(qb, kc, vc, mask, lp['sinks']).reshape(B, S, Q_W)
    a_hist = jnp.pad(a, ((0, 0), (CONV_W - 1, 0), (0, 0)))
    conv_o = conv_branch(a_hist, lp)
    y = finish(x, attn_o, conv_o, gts, p, lp)
    wb = min(WINDOW, S)
    return y, k[:, S - wb:], v[:, S - wb:], a[:, S - (CONV_W - 1):]


def sample_layer(x, p, ck, cv, cs, lp):
    Bd, T = x.shape[0], x.shape[1]
    wb = ck.shape[1]
    pos = PAST_LEN + jnp.arange(T, dtype=jnp.int32)
    q, k, v, a, gts = branch_inputs(x, pos, lp)
    kc = jnp.concatenate([ck.astype(k.dtype), k], axis=1)
    vc = jnp.concatenate([cv.astype(v.dtype), v], axis=1)
    kpos = PAST_LEN - wb + jnp.arange(wb + T, dtype=jnp.int32)
    rel = pos[:, None] - kpos[None, :]
    mask = ((rel >= 0) & (rel < WINDOW) & (kpos[None, :] >= 0))[None]
    qb = q.reshape(Bd, 1, T, N_KV_HEADS, GROUP, HEAD_DIM)
    attn_o = sink_attention(qb, kc[:, None], vc[:, None], mask, lp['sinks']).reshape(Bd, T, Q_W)
    a_hist = jnp.concatenate([cs.astype(a.dtype), a], axis=1)
    conv_o = conv_branch(a_hist, lp)
    y = finish(x, attn_o, conv_o, gts, p, lp)
    return y, kc[:, T:], vc[:, T:], a_hist[:, T:]


def setup_inputs(seed: int = 0) -> dict:
    key = jax.random.key(seed)
    ks = iter(jax.random.split(key, 40))
    f32 = jnp.float32

    def nrm(shape, scale):
        return jax.random.normal(next(ks), shape, f32) * scale

    L = DEPTH
    wb = min(WINDOW, PAST_LEN)
    return {
        "x_prompt": nrm((BATCH, SEQ, D_MODEL), 1.0),
        "x_sample": nrm((DEC_BATCH, DEC_SEQ, D_MODEL), 1.0),
        "cache_k": nrm((L, DEC_BATCH, wb, N_KV_HEADS, HEAD_DIM), 1.0),
        "cache_v": nrm((L, DEC_BATCH, wb, N_KV_HEADS, HEAD_DIM), 1.0),
        "state_conv": nrm((L, DEC_BATCH, CONV_W - 1, C_CONV), 0.5),
        "p_prompt": nrm((L, BATCH, SEQ, D_PLE), 1.0),
        "p_sample": nrm((L, DEC_BATCH, DEC_SEQ, D_PLE), 1.0),
        "ln1": 1.0 + nrm((L, D_MODEL), 0.02),
        "w_in": nrm((L, D_MODEL, D_IN), D_MODEL ** -0.5),
        "b_glu": nrm((L, 2 * C_CONV), 0.02),
        "q_norm": 1.0 + nrm((L, HEAD_DIM), 0.02),
        "k_norm": 1.0 + nrm((L, HEAD_DIM), 0.02),
        "sinks": nrm((L, N_HEADS), 0.5),
        "w_o_attn": nrm((L, Q_W, D_MODEL), Q_W ** -0.5),
        "conv_dw": nrm((L, CONV_W, C_CONV), CONV_W ** -0.5),
        "conv_dw_b": nrm((L, C_CONV), 0.02),
        "conv_ln_g": 1.0 + nrm((L, C_CONV), 0.02),
        "conv_ln_b": nrm((L, C_CONV), 0.02),
        "w_conv_out": nrm((L, C_CONV, D_MODEL), C_CONV ** -0.5),
        "b_conv_out": nrm((L, D_MODEL), 0.02),
        "w_out": nrm((L, D_MODEL, D_MODEL), D_MODEL ** -0.5),
        "ln2": 1.0 + nrm((L, D_MODEL), 0.02),
        "w_ff1": nrm((L, D_MODEL, D_FF), D_MODEL ** -0.5),
        "w_ff2": nrm((L, D_FF, D_MODEL), D_FF ** -0.5),
        "ln_ple": 1.0 + nrm((L, D_MODEL), 0.02),
        "w_ple_gate": nrm((L, D_MODEL, D_MODEL), D_MODEL ** -0.5),
        "w_ple": nrm((L, D_PLE, D_MODEL), D_PLE ** -0.5),
    }


def reference(x_prompt, x_sample, cache_k, cache_v, state_conv, p_prompt, p_sample,
              ln1, w_in, b_glu, q_norm, k_norm, sinks, w_o_attn, conv_dw, conv_dw_b,
              conv_ln_g, conv_ln_b, w_conv_out, b_conv_out, w_out, ln2, w_ff1, w_ff2,
              ln_ple, w_ple_gate, w_ple):
    xp, xs = x_prompt, x_sample
    kp_l, vp_l, cp_l, ks_l, vs_l, cs_l = [], [], [], [], [], []
    for i in range(DEPTH):
        lp = dict(ln1=ln1[i], w_in=w_in[i], b_glu=b_glu[i], q_norm=q_norm[i], k_norm=k_norm[i],
                  sinks=sinks[i], w_o_attn=w_o_attn[i], conv_dw=conv_dw[i], conv_dw_b=conv_dw_b[i],
                  conv_ln_g=conv_ln_g[i], conv_ln_b=conv_ln_b[i], w_conv_out=w_conv_out[i],
                  b_conv_out=b_conv_out[i], w_out=w_out[i], ln2=ln2[i], w_ff1=w_ff1[i],
                  w_ff2=w_ff2[i], ln_ple=ln_ple[i], w_ple_gate=w_ple_gate[i], w_ple=w_ple[i])
        xp, kp, vp, cp = prompt_layer(xp, p_prompt[i], lp)
        xs, ks_, vs_, cs_ = sample_layer(xs, p_sample[i], cache_k[i], cache_v[i], state_conv[i], lp)
        kp_l.append(kp); vp_l.append(vp); cp_l.append(cp)
        ks_l.append(ks_); vs_l.append(vs_); cs_l.append(cs_)
    new_k_prompt = jnp.stack(kp_l)
    new_v_prompt = jnp.stack(vp_l)
    new_conv_prompt = jnp.stack(cp_l)
    new_k_sample = jnp.stack(ks_l)
    new_v_sample = jnp.stack(vs_l)
    new_conv_sample = jnp.stack(cs_l)
    return (xp, xs, new_k_prompt, new_v_prompt, new_conv_prompt, new_k_sample, new_v_sample, new_conv_sample)
```

```python
import numpy as np
from contextlib import ExitStack
import concourse.bass as bass
import concourse.mybir as mybir
from concourse.bass_utils import run_bass_kernel_spmd

F32 = mybir.dt.float32
BF16 = mybir.dt.bfloat16
AF = mybir.ActivationFunctionType
ALU = mybir.AluOpType
AX = mybir.AxisListType

ENG_ATTR = {'pe': 'tensor', 'act': 'scalar', 'dve': 'vector', 'pool': 'gpsimd', 'sp': 'sync'}
EPS = 1e-6
NT = 8
TT = 512
NSLOT = 6
PAST_LEN = 16384
NT_RUN = NT
STRICT = True
STOP_STAGE = 99
KC_PER = 8
USE_SCRATCH = True
DEFER_HALF = True
SPLIT_RMS = True
DIAG_ENG = 'pool'
MAX_OPS = 10 ** 9
RUN_SAMPLE = True


LOCK_SHARED = {'qT', 'qkf', 'tmpA', 'qrot', 'tmpB', 'Vs', 'aoTp'}


class _Op:
    __slots__ = ('eng', 'fn', 'deps', 'dma', 'sig', 'val')

    def __init__(self, eng, fn, deps, dma):
        self.eng, self.fn, self.deps, self.dma = eng, fn, deps, dma
        self.sig = False
        self.val = 0


class Prog:
    def __init__(self, nc):
        self.nc = nc
        self.ops = []
        self.lastw = {}
        self.rds = {}

    def add(self, eng, fn, reads=(), writes=(), dma=None):
        idx = len(self.ops)
        if idx >= MAX_OPS:
            return idx
        reads = list(reads)
        writes = list(writes)
        alln = reads + writes
        if any(n in LOCK_SHARED for n in alln):
            reads.append('hidlock')
        if 'hid' in alln:
            writes.append('hidlock')
        deps = {}
        for b in reads:
            w = self.lastw.get(b)
            if w is not None:
                deps[w] = 'raw'
            if b.startswith('ps'):
                for r in self.rds.get(b, ()):
                    if self.ops[r].eng != eng and r not in deps:
                        deps[r] = 'psx'
        for b in writes:
            w = self.lastw.get(b)
            if w is not None and w not in deps:
                deps[w] = 'waw'
            for r in self.rds.get(b, ()):
                if r not in deps:
                    deps[r] = 'war'
        self.ops.append(_Op(eng, fn, deps, dma))
        for b in reads:
            self.rds.setdefault(b, []).append(idx)
        for b in writes:
            self.lastw[b] = idx
            self.rds[b] = []
        return idx

    def emit(self):
        nc = self.nc
        ops = self.ops
        for op in ops:
            keep = {}
            for d, kind in op.deps.items():
                D = ops[d]
                if D.dma is None and op.dma is None and D.eng == op.eng:
                    if op.eng == 'pe':
                        continue
                    if kind != 'raw' and not STRICT:
                        continue
                keep[d] = kind
                if D.dma is None:
                    D.sig = True
            op.deps = keep
        cnt = {}
        dcnt = {}
        dpos = {}
        for oi, op in enumerate(ops):
            if op.dma is not None:
                dcnt[op.dma] = dcnt.get(op.dma, 0) + 1
                dpos.setdefault(op.dma, []).append(oi)
                op.val = 16 * dcnt[op.dma]
            elif op.sig:
                cnt[op.eng] = cnt.get(op.eng, 0) + 1
                op.val = cnt[op.eng]
        engines = ['pe', 'act', 'dve', 'pool', 'sp']
        with ExitStack() as es:
            esem = {e: es.enter_context(nc.semaphore("s_" + e)) for e in engines}
            dsem = {k: es.enter_context(nc.semaphore("d_%d" % i)) for i, k in enumerate(sorted(dcnt))}
            block = es.enter_context(nc.Block())

            import bisect

            def body(e, eng):
                waited = {}
                for oi, op in enumerate(ops):
                    if op.eng != eng:
                        continue
                    need = {}
                    for d in op.deps:
                        D = ops[d]
                        key = ('d', D.dma) if D.dma is not None else ('e', D.eng)
                        v = D.val
                        if D.dma is not None:
                            v = 16 * bisect.bisect_left(dpos[D.dma], oi)
                        if v > need.get(key, 0):
                            need[key] = v
                    for key, v in need.items():
                        if waited.get(key, 0) >= v:
                            continue
                        waited[key] = v
                        e.wait_ge(dsem[key[1]] if key[0] == 'd' else esem[key[1]], v)
                    ins = op.fn(e)
                    if op.dma is not None:
                        ins.then_inc(dsem[op.dma], 16)
                    elif op.sig:
                        ins.then_inc(esem[eng], 1)
                if eng == 'sp':
                    for k, c in dcnt.items():
                        if waited.get(('d', k), 0) < 16 * c:
                            e.wait_ge(dsem[k], 16 * c)
                    for en, c in cnt.items():
                        if c:
                            e.wait_ge(esem[en], c)

            for eng in engines:
                getattr(block, ENG_ATTR[eng])(lambda e, eng=eng: body(e, eng))
        return nc


class DryProg:
    def add(self, *a, **k):
        return 0


class WRec:
    def __init__(self):
        self.groups = []

    def next(self, spec):
        self.groups.append(spec)
        return 0

    def release(self, slot):
        pass

    def prefetch(self):
        pass


class WLoader:
    def __init__(self, P, T, groups, scr, ng):
        self.P, self.T, self.groups = P, T, groups
        self.scr, self.ng = scr, ng
        self.use = 0
        self.load = 0
        self.free = list(range(NSLOT))
        self.slot_of = {}
        self.kidx = {}
        self.seen = {}

    def _issue(self, gi, slot):
        wt = self.T.W[slot]
        key, spec = self.groups[gi]
        if key in self.kidx and USE_SCRATCH:
            g0 = self.kidx[key]
            self.P.add('pool', lambda e, wt=wt, g0=g0: e.dma_start(out=wt[:, :], in_=self.scr[g0]),
                       reads=['wscr.%d' % g0], writes=wnames(slot), dma='w%d' % slot)
            return
        occ = self.seen.get(key, 0)
        self.seen[key] = occ + 1
        first = (occ >= 1) or (len(self.seen) % 2 == 0) or not DEFER_HALF
        if first:
            self.kidx[key] = len(self.kidx)
        gidx = self.kidx.get(key, -1)
        for pi, (src, d0, d1, a, b) in enumerate(spec):
            dst = wt[:, d0:d1].rearrange("p (a b) -> p a b", a=a)
            for k0 in range(0, a, KC_PER):
                k1 = min(a, k0 + KC_PER)
                self.P.add('pool', lambda e, dst=dst, src=src, k0=k0, k1=k1: e.dma_start(out=dst[:, k0:k1, :], in_=src[:, k0:k1, :]),
                           writes=['w%d.%d.%d' % (slot, pi, k0)], dma='w%d' % slot)
        if USE_SCRATCH and first:
            self.P.add('sp', lambda e, wt=wt, gidx=gidx: e.dma_start(out=self.scr[gidx], in_=wt[:, :]),
                       reads=wnames(slot), writes=['wscr.%d' % gidx], dma='wst%d' % slot)

    def _prefetch(self):
        while self.load < len(self.groups) and self.free:
            slot = self.free.pop(0)
            self.slot_of[self.load] = slot
            self._issue(self.load, slot)
            self.load += 1

    def next(self, spec):
        self._prefetch()
        assert self.use in self.slot_of, "no free weight slot (too many groups held)"
        slot = self.slot_of.pop(self.use)
        self.use += 1
        return slot

    def release(self, slot):
        self.free.append(slot)
        self._prefetch()

    def prefetch(self):
        self._prefetch()


def wnames(slot):
    return ['w%d.%d.%d' % (slot, pi, k0) for pi in range(2) for k0 in range(0, 8, KC_PER)]


class TT_:
    pass


def emit_all(P, W, T, D):
    bankctr = [0]
    held = set()

    def bank():
        while True:
            i = bankctr[0] % 8
            bankctr[0] += 1
            if i not in held:
                return T.ps[i], 'ps%d' % i, i

    def psb(bk):
        return bk[:].bitcast(BF16)

    rr = {'sg': 0, 'pt': 0, 'dg': 0, 'y2': 0, 'tl': 0, 'rl': 0, 'ev': 0}

    def rot(key, n=2):
        rr[key] = (rr[key] + 1) % n
        return rr[key]

    def cn(base, n=8):
        if base == 'xn':
            return cn('ybf', 2 * n)
        return ['%s.%d' % (base, i) for i in range(n)]

    def xnn(s):
        return ['ybf.%d' % (2 * s), 'ybf.%d' % (2 * s + 1)]

    def wv(ap2d, c0, ncol, kc=8):
        return ap2d.rearrange("(kc p) n -> p kc n", p=128)[:, :, c0:c0 + ncol]

    def g_cols(ap2d, c0):
        return (('c', ap2d.tensor.name, c0), [(wv(ap2d, c0, 512), 0, 4096, 8, 512)])

    def g_glu(gi):
        return (('glu', gi), [(wv(D.w_in, 1536 + gi * 256, 256), 0, 2048, 8, 256),
                              (wv(D.w_in, 2560 + gi * 256, 256), 2048, 4096, 8, 256)])

    def g_ff2(fg, nh):
        src = D.w_ff2.rearrange("(fc p) n -> p fc n", p=128)[:, fg * 8:(fg + 1) * 8, nh * 512:(nh + 1) * 512]
        return (('ff2', fg, nh), [(src, 0, 4096, 8, 512)])

    def g_ple():
        return (('ple',), [(D.w_ple.rearrange("(kc p) n -> p kc n", p=128), 0, 2048, 2, 1024)])

    def wslot(slot, a=8, b=512):
        return T.W[slot][:, 0:a * b].rearrange("p (a b) -> p a b", a=a)

    W.prefetch()
    P.add('pool', lambda e: e.memset(T.identf[:], 0.0), writes=['identf'])
    P.add('pool', lambda e: e.affine_select(out=T.identf[:], in_=T.identf[:], pattern=[[-1, 128]],
                                            compare_op=ALU.not_equal, fill=1.0, base=0, channel_multiplier=1),
          reads=['identf'], writes=['identf'])
    P.add('dve', lambda e: e.tensor_copy(out=T.ident[:], in_=T.identf[:]), reads=['identf'], writes=['ident'])
    P.add('pool', lambda e: e.memset(T.ones[:], 1.0), writes=['ones'])
    P.add('pool', lambda e: e.memset(T.neghalf[:], -0.5), writes=['neghalf'])
    P.add('pool', lambda e: e.memset(T.maskf[:], 1.0), writes=['maskf'])
    P.add('pool', lambda e: e.affine_select(out=T.maskf[:, 0, :], in_=T.maskf[:, 0, :], pattern=[[-1, 128]],
                                            compare_op=ALU.is_gt, fill=0.0, base=0, channel_multiplier=1),
          reads=['maskf'], writes=['maskf'])
    P.add('pool', lambda e: e.affine_select(out=T.maskf[:, 1, :], in_=T.maskf[:, 1, :], pattern=[[1, 128]],
                                            compare_op=ALU.is_ge, fill=0.0, base=0, channel_multiplier=-1),
          reads=['maskf'], writes=['maskf'])
    P.add('dve', lambda e: e.tensor_copy(out=T.mask[:], in_=T.maskf[:]), reads=['maskf'], writes=['mask'])
    P.add('pool', lambda e: e.memset(T.Vaug[:], 1.0), writes=cn('V', 5))
    P.add('sp', lambda e: e.dma_start(out=T.cos[:, 0, :], in_=D.cos[:, 0, :]), writes=['cos'], dma='cos')
    P.add('sp', lambda e: e.dma_start(out=T.sin[:, 0, :], in_=D.sin[:, 0, :]), writes=['sin'], dma='sin')
    P.add('sp', lambda e: e.dma_start(out=T.flag[:], in_=D.flag), writes=['flag'], dma='flag')
    P.add('sp', lambda e: e.dma_start(out=T.gq[:], in_=D.q_norm.partition_broadcast(128)), writes=['gq'], dma='gq')
    P.add('sp', lambda e: e.dma_start(out=T.gk[:], in_=D.k_norm.partition_broadcast(128)), writes=['gk'], dma='gk')
    P.add('sp', lambda e: e.dma_start(out=T.esink[:], in_=D.sinks.partition_broadcast(128)), writes=['esink'], dma='esink')
    rows = [D.ln1, D.ln2, D.ln_ple, D.b_glu[0:1024], D.b_glu[1024:2048], D.conv_dw_b, D.conv_ln_g, D.conv_ln_b,
            D.b_conv_out]
    for r, src in enumerate(rows):
        P.add('sp', lambda e, r=r, src=src: e.dma_start(out=T.prows[r:r + 1, :], in_=src.rearrange("(o n) -> o n", o=1)),
              writes=['pr%d' % r], dma='prm')
    P.add('sp', lambda e: e.dma_start(out=T.prows[9:40, :], in_=D.conv_dw), writes=['pr9'], dma='prm')
    if RUN_SAMPLE:
        P.add('sp', lambda e: e.dma_start(out=D.ks[:, 0:127, :], in_=D.ck[:, 1:128, :]), writes=['ks.A'], dma='ksA')
        P.add('sp', lambda e: e.dma_start(out=D.vs[:, 0:127, :], in_=D.cv[:, 1:128, :]), writes=['vs.A'], dma='vsA')
        P.add('sp', lambda e: e.dma_start(out=D.cs[:, 0:29, :], in_=D.st[:, 1:30, :]), writes=['cs.A'], dma='csA')
    bk, bkn, _ = bank()

    def ptr(e):
        for c in range(8):
            ins = e.transpose(out=bk[:, c * 40:(c + 1) * 40], in_=T.prows[0:40, c * 128:(c + 1) * 128],
                              identity=T.identf[0:40, 0:40])
        return ins
    P.add('pe', ptr, reads=['pr%d' % r for r in range(10)] + ['identf', 'tl0', 'tl1'], writes=[bkn])
    P.add('dve', lambda e: e.tensor_copy(out=T.prm[:].rearrange("p c r -> p (c r)"), in_=bk[:, 0:320]),
          reads=[bkn], writes=['prm'])
    P.add('dve', lambda e: e.tensor_copy(out=T.wdw[:], in_=T.prm[:, :, 9:40]), reads=['prm'], writes=['wdw'])
    P.add('dve', lambda e: e.tensor_scalar(out=T.gfull[:, 0:16, :], in0=T.gq[:].unsqueeze(1).to_broadcast([128, 16, 64]),
                                           scalar1=0.125, scalar2=None, op0=ALU.mult), reads=['gq'], writes=['gfull'])
    P.add('dve', lambda e: e.tensor_copy(out=T.gfull[:, 16:20, :], in_=T.gk[:].unsqueeze(1).to_broadcast([128, 4, 64])),
          reads=['gk'], writes=['gfull'])
    P.add('act', lambda e: e.activation(out=T.esink[:], in_=T.esink[:], func=AF.Exp), reads=['esink'], writes=['esink'])

    def rms_head_sub(src_fn, src_names, Pn, s):
        P.add('act', lambda e: e.activation(out=T.xn[:Pn, s, :], in_=src_fn(s), func=AF.Square,
                                            accum_out=T.ss[:Pn, s:s + 1]), reads=src_names, writes=xnn(s) + ['ss.%d' % s])
        P.add('dve', lambda e: e.tensor_scalar(out=T.ms[:Pn, s:s + 1], in0=T.ss[:Pn, s:s + 1], scalar1=1.0 / 1024,
                                               scalar2=EPS, op0=ALU.mult, op1=ALU.add), reads=['ss.%d' % s], writes=['ms.%d' % s])
        P.add('pool', lambda e: e.tensor_tensor(out=T.rstd[:Pn, s:s + 1], in0=T.ms[:Pn, s:s + 1],
                                                in1=T.neghalf[:Pn, s:s + 1], op=ALU.pow),
              reads=['ms.%d' % s, 'neghalf'], writes=['rstd.%d' % s])
        P.add('dve', lambda e: e.tensor_scalar(out=T.xn[:Pn, s, :], in0=src_fn(s), scalar1=T.rstd[:Pn, s:s + 1],
                                               scalar2=None, op0=ALU.mult), reads=src_names + ['rstd.%d' % s], writes=xnn(s))

    def rms_T(src_fn, src_names, Pn, nsub, gidx, dst, dnames, col0, heads_done=False):
        N = Pn * nsub
        src_names = [n if isinstance(n, list) else [n] for n in src_names]
        if heads_done:
            return rms_tail(Pn, nsub, gidx, dst, dnames, col0)
        for s in range(nsub):
            if True:
                P.add('act', lambda e, s=s: e.activation(out=T.xn[:Pn, s, :], in_=src_fn(s), func=AF.Square,
                                                          accum_out=T.ss[:Pn, s:s + 1]),
                      reads=src_names[s], writes=xnn(s) + ['ss.%d' % s])
            else:
                P.add('dve', lambda e, s=s: e.tensor_tensor_reduce(out=T.xn[:Pn, s, :], in0=src_fn(s), in1=src_fn(s),
                                                                   scale=1.0, scalar=0.0, op0=ALU.mult, op1=ALU.add,
                                                                   accum_out=T.ss[:Pn, s:s + 1]),
                      reads=src_names[s], writes=xnn(s) + ['ss.%d' % s])
        P.add('dve', lambda e: e.tensor_scalar(out=T.ms[:Pn, 0:nsub], in0=T.ss[:Pn, 0:nsub], scalar1=1.0 / 1024,
                                               scalar2=EPS, op0=ALU.mult, op1=ALU.add), reads=['ss.%d' % s for s in range(nsub)],
              writes=['ms.%d' % s for s in range(nsub)])
        P.add('pool', lambda e: e.tensor_tensor(out=T.rstd[:Pn, 0:nsub], in0=T.ms[:Pn, 0:nsub],
                                                in1=T.neghalf[:Pn, 0:nsub], op=ALU.pow),
              reads=['ms.%d' % s for s in range(nsub)] + ['neghalf'], writes=['rstd.%d' % s for s in range(nsub)])
        for s in range(nsub):
            if s % 2 == 0 or not SPLIT_RMS:
                P.add('dve', lambda e, s=s: e.tensor_scalar(out=T.xn[:Pn, s, :], in0=src_fn(s), scalar1=T.rstd[:Pn, s:s + 1],
                                                            scalar2=None, op0=ALU.mult),
                      reads=src_names[s] + ['rstd.%d' % s], writes=xnn(s))
            else:
                P.add('act', lambda e, s=s: e.activation(out=T.xn[:Pn, s, :], in_=src_fn(s), func=AF.Copy,
                                                          scale=T.rstd[:Pn, s:s + 1]),
                      reads=src_names[s] + ['rstd.%d' % s], writes=xnn(s))
        rms_tail(Pn, nsub, gidx, dst, dnames, col0)

    def rms_tail(Pn, nsub, gidx, dst, dnames, col0):
        N = Pn * nsub
        for c in range(8):
            bk, bkn, _ = bank()
            pb = psb(bk)

            def tr(e, c=c, pb=pb):
                for s in range(nsub):
                    ins = e.transpose(out=pb[:, s * Pn:(s + 1) * Pn], in_=T.xn[:Pn, s, c * 128:(c + 1) * 128],
                                      identity=T.ident[:Pn, :Pn])
                return ins
            P.add('pe', tr, reads=sum([xnn(s) for s in range(nsub)], []) + ['ident'], writes=[bkn])
            if c % 2 == 0:
                P.add('act', lambda e, c=c, pb=pb: e.activation(out=dst[:, c, col0:col0 + N], in_=pb[:, 0:N], func=AF.Copy,
                                                                scale=T.prm[:, c, gidx:gidx + 1]),
                      reads=[bkn, 'prm'], writes=[dnames[c]])
            else:
                P.add('dve', lambda e, c=c, pb=pb: e.tensor_scalar(out=dst[:, c, col0:col0 + N], in0=pb[:, 0:N],
                                                                   scalar1=T.prm[:, c, gidx:gidx + 1], scalar2=None,
                                                                   op0=ALU.mult),
                      reads=[bkn, 'prm'], writes=[dnames[c]])

    def qkv_sub(Pn, ucol, tblk, slots, h0, qcol, kname, kcol, vblk, kvout, part='ab'):
        if 'a' in part:
            qkv_a(Pn, ucol, tblk, slots, h0, vblk, kvout)
        if 'b' in part:
            qkv_b(Pn, h0, qcol, kname, kcol)

    def qkv_a(Pn, ucol, tblk, slots, h0, vblk, kvout):
        sq0, sq1, skv = slots
        e0 = h0 * 64
        banks = []
        grp = ([(sq0, 0), (sq1, 512)] if h0 == 0 else []) + [(skv, 1024)]
        for slot, qo in grp:
            bk, bkn, _ = bank()
            banks.append((bk, bkn, qo))

            def mm(e, slot=slot, bk=bk):
                w = wslot(slot)
                for kc in range(8):
                    ins = e.matmul(bk[:Pn, :], lhsT=T.uT[:, kc, ucol:ucol + Pn], rhs=w[:, kc, :], start=(kc == 0),
                                   stop=(kc == 7))
                return ins
            P.add('pe', mm, reads=cn('uT') + wnames(slot), writes=[bkn])
        for bk, bkn, qo in banks:
            ncol = 512 if qo < 1024 else 256
            P.add('act', lambda e, bk=bk, qo=qo, ncol=ncol: e.activation(out=T.qkf[:Pn, qo:qo + ncol], in_=bk[:Pn, 0:ncol],
                                                                          func=AF.Copy),
                  reads=[bkn], writes=['qkf'])
        bkv, bkvn, _ = banks[-1]
        P.add('dve', lambda e: e.tensor_copy(out=T.Vaug[:Pn, vblk, :, 0:64],
                                             in_=bkv[:Pn, 256:512].rearrange("p (h d) -> p h d", h=4)),
              reads=[bkvn], writes=['V.%d' % vblk])
        if kvout:
            P.add('act', lambda e: e.activation(out=T.kvout[:Pn, 256:512], in_=bkv[:Pn, 256:512], func=AF.Copy),
                  reads=[bkvn], writes=['kvout.v'])
        nh_ = 20 - h0
        qv = T.qkf[:Pn, e0:1280]
        rv = T.qrot[:Pn, e0:1280]
        P.add('act', lambda e: e.activation(out=rv, in_=qv, func=AF.Square), reads=['qkf'], writes=['qrot'])
        P.add('dve', lambda e: e.tensor_reduce(out=T.ssqk[:Pn, h0:20], in_=rv.rearrange("p (h d) -> p h d", d=64),
                                               axis=AX.X, op=ALU.add), reads=['qrot'], writes=['ssqk'])
        P.add('dve', lambda e: e.tensor_scalar(out=T.msqk[:Pn, h0:20], in0=T.ssqk[:Pn, h0:20], scalar1=1.0 / 64,
                                               scalar2=EPS, op0=ALU.mult, op1=ALU.add), reads=['ssqk'], writes=['msqk'])
        P.add('pool', lambda e: e.tensor_tensor(out=T.rqk[:Pn, h0:20], in0=T.msqk[:Pn, h0:20], in1=T.neghalf[:Pn, h0:20],
                                                op=ALU.pow), reads=['msqk', 'neghalf'], writes=['rqk'])
        P.add('dve', lambda e: e.tensor_tensor(out=qv, in0=qv, in1=T.gfull[:Pn, h0:20, :].rearrange("p h d -> p (h d)"),
                                               op=ALU.mult), reads=['qkf', 'gfull'], writes=['qkf'])
        g4 = qv.rearrange("p (h t d) -> p h t d", t=2, d=32)
        r4 = rv.rearrange("p (h t d) -> p h t d", t=2, d=32)
        g1, g2 = g4[:, :, 0, :], g4[:, :, 1, :]
        cb = T.cos[:Pn, tblk, :].unsqueeze(1).to_broadcast([Pn, nh_, 32])
        sb_ = T.sin[:Pn, tblk, :].unsqueeze(1).to_broadcast([Pn, nh_, 32])
        tA = T.tmpA[:Pn, 0:nh_, :]
        tB = T.tmpB[:Pn, 0:nh_, :]
        P.add('dve', lambda e: e.tensor_tensor(out=tA, in0=g1, in1=cb, op=ALU.mult), reads=['qkf', 'cos'], writes=['tmpA'])
        P.add('dve', lambda e: e.tensor_tensor(out=tB, in0=g2, in1=sb_, op=ALU.mult), reads=['qkf', 'sin'], writes=['tmpB'])
        P.add('dve', lambda e: e.tensor_tensor(out=r4[:, :, 0, :], in0=tA, in1=tB, op=ALU.subtract),
              reads=['tmpA', 'tmpB'], writes=['qrot'])
        P.add('dve', lambda e: e.tensor_tensor(out=tA, in0=g2, in1=cb, op=ALU.mult), reads=['qkf', 'cos'], writes=['tmpA'])
        P.add('dve', lambda e: e.tensor_tensor(out=tB, in0=g1, in1=sb_, op=ALU.mult), reads=['qkf', 'sin'], writes=['tmpB'])
        P.add('dve', lambda e: e.tensor_tensor(out=r4[:, :, 1, :], in0=tA, in1=tB, op=ALU.add),
              reads=['tmpA', 'tmpB'], writes=['qrot'])
        r3 = T.qrot[:Pn, :].rearrange("p (h d) -> p h d", d=64)
        if h0 == 0:
            P.add('dve', lambda e: e.tensor_tensor(out=T.qbf[:Pn, :, :], in0=r3[:, 0:16, :],
                                                   in1=T.rqk[:Pn, 0:16].unsqueeze(2).to_broadcast([Pn, 16, 64]),
                                                   op=ALU.mult), reads=['qrot', 'rqk'], writes=['qbf'])
        P.add('dve', lambda e: e.tensor_tensor(
            out=T.kdbf[:Pn, :, :, :], in0=r3[:, 16:20, :].unsqueeze(2).to_broadcast([Pn, 4, 2, 64]),
            in1=T.rqk[:Pn, 16:20].unsqueeze(2).unsqueeze(3).to_broadcast([Pn, 4, 2, 64]), op=ALU.mult),
            reads=['qrot', 'rqk'], writes=['kdbf'])
        if kvout:
            P.add('dve', lambda e: e.tensor_tensor(out=T.kvout[:Pn, 0:256].rearrange("p (h d) -> p h d", d=64),
                                                   in0=r3[:, 16:20, :],
                                                   in1=T.rqk[:Pn, 16:20].unsqueeze(2).to_broadcast([Pn, 4, 64]),
                                                   op=ALU.mult), reads=['qrot', 'rqk'], writes=['kvout.k'])

    def qkv_b(Pn, h0, qcol, kname, kcol):
        if h0 == 0:
            bk, bkn, _ = bank()
            pb = psb(bk)

            def trq(e, pb=pb):
                qf = T.qbf[:Pn, :, :].rearrange("p h d -> p (h d)")
                for c in range(8):
                    ins = e.transpose(out=pb[:, c * 128:c * 128 + Pn], in_=qf[:, c * 128:(c + 1) * 128],
                                      identity=T.ident[:Pn, :Pn])
                return ins
            P.add('pe', trq, reads=['qbf', 'ident'], writes=[bkn])
            P.add('act', lambda e, pb=pb: e.activation(out=T.qT[:, :, qcol:qcol + Pn],
                                                       in_=pb.rearrange("p (c t) -> p c t", c=8)[:, :, 0:Pn], func=AF.Copy),
                  reads=[bkn], writes=['qT'])
        if kname is not None:
            bk, bkn, _ = bank()
            pb = psb(bk)

            def trk(e, pb=pb):
                kf = T.kdbf[:Pn, :, :, :].rearrange("p h t d -> p (h t d)")
                for hk in range(4):
                    ins = e.transpose(out=pb[:, hk * 128:hk * 128 + Pn], in_=kf[:, hk * 128:(hk + 1) * 128],
                                      identity=T.ident[:Pn, :Pn])
                return ins
            P.add('pe', trk, reads=['kdbf', 'ident'], writes=[bkn])
            P.add('dve', lambda e, pb=pb: e.tensor_copy(out=T.kT[:, :, kcol:kcol + Pn],
                                                        in_=pb[:, 0:512].rearrange("p (c t) -> p c t", c=4)[:, :, 0:Pn]),
                  reads=[bkn], writes=[kname])

    def normalize_heads(bO, bOn, hk, Pn):
        ov = bO[:Pn, 0:260].rearrange("p (g d) -> p g d", g=4)
        P.add('dve', lambda e: e.tensor_tensor(out=T.den[:Pn, :], in0=ov[:, :, 64], in1=T.esink[:Pn, 4 * hk:4 * hk + 4],
                                               op=ALU.add), reads=[bOn, 'esink'], writes=['den'])
        P.add('dve', lambda e: e.reciprocal(out=T.rden[:Pn, :], in_=T.den[:Pn, :]), reads=['den'], writes=['rden'])
        P.add('dve', lambda e: e.tensor_tensor(out=T.ao[:Pn, 4 * hk:4 * hk + 4, :], in0=ov[:, :, 0:64],
                                               in1=T.rden[:Pn, :].unsqueeze(2).to_broadcast([Pn, 4, 64]), op=ALU.mult),
              reads=[bOn, 'rden'], writes=['ao'])

    def ao_transpose(Pn, col, dst=None, dnames=None):
        if dst is None:
            dst, dnames = T.aoT, cn('sT')
        bk, bkn, _ = bank()
        pb = psb(bk)

        def tr(e):
            af = T.ao[:Pn, :, :].rearrange("p h d -> p (h d)")
            for c in range(8):
                ins = e.transpose(out=pb[:, c * 128:c * 128 + Pn], in_=af[:, c * 128:(c + 1) * 128],
                                  identity=T.ident[:Pn, :Pn])
            return ins
        P.add('pe', tr, reads=['ao', 'ident'], writes=[bkn])
        P.add('act', lambda e: e.activation(out=dst[:, :, col:col + Pn],
                                            in_=pb.rearrange("p (c t) -> p c t", c=8)[:, :, 0:Pn], func=AF.Copy),
              reads=[bkn], writes=dnames)

    ast = {}

    def attn_sc(s, hk):
        qc = s * 128
        bp, bpn, _ = bank()
        bo, bon, _ = bank()

        def sc(e):
            banks2 = ((bp, s * 128), (bo, 128 + s * 128))
            for par in (0, 1):
                if par == 1:
                    e.matmul(bp[:, 128:129], lhsT=T.ident[:, :], rhs=T.ident[:, 0:1], start=True, stop=True)
                for bk, kcol in banks2:
                    v = bk[:].rearrange("p (g q) -> p g q", g=4)
                    for g in (par, par + 2):
                        ins = e.matmul(v[:, g, :], lhsT=T.kT[64 * par:64 * par + 64, hk, kcol:kcol + 128],
                                       rhs=T.qT[64 * par:64 * par + 64, 2 * hk + g // 2, qc:qc + 128],
                                       start=True, stop=True)
            return ins
        P.add('pe', sc, reads=['kT.%d' % s, 'kT.%d' % (s + 1), 'qT', 'ident'], writes=[bpn, bon])
        i = rot('pt')
        pt = T.PT[i]
        ptn = 'PT%d' % i
        P.add('act', lambda e: e.activation(out=pt[:, 0, :], in_=bp[:], func=AF.Exp), reads=[bpn], writes=[ptn])
        P.add('act', lambda e: e.activation(out=pt[:, 1, :], in_=bo[:], func=AF.Exp), reads=[bon], writes=[ptn])
        pv4 = pt[:].rearrange("p w (g q) -> p w g q", g=4)
        P.add('dve', lambda e: e.tensor_tensor(out=pv4, in0=pv4, in1=T.mask[:].unsqueeze(2).to_broadcast([128, 2, 4, 128]),
                                               op=ALU.mult), reads=[ptn, 'mask'], writes=[ptn])
        ast[(s, hk)] = (pt, ptn)

    def attn_pv(s, hk):
        pt, ptn = ast.pop((s, hk))
        bO, bOn, _ = bank()

        def pvm(e):
            for g in range(4):
                e.matmul(bO[:, g * 65:(g + 1) * 65], lhsT=pt[:, 0, g * 128:(g + 1) * 128], rhs=T.Vaug[:, s, hk, :],
                         start=True, stop=False)
                ins = e.matmul(bO[:, g * 65:(g + 1) * 65], lhsT=pt[:, 1, g * 128:(g + 1) * 128],
                               rhs=T.Vaug[:, s + 1, hk, :], start=False, stop=True)
            return ins
        P.add('pe', pvm, reads=[ptn, 'V.%d' % s, 'V.%d' % (s + 1)], writes=[bOn])
        normalize_heads(bO, bOn, hk, 128)

    def attn_fin(s):
        ao_transpose(128, s * 128, T.aoTp, ['aoTp'])

    def attn_block(s):
        attn_sc(s, 0)
        for hk in range(4):
            if hk + 1 < 4:
                attn_sc(s, hk + 1)
            attn_pv(s, hk)
        attn_fin(s)

    def glu_stage(N, ucol, mode, last):
        for gi in range(4):
            glu_group(gi, N, ucol, mode, last)

    def glu_group(gi, N, ucol, mode, last):
        if True:
            slot = W.next(g_glu(gi))
            wa = T.W[slot][:, 0:2048].rearrange("p (a b) -> p a b", a=8)
            wb = T.W[slot][:, 2048:4096].rearrange("p (a b) -> p a b", a=8)
            for jj in range(2):
                j = 2 * gi + jj
                calls = [(N, ucol, mode)]
                if mode == 'main0':
                    calls = [(32, 96, 'halo'), (N, ucol, 'main')]
                for (n_, uc, md) in calls:
                    b1, b1n, _ = bank()
                    b2, b2n, _ = bank()

                    def mm(e, b1=b1, b2=b2, n_=n_, uc=uc, jj=jj, wa=wa, wb=wb):
                        for kc in range(8):
                            e.matmul(b1[:, 0:n_], lhsT=wa[:, kc, jj * 128:(jj + 1) * 128], rhs=T.uT[:, kc, uc:uc + n_],
                                     start=(kc == 0), stop=(kc == 7))
                        for kc in range(8):
                            ins = e.matmul(b2[:, 0:n_], lhsT=wb[:, kc, jj * 128:(jj + 1) * 128],
                                           rhs=T.uT[:, kc, uc:uc + n_], start=(kc == 0), stop=(kc == 7))
                        return ins
                    P.add('pe', mm, reads=cn('uT') + wnames(slot), writes=[b1n, b2n])
                    i = rot('sg')
                    sg = T.sg[i]
                    sgn = 'sg%d' % i
                    P.add('act', lambda e, sg=sg, b2=b2, n_=n_, j=j: e.activation(out=sg[:, 0:n_], in_=b2[:, 0:n_],
                                                                                 func=AF.Sigmoid, bias=T.prm[:, j, 4:5]),
                          reads=[b2n, 'prm'], writes=[sgn])
                    if md == 'halo':
                        P.add('dve', lambda e, sg=sg, b1=b1, j=j: e.scalar_tensor_tensor(
                            out=T.aT[:, j, 0:30], in0=b1[:, 2:32], scalar=T.prm[:, j, 3:4], in1=sg[:, 2:32],
                            op0=ALU.add, op1=ALU.mult), reads=[b1n, sgn, 'prm'], writes=['aT.h'])
                        P.add('dve', lambda e, j=j: e.tensor_scalar(out=T.aT[:, j, 0:30], in0=T.aT[:, j, 0:30],
                                                                    scalar1=T.flag[:, 0:1], scalar2=None, op0=ALU.mult),
                              reads=['aT.h', 'flag'], writes=['aT.h'])
                    elif md == 'sample':
                        P.add('dve', lambda e, sg=sg, b1=b1, j=j: e.scalar_tensor_tensor(
                            out=T.a32[:, j, 0:16], in0=b1[:, 0:16], scalar=T.prm[:, j, 3:4], in1=sg[:, 0:16],
                            op0=ALU.add, op1=ALU.mult), reads=[b1n, sgn, 'prm'], writes=['a32'])
                        P.add('dve', lambda e, j=j: e.tensor_copy(out=T.histT[:, j, :, 30], in_=T.a32[:, j, 0:16]),
                              reads=['a32'], writes=['qT'])
                    else:
                        P.add('dve', lambda e, sg=sg, b1=b1, j=j, n_=n_: e.scalar_tensor_tensor(
                            out=T.aT[:, j, 30:30 + n_], in0=b1[:, 0:n_], scalar=T.prm[:, j, 3:4], in1=sg[:, 0:n_],
                            op0=ALU.add, op1=ALU.mult), reads=[b1n, sgn, 'prm'], writes=['aT.m%d' % j])
                        if last:
                            P.add('dve', lambda e, sg=sg, b1=b1, j=j, n_=n_: e.scalar_tensor_tensor(
                                out=T.a32[:, j, 0:32], in0=b1[:, n_ - 32:n_], scalar=T.prm[:, j, 3:4], in1=sg[:, n_ - 32:n_],
                                op0=ALU.add, op1=ALU.mult), reads=[b1n, sgn, 'prm'], writes=['a32'])

            W.release(slot)

    cst = {}

    def conv_ln_stage(N, sample):
        conv_begin()
        for j in range(8):
            conv_chunk(j, N, sample)
        conv_finish(N)

    def conv_begin():
        bsum, bsumn, isum = bank()
        held.add(isum)
        bsq, bsqn, isq = bank()
        held.add(isq)
        cst['b'] = (bsum, bsumn, isum, bsq, bsqn, isq)

    def conv_chunk(j, N, sample):
        if not sample:
            diag_build(j)
        conv_mm(j, N, sample)

    def diag_build(j):
        i = rot('dg')
        dg = T.diag[i]
        dgn = 'diag%d' % i
        cst['dg%d' % j] = (dg, dgn)
        P.add(DIAG_ENG, lambda e, dg=dg, j=j: e.tensor_tensor(
            out=dg[:], in0=T.ident[:].unsqueeze(1).to_broadcast([128, 31, 128]),
            in1=T.wdw[:, j, :].unsqueeze(2).to_broadcast([128, 31, 128]), op=ALU.mult),
            reads=['ident', 'wdw'], writes=[dgn])

    def conv_mm(j, N, sample):
        bsum, bsumn, isum, bsq, bsqn, isq = cst['b']
        if sample:
            i = rot('tl')
            tl = T.tl[i]
            tln = 'tl%d' % i
            t3 = tl[:, 0:N * 31].rearrange("p (b k) -> p b k", k=31)
            P.add('dve', lambda e: e.tensor_tensor(out=t3, in0=T.histT[:, j, :, :],
                                                   in1=T.wdw[:, j, :].unsqueeze(1).to_broadcast([128, N, 31]), op=ALU.mult),
                  reads=['qT', 'wdw'], writes=[tln])
            bc = T.ycs
            bcn = 'ycs'
            P.add('dve', lambda e: e.tensor_reduce(out=T.ycs[:, 0:N], in_=t3, axis=AX.X, op=ALU.add), reads=[tln],
                  writes=['ycs'])
        else:
            dg, dgn = cst['dg%d' % j]
            bc, bcn, _ = bank()

            def cm(e, dg=dg, bc=bc, j=j):
                for k in range(31):
                    ins = e.matmul(bc[:, 0:N], lhsT=dg[:, k, :], rhs=T.aT[:, j, k:k + N], start=(k == 0), stop=(k == 30))
                return ins
            P.add('pe', cm, reads=[dgn, 'aT.h', 'aT.m%d' % j], writes=[bcn])
        if True:
            P.add('act', lambda e, bc=bc, j=j: e.activation(out=T.ybf[:, j, 0:N], in_=bc[:, 0:N], func=AF.Identity,
                                                            bias=T.prm[:, j, 5:6]), reads=[bcn, 'prm'], writes=['ybf.%d' % j])
            i2 = rot('y2')
            y2 = T.y2[i2]
            y2n = 'y2%d' % i2
            P.add('act', lambda e, bc=bc, j=j, y2=y2: e.activation(out=y2[:, 0:N], in_=bc[:, 0:N], func=AF.Square,
                                                                   bias=T.prm[:, j, 5:6]), reads=[bcn, 'prm'], writes=[y2n])

            def st(e, j=j, y2=y2):
                e.matmul(bsum[:, 0:N], lhsT=T.ones[:], rhs=T.ybf[:, j, 0:N], start=(j == 0), stop=(j == 7))
                return e.matmul(bsq[:, 0:N], lhsT=T.ones[:], rhs=y2[:, 0:N], start=(j == 0), stop=(j == 7))
            P.add('pe', st, reads=['ones', 'ybf.%d' % j, y2n], writes=[bsumn, bsqn])

    def conv_finish(N):
        ln_head(N)
        for j in range(8):
            ln_chunk(j, N)

    def ln_head(N):
        bsum, bsumn, isum, bsq, bsqn, isq = cst['b']
        P.add('act', lambda e: e.activation(out=T.mean[:, 0:N], in_=bsum[:, 0:N], func=AF.Copy, scale=1.0 / 1024),
              reads=[bsumn], writes=['mean'])
        P.add('dve', lambda e: e.tensor_tensor(out=T.m2[:, 0:N], in0=T.mean[:, 0:N], in1=T.mean[:, 0:N], op=ALU.mult),
              reads=['mean'], writes=['m2'])
        P.add('dve', lambda e: e.scalar_tensor_tensor(out=T.m2[:, 0:N], in0=bsq[:, 0:N], scalar=1.0 / 1024, in1=T.m2[:, 0:N],
                                                      op0=ALU.mult, op1=ALU.subtract), reads=[bsqn, 'm2'], writes=['m2'])
        P.add('dve', lambda e: e.tensor_scalar(out=T.m2[:, 0:N], in0=T.m2[:, 0:N], scalar1=EPS, scalar2=None, op0=ALU.add),
              reads=['m2'], writes=['m2'])
        P.add('act', lambda e: e.activation(out=T.rstdL[:, 0:N], in_=T.m2[:, 0:N], func=AF.Ln), reads=['m2'], writes=['rstdL'])
        P.add('act', lambda e: e.activation(out=T.rstdL[:, 0:N], in_=T.rstdL[:, 0:N], func=AF.Exp, scale=-0.5),
              reads=['rstdL'], writes=['rstdL'])
        held.discard(isum)
        held.discard(isq)

    def ln_chunk(j, N):
        i = rot('tl')
        tl = T.tl[i]
        tln = 'tl%d' % i
        P.add('dve', lambda e: e.tensor_tensor(out=tl[:, 0:N], in0=T.ybf[:, j, 0:N], in1=T.mean[:, 0:N], op=ALU.subtract),
              reads=['ybf.%d' % j, 'mean'], writes=[tln])
        P.add('dve', lambda e: e.tensor_tensor(out=tl[:, 0:N], in0=tl[:, 0:N], in1=T.rstdL[:, 0:N], op=ALU.mult),
              reads=[tln, 'rstdL'], writes=[tln])
        P.add('dve', lambda e: e.tensor_scalar(out=tl[:, 0:N], in0=tl[:, 0:N], scalar1=T.prm[:, j, 6:7],
                                               scalar2=T.prm[:, j, 7:8], op0=ALU.mult, op1=ALU.add),
              reads=[tln, 'prm'], writes=[tln])
        P.add('act', lambda e: e.activation(out=T.m2[:, 0:N], in_=tl[:, 0:N], func=AF.Sigmoid), reads=[tln], writes=['m2'])
        P.add('dve', lambda e: e.tensor_tensor(out=T.sT[:, j, 0:N], in0=tl[:, 0:N], in1=T.m2[:, 0:N], op=ALU.mult),
              reads=[tln, 'm2'], writes=['sT.%d' % j])

    def mix_stage(N, ucol, which, src=None, srcn=None, pre=None):
        wsrc = D.w_o_attn if which == 'a' else D.w_conv_out
        gcol = 3584 if which == 'a' else 4608
        if src is None:
            src = T.aoT if which == 'a' else T.sT
            srcn = cn('sT')
        for half in range(2):
            s1 = W.next(g_cols(wsrc, half * 512))
            s2 = W.next(g_cols(D.w_in, gcol + half * 512))
            w1, w2 = wslot(s1), wslot(s2)
            for jj in range(4):
                j = half * 4 + jj
                if pre is not None:
                    pre(j)
                b1, b1n, _ = bank()
                b2, b2n, _ = bank()

                def mm(e, b1=b1, b2=b2, jj=jj, w1=w1, w2=w2):
                    for kc in range(8):
                        e.matmul(b1[:, 0:N], lhsT=w1[:, kc, jj * 128:(jj + 1) * 128], rhs=src[:, kc, 0:N], start=(kc == 0),
                                 stop=(kc == 7))
                    for kc in range(8):
                        ins = e.matmul(b2[:, 0:N], lhsT=w2[:, kc, jj * 128:(jj + 1) * 128], rhs=T.uT[:, kc, ucol:ucol + N],
                                       start=(kc == 0), stop=(kc == 7))
                    return ins
                P.add('pe', mm, reads=srcn + cn('uT') + wnames(s1) + wnames(s2), writes=[b1n, b2n])
                i = rot('sg')
                sg = T.sg[i]
                sgn = 'sg%d' % i
                P.add('act', lambda e, sg=sg, b2=b2: e.activation(out=sg[:, 0:N], in_=b2[:, 0:N], func=AF.Sigmoid),
                      reads=[b2n], writes=[sgn])
                if which == 'a':
                    P.add('dve', lambda e, sg=sg, b1=b1, j=j: e.tensor_tensor(out=T.mixT[:, j, 0:N], in0=b1[:, 0:N],
                                                                              in1=sg[:, 0:N], op=ALU.mult),
                          reads=[b1n, sgn], writes=['mixT.%d' % j])
                else:
                    P.add('dve', lambda e, sg=sg, b1=b1, j=j: e.scalar_tensor_tensor(
                        out=sg[:, 0:N], in0=b1[:, 0:N], scalar=T.prm[:, j, 8:9], in1=sg[:, 0:N], op0=ALU.add, op1=ALU.mult),
                        reads=[b1n, sgn, 'prm'], writes=[sgn])
                    P.add('dve', lambda e, sg=sg, j=j: e.tensor_tensor(out=T.mixT[:, j, 0:N], in0=sg[:, 0:N],
                                                                       in1=T.mixT[:, j, 0:N], op=ALU.add),
                          reads=[sgn, 'mixT.%d' % j], writes=['mixT.%d' % j])
            W.release(s1)
            W.release(s2)

    def wout_stage(Pn, nsub):
        for nh in range(2):
            slot = W.next(g_cols(D.w_out, nh * 512))
            w = wslot(slot)
            for s in range(nsub):
                bk, bkn, _ = bank()

                def mm(e, bk=bk, s=s, w=w):
                    for kc in range(8):
                        ins = e.matmul(bk[:Pn, :], lhsT=T.mixT[:, kc, s * Pn:(s + 1) * Pn], rhs=w[:, kc, :], start=(kc == 0),
                                       stop=(kc == 7))
                    return ins
                P.add('pe', mm, reads=cn('mixT') + wnames(slot), writes=[bkn])
                xv = T.xbuf[:Pn, s, nh * 512:(nh + 1) * 512]
                P.add('dve', lambda e, bk=bk, xv=xv: e.tensor_tensor(out=xv, in0=bk[:Pn, :], in1=xv, op=ALU.add),
                      reads=[bkn, 'x.%d' % s], writes=['x.%d' % s])
            W.release(slot)

    def ffn_stage(Pn, nsub, ucol):
        N = Pn * nsub
        for g in range(8):
            slot = W.next(g_cols(D.w_ff1, g * 512))
            w = wslot(slot)
            for jj in range(4):
                f = 4 * g + jj
                bk, bkn, _ = bank()

                def mm(e, bk=bk, jj=jj, w=w):
                    for kc in range(8):
                        ins = e.matmul(bk[:, 0:N], lhsT=w[:, kc, jj * 128:(jj + 1) * 128], rhs=T.uT[:, kc, ucol:ucol + N],
                                       start=(kc == 0), stop=(kc == 7))
                    return ins
                P.add('pe', mm, reads=cn('uT') + wnames(slot), writes=[bkn])
                i = rot('sg')
                rl = T.sg[i]
                rln = 'sg%d' % i
                P.add('act', lambda e, rl=rl, bk=bk: e.activation(out=rl[:, 0:N], in_=bk[:, 0:N], func=AF.Relu),
                      reads=[bkn], writes=[rln])
                P.add('dve', lambda e, rl=rl, f=f: e.tensor_tensor(out=T.hid[:, f, 0:N], in0=rl[:, 0:N], in1=rl[:, 0:N],
                                                                   op=ALU.mult), reads=[rln], writes=['hid'])
            W.release(slot)
        for nh in range(2):
            bks = [bank() for _ in range(nsub)]
            for fg in range(4):
                slot = W.next(g_ff2(fg, nh))
                w = wslot(slot)
                for s in range(nsub):
                    bk, bkn, _ = bks[s]

                    def mm(e, bk=bk, s=s, w=w, fg=fg):
                        for fc in range(8):
                            ins = e.matmul(bk[:Pn, :], lhsT=T.hid[:, fg * 8 + fc, s * Pn:(s + 1) * Pn], rhs=w[:, fc, :],
                                           start=(fg == 0 and fc == 0), stop=(fg == 3 and fc == 7))
                        return ins
                    P.add('pe', mm, reads=['hid'] + wnames(slot), writes=[bkn])
                W.release(slot)
            for s in range(nsub):
                bk, bkn, _ = bks[s]
                xv = T.xbuf[:Pn, s, nh * 512:(nh + 1) * 512]
                P.add('dve', lambda e, bk=bk, xv=xv: e.tensor_tensor(out=xv, in0=bk[:Pn, :], in1=xv, op=ALU.add),
                      reads=[bkn, 'x.%d' % s], writes=['x.%d' % s])

    def ple_stage(Pn, nsub, ucol, psrc, ydst_fn, xnext_fn):
        P.add('sp', lambda e: e.dma_start(out=T.pbuf[:Pn, 0:nsub, :], in_=psrc), writes=['tl0', 'tl1'], dma='pbuf')
        P.add('dve', lambda e: e.tensor_copy(out=T.pbf[:Pn, 0:nsub, :], in_=T.pbuf[:Pn, 0:nsub, :]), reads=['tl0', 'tl1'],
              writes=['pbf'])
        bk, bkn, _ = bank()
        pb = psb(bk)
        N = Pn * nsub

        def tr(e):
            for c in range(2):
                for s in range(nsub):
                    ins = e.transpose(out=pb[:, c * 512 + s * Pn:c * 512 + (s + 1) * Pn],
                                      in_=T.pbf[:Pn, s, c * 128:(c + 1) * 128], identity=T.ident[:Pn, :Pn])
            return ins
        P.add('pe', tr, reads=['pbf', 'ident'], writes=[bkn])
        P.add('act', lambda e: e.activation(out=T.pT[:, :, 0:N], in_=pb.rearrange("p (c t) -> p c t", c=2)[:, :, 0:N],
                                            func=AF.Copy), reads=[bkn], writes=['pT'])
        sp_ = W.next(g_ple())
        wp = wslot(sp_, 2, 1024)
        gs = [W.next(g_cols(D.w_ple_gate, nh * 512)) for nh in range(2)]
        for s in range(nsub):
            yb = T.ybuf[s % 2]
            ybn = 'diag%d' % (s % 2)
            for nh in range(2):
                slot = gs[nh]
                w = wslot(slot)
                bg, bgn, _ = bank()
                bp, bpn, _ = bank()

                def mm(e, bg=bg, bp=bp, s=s, w=w, nh=nh):
                    for kc in range(8):
                        e.matmul(bg[:Pn, :], lhsT=T.uT[:, kc, ucol + s * Pn:ucol + (s + 1) * Pn], rhs=w[:, kc, :],
                                 start=(kc == 0), stop=(kc == 7))
                    for c in range(2):
                        ins = e.matmul(bp[:Pn, :], lhsT=T.pT[:, c, s * Pn:(s + 1) * Pn], rhs=wp[:, c, nh * 512:(nh + 1) * 512],
                                       start=(c == 0), stop=(c == 1))
                    return ins
                P.add('pe', mm, reads=cn('uT') + ['pT'] + wnames(slot) + wnames(sp_), writes=[bgn, bpn])
                i = rot('sg')
                sg = T.sg[i]
                sgn = 'sg%d' % i
                P.add('act', lambda e, sg=sg, bg=bg: e.activation(out=sg[:Pn, :], in_=bg[:Pn, :], func=AF.Sigmoid),
                      reads=[bgn], writes=[sgn])
                P.add('dve', lambda e, sg=sg, bp=bp: e.tensor_tensor(out=sg[:Pn, :], in0=bp[:Pn, :], in1=sg[:Pn, :],
                                                                     op=ALU.mult), reads=[bpn, sgn], writes=[sgn])
                xv = T.xbuf[:Pn, s, nh * 512:(nh + 1) * 512]
                P.add('dve', lambda e, sg=sg, xv=xv, yb=yb, nh=nh: e.tensor_tensor(out=yb[:Pn, nh * 512:(nh + 1) * 512],
                                                                               in0=sg[:Pn, :], in1=xv, op=ALU.add),
                      reads=[sgn, 'x.%d' % s], writes=[ybn])
            P.add('sp', lambda e, s=s, yb=yb: e.dma_start(out=ydst_fn(s), in_=yb[:Pn, :]), reads=[ybn], dma='yst')
            if xnext_fn is not None:
                P.add('sp', lambda e, s=s: e.dma_start(out=T.xbuf[:, s, :], in_=xnext_fn(s)), writes=['x.%d' % s],
                      dma='x%d' % s)
                if s >= 1:
                    rms_head_sub(lambda q: T.xbuf[:, q, :], ['x.%d' % (s - 1)], 128, s - 1)
        if xnext_fn is not None:
            rms_head_sub(lambda q: T.xbuf[:, q, :], ['x.%d' % (nsub - 1)], 128, nsub - 1)
        for slot in gs:
            W.release(slot)
        W.release(sp_)

    def a32_out(nrows, dst, r0):
        for hh in range(2):
            bk, bkn, _ = bank()

            def tr(e, bk=bk, hh=hh):
                for jj in range(4):
                    ins = e.transpose(out=bk[:nrows, jj * 128:(jj + 1) * 128], in_=T.a32[:, hh * 4 + jj, 0:nrows],
                                      identity=T.identf[:, :])
                return ins
            P.add('pe', tr, reads=['a32', 'identf'], writes=[bkn])
            P.add('dve', lambda e, bk=bk, hh=hh: e.tensor_copy(out=T.qkf[:nrows, hh * 512:(hh + 1) * 512], in_=bk[:nrows, :]),
                  reads=[bkn], writes=['qkf'])
        P.add('sp', lambda e: e.dma_start(out=dst, in_=T.qkf[r0:nrows, 0:1024]), reads=['qkf'], dma='qkfo')

    for it in range(NT_RUN):
        last = (it == NT - 1)
        r0 = 128 + it * TT
        if it == 0:
            for s in range(4):
                P.add('sp', lambda e, s=s: e.dma_start(out=T.xbuf[:, s, :], in_=D.x[128 + s * 128:256 + s * 128, :]),
                      writes=['x.%d' % s], dma='x%d' % s)
        P.add('sp', lambda e, it=it: e.dma_start(out=T.cos[:, 1:5, :], in_=D.cos[:, 1 + 4 * it:5 + 4 * it, :]), writes=['cos'], dma='cos')
        P.add('sp', lambda e, it=it: e.dma_start(out=T.sin[:, 1:5, :], in_=D.sin[:, 1 + 4 * it:5 + 4 * it, :]), writes=['sin'], dma='sin')
        if it == 0:
            P.add('sp', lambda e: e.dma_start(out=T.xh[:, :], in_=D.x[0:128, :]), writes=['sg0', 'sg1'], dma='xh')
            rms_T(lambda s: T.xh[:, :], [['sg0', 'sg1']], 128, 1, 0, T.uT, cn('uT'), 0)
        rms_T(lambda s: T.xbuf[:, s, :], cn('x', 4), 128, 4, 0, T.uT, cn('uT'), 128, heads_done=(it > 0))
        if STOP_STAGE == 1:
            return
        sq0 = W.next(g_cols(D.w_in, 0))
        sq1 = W.next(g_cols(D.w_in, 512))
        skv = W.next(g_cols(D.w_in, 1024))
        slots = (sq0, sq1, skv)
        if it == 0:
            qkv_sub(128, 0, 0, slots, 16, None, 'kT.0', 0, 0, False)
            P.add('dve', lambda e: e.tensor_scalar(out=T.Vaug[:, 0, :, :], in0=T.Vaug[:, 0, :, :], scalar1=T.flag[:, 0:1],
                                                   scalar2=None, op0=ALU.mult), reads=['V.0', 'flag'], writes=['V.0'])
        if STOP_STAGE == 11:
            return
        gmode = 'main0' if it == 0 else 'main'
        conv_begin()
        for s in range(4):
            A = s - 1
            if DIAG_ENG == 'dve' or s == 0:
                diag_build(2 * s)
                diag_build(2 * s + 1)
            glu_group(s, TT, 128, gmode, last)
            if A >= 0:
                attn_sc(A, 0)
            qkv_sub(128, 128 + s * 128, 1 + s, slots, 0, s * 128, 'kT.%d' % (s + 1), 128 + s * 128, s + 1,
                    last and s == 3, part='a')
            if s == 3:
                for sl_ in slots:
                    W.release(sl_)
            if A >= 0:
                attn_pv(A, 0)
                attn_sc(A, 1)
            conv_mm(2 * s, TT, False)
            if A >= 0:
                attn_pv(A, 1)
                attn_sc(A, 2)
            conv_mm(2 * s + 1, TT, False)
            if DIAG_ENG != 'dve' and s < 3:
                diag_build(2 * s + 2)
                diag_build(2 * s + 3)
            if A >= 0:
                attn_pv(A, 2)
                attn_sc(A, 3)
            qkv_sub(128, 128 + s * 128, 1 + s, slots, 0, s * 128, 'kT.%d' % (s + 1), 128 + s * 128, s + 1,
                    last and s == 3, part='b')
            if A >= 0:
                attn_pv(A, 3)
                attn_fin(A)
        attn_block(3)
        if last:
            P.add('sp', lambda e: e.dma_start(out=D.kp, in_=T.kvout[:, 0:256]), reads=['kvout.k'], dma='kvo')
            P.add('sp', lambda e: e.dma_start(out=D.vp, in_=T.kvout[:, 256:512]), reads=['kvout.v'], dma='kvo')
        ln_head(TT)
        mix_stage(TT, 128, 'a', src=T.aoTp, srcn=['aoTp'], pre=lambda j: ln_chunk(j, TT))
        if last:
            a32_out(32, D.cp, 2)
        mix_stage(TT, 128, 'c')
        if STOP_STAGE == 6:
            return
        wout_stage(128, 4)
        if STOP_STAGE == 7:
            return
        if not last:
            P.add('dve', lambda e: e.tensor_copy(out=T.kT[:, :, 0:128], in_=T.kT[:, :, 512:640]), reads=['kT.4'], writes=['kT.0'])
            P.add('dve', lambda e: e.tensor_copy(out=T.Vaug[:, 0, :, :], in_=T.Vaug[:, 4, :, :]), reads=['V.4'], writes=['V.0'])
            P.add('dve', lambda e: e.tensor_copy(out=T.aT[:, :, 0:30], in_=T.aT[:, :, 512:542]), reads=['aT.m%d' % j for j in range(8)], writes=['aT.h'])
        rms_T(lambda s: T.xbuf[:, s, :], cn('x', 4), 128, 4, 1, T.uT, cn('uT'), 128)
        ffn_stage(128, 4, 128)
        if STOP_STAGE == 8:
            return
        rms_T(lambda s: T.xbuf[:, s, :], cn('x', 4), 128, 4, 2, T.uT, cn('uT'), 128)
        p0 = it * TT
        r1 = 128 + (it + 1) * TT
        ple_stage(128, 4, 128, D.p[p0:p0 + TT, :].rearrange("(s p) n -> p s n", p=128),
                  lambda s, p0=p0: D.y[p0 + s * 128:p0 + (s + 1) * 128, :],
                  (lambda s, r1=r1: D.x[r1 + s * 128:r1 + (s + 1) * 128, :]) if it + 1 < NT_RUN else None)

    if not RUN_SAMPLE:
        return
    NS = 16
    P.add('sp', lambda e: e.dma_start(out=T.xbuf[:NS, 0, :], in_=D.xs), writes=['x.0'], dma='x0')
    P.add('sp', lambda e: e.dma_start(out=T.cos[:, 0, :], in_=D.cos[:, 33, :]), writes=['cos'], dma='cos')
    P.add('sp', lambda e: e.dma_start(out=T.sin[:, 0, :], in_=D.sin[:, 33, :]), writes=['sin'], dma='sin')
    rms_T(lambda s: T.xbuf[:NS, 0, :], ['x.0'], NS, 1, 0, T.uT, cn('uT'), 128)
    sq0 = W.next(g_cols(D.w_in, 0))
    sq1 = W.next(g_cols(D.w_in, 512))
    skv = W.next(g_cols(D.w_in, 1024))
    qkv_sub(NS, 128, 0, (sq0, sq1, skv), 0, 0, None, 0, 0, True)
    for sl_ in (sq0, sq1, skv):
        W.release(sl_)
    P.add('sp', lambda e: e.dma_start(out=D.ks[:, 127, :], in_=T.kvout[:NS, 0:256]), reads=['kvout.k'], writes=['ks.B'], dma='ksB')
    P.add('sp', lambda e: e.dma_start(out=D.vs[:, 127, :], in_=T.kvout[:NS, 256:512]), reads=['kvout.v'], writes=['vs.B'], dma='vsB')
    ksv = D.ks.rearrange("b j (h d) -> j b h d", h=4)
    P.add('pool', lambda e: e.dma_start(out=T.Kds[:, :, :, :], in_=ksv), reads=['ks.A', 'ks.B'],
          writes=['aT.m%d' % j for j in range(8)] + ['aT.h'], dma='kds')
    P.add('pool', lambda e: e.memset(T.Vs[:, :, :, 64:65], 1.0), writes=['Vs'])
    vsv = D.vs.rearrange("b j (h d) -> j b h d", h=4)
    for hk in range(4):
        P.add('pool', lambda e, hk=hk: e.dma_start(out=T.Vs[:, :, hk, 0:64], in_=vsv[:, :, hk, :]),
              reads=['vs.A', 'vs.B'], writes=['Vs'], dma='vsr')
    P.add('pool', lambda e: e.memset(T.Pexp[:], 0.0), writes=cn('ybf'))
    bS, bSn, iS = bank()
    held.add(iS)
    bS4 = bS[:, 0:256].rearrange("p (b h g) -> p b h g", b=16, h=4)
    for b in range(NS):
        bk, bkn, _ = bank()
        pb = psb(bk)

        P.add('dve', lambda e, b=b: e.tensor_copy(out=T.kdbf[:, :, :, :],
                                                  in_=T.Kds[:, b, :, :].unsqueeze(2).to_broadcast([128, 4, 2, 64])),
              reads=['aT.m%d' % j for j in range(8)] + ['aT.h'], writes=['kdbf'])

        def trk(e, b=b, pb=pb):
            kf = T.kdbf[:, :, :, :].rearrange("p h t d -> p (h t d)")
            for hk in range(4):
                ins = e.transpose(out=pb[:, hk * 128:(hk + 1) * 128], in_=kf[:, hk * 128:(hk + 1) * 128],
                                  identity=T.ident[:, :])
            return ins
        P.add('pe', trk, reads=['kdbf', 'ident'], writes=[bkn])
        i = rot('ev')
        kt = T.KTs[i]
        ktn = 'pT'
        P.add('dve', lambda e, kt=kt, pb=pb: e.tensor_copy(out=kt[:].rearrange("p h k -> p (h k)"), in_=pb[:, 0:512]),
              reads=[bkn], writes=[ktn])

        def sc(e, b=b, kt=kt):
            for par in (0, 1):
                if par == 1:
                    e.matmul(bS4[:, b, 0, 1:2], lhsT=T.ident[:, :], rhs=T.ident[:, 0:1], start=True, stop=True)
                for hk in range(4):
                    ins = e.matmul(bS4[:, b, hk, par::2], lhsT=kt[64 * par:64 * par + 64, hk, :],
                                   rhs=T.qT[64 * par:64 * par + 64, 2 * hk:2 * hk + 2, b], start=True, stop=True)
            return ins
        P.add('pe', sc, reads=[ktn, 'qT', 'ident'], writes=[bSn])
    P.add('act', lambda e: e.activation(out=T.Pexp[:].rearrange("p h b c -> p h (b c)")[:, :, ::17],
                                        in_=bS[:, 0:256].rearrange("p (b h) -> p h b", b=16), func=AF.Exp),
          reads=[bSn], writes=cn('ybf'))
    held.discard(iS)
    for hk in range(4):
        bO, bOn, _ = bank()

        def pvm(e, hk=hk, bO=bO):
            for g in range(4):
                for b in range(NS):
                    ins = e.matmul(bO[:NS, g * 65:(g + 1) * 65], lhsT=T.Pexp[:, 4 * hk + g, b, :], rhs=T.Vs[:, b, hk, :],
                                   start=(b == 0), stop=(b == NS - 1))
            return ins
        P.add('pe', pvm, reads=cn('ybf') + ['Vs'], writes=[bOn])
        normalize_heads(bO, bOn, hk, NS)
    ao_transpose(NS, 0)
    mix_stage(NS, 128, 'a')
    stv = D.st.rearrange("(i bb) k c -> i (bb k) c", i=4)
    for i4 in range(4):
        P.add('sp', lambda e, i4=i4: e.dma_start(out=T.xh[:120, :], in_=stv[i4]), writes=['sg0', 'sg1'], dma='xh')
        P.add('dve', lambda e, i4=i4: e.tensor_copy(out=T.xn[:120, i4, :], in_=T.xh[:120, :]), reads=['sg0', 'sg1'],
              writes=xnn(i4))
    for i4 in range(4):
        bk, bkn, _ = bank()
        pb = psb(bk)

        def trs(e, i4=i4, pb=pb):
            for c in range(8):
                ins = e.transpose(out=pb[:, c * 120:(c + 1) * 120], in_=T.xn[:120, i4, c * 128:(c + 1) * 128],
                                  identity=T.ident[:120, :120])
            return ins
        P.add('pe', trs, reads=cn('xn', 4) + ['ident'], writes=[bkn])
        P.add('dve', lambda e, i4=i4, pb=pb: e.tensor_copy(
            out=T.histT[:, :, 4 * i4:4 * i4 + 4, 0:30], in_=pb[:, 0:960].rearrange("p (c b k) -> p c b k", c=8, b=4)),
            reads=[bkn], writes=['qT'])
    glu_stage(NS, 128, 'sample', False)
    conv_ln_stage(NS, True)
    a32_out(NS, D.cs[:, 29, :], 0)
    mix_stage(NS, 128, 'c')
    wout_stage(NS, 1)
    rms_T(lambda s: T.xbuf[:NS, 0, :], ['x.0'], NS, 1, 1, T.uT, cn('uT'), 128)
    ffn_stage(NS, 1, 128)
    rms_T(lambda s: T.xbuf[:NS, 0, :], ['x.0'], NS, 1, 2, T.uT, cn('uT'), 128)
    ple_stage(NS, 1, 128, D.psamp.rearrange("(s p) n -> p s n", s=1), lambda s: D.ys, None)


def build_program():
    nc = bass.Bass("TRN2", target_bir_lowering=False)
    D = TT_()

    def din(name, shape):
        setattr(D, name, nc.dram_tensor(name, shape, F32, kind="ExternalInput").ap())

    def dout(name, shape):
        setattr(D, name, nc.dram_tensor(name, shape, F32, kind="ExternalOutput").ap())
    din("x", [128 + NT * TT, 1024]); din("p", [NT * TT, 256]); din("flag", [128, 1])
    din("cos", [128, 34, 32]); din("sin", [128, 34, 32])
    din("xs", [16, 1024]); din("psamp", [16, 256]); din("ck", [16, 128, 256]); din("cv", [16, 128, 256])
    din("st", [16, 30, 1024])
    for n, sh in [("ln1", [1024]), ("w_in", [1024, 5632]), ("b_glu", [2048]), ("q_norm", [64]), ("k_norm", [64]),
                  ("sinks", [16]), ("w_o_attn", [1024, 1024]), ("conv_dw", [31, 1024]), ("conv_dw_b", [1024]),
                  ("conv_ln_g", [1024]), ("conv_ln_b", [1024]), ("w_conv_out", [1024, 1024]), ("b_conv_out", [1024]),
                  ("w_out", [1024, 1024]), ("ln2", [1024]), ("w_ff1", [1024, 4096]), ("w_ff2", [4096, 1024]),
                  ("ln_ple", [1024]), ("w_ple_gate", [1024, 1024]), ("w_ple", [256, 1024])]:
        din(n, sh)
    dout("y", [NT * TT, 1024]); dout("ys", [16, 1024]); dout("kp", [128, 256]); dout("vp", [128, 256])
    dout("cp", [30, 1024]); dout("ks", [16, 128, 256]); dout("vs", [16, 128, 256]); dout("cs", [16, 30, 1024])

    rec = WRec()
    T0 = TT_()

    class _Any:
        def __getattr__(self, k):
            return _Any()

        def __getitem__(self, k):
            return _Any()

        def __call__(self, *a, **k):
            return _Any()
    for nme in ['ps', 'W', 'sg', 'PT', 'diag', 'y2', 'tl', 'rl', 'KTs']:
        setattr(T0, nme, [_Any() for _ in range(8)])

    class _T0(TT_):
        def __getattr__(self, k):
            return _Any()
    T0d = _T0()
    T0d.ps = T0.ps; T0d.W = T0.W; T0d.sg = T0.sg; T0d.PT = T0.PT; T0d.diag = T0.diag; T0d.y2 = T0.y2
    T0d.tl = T0.tl; T0d.rl = T0.rl; T0d.KTs = T0.KTs
    emit_all(DryProg(), rec, T0d, D)

    with ExitStack() as es:
        T = TT_()

        def sb(name, shape, dt):
            t = es.enter_context(nc.sbuf_tensor("sb_" + name, shape, dt))
            setattr(T, name, t)
            return t
        sb("identf", [128, 128], F32); sb("ident", [128, 128], BF16); sb("ones", [128, 128], BF16)
        sb("neghalf", [128, 32], F32); sb("maskf", [128, 2, 128], F32); sb("mask", [128, 2, 128], BF16)
        sb("cos", [128, 5, 32], F32); sb("sin", [128, 5, 32], F32); sb("flag", [128, 1], F32)
        sb("gq", [128, 64], F32); sb("gk", [128, 64], F32); sb("esink", [128, 16], F32)
        sb("prm", [128, 8, 40], F32); sb("wdw", [128, 8, 31], BF16)
        sb("gfull", [128, 20, 64], F32)
        sb("xbuf", [128, 4, 1024], F32)
        sb("ss", [128, 4], F32); sb("ms", [128, 4], F32); sb("rstd", [128, 4], F32)
        sb("uT", [128, 8, 640], BF16)
        sb("ssqk", [128, 20], F32); sb("msqk", [128, 20], F32); sb("rqk", [128, 20], F32)
        sb("qbf", [128, 16, 64], BF16); sb("kdbf", [128, 4, 2, 64], BF16); sb("kvout", [128, 512], F32)
        sb("kT", [128, 4, 640], BF16); sb("Vaug", [128, 5, 4, 65], BF16)
        T.PT = [sb("PT%d" % i, [128, 2, 512], BF16) for i in range(2)]
        sb("den", [128, 4], F32); sb("rden", [128, 4], F32); sb("ao", [128, 16, 64], BF16)
        sb("aT", [128, 8, 542], BF16); sb("a32", [128, 8, 32], F32)
        sg2 = sb("sg2", [128, 2, 512], F32)
        T.sg = [sg2[:, 0, :], sg2[:, 1, :]]
        T.xh = sg2[:].rearrange("p a b -> p (a b)")
        T.diag = [sb("diag%d" % i, [128, 31, 128], BF16) for i in range(2)]
        sb("ybf", [128, 8, 512], BF16)
        T.xn = T.ybf[:].rearrange("p j t -> p (j t)").rearrange("p (s n) -> p s n", s=4)
        y22 = sb("y22", [128, 2, 512], BF16)
        T.y2 = [y22[:, 0, :], y22[:, 1, :]]
        T.junk = y22[:].rearrange("p a b -> p (a b)")
        sb("ycs", [128, 16], F32)
        sb("mean", [128, 512], F32); sb("m2", [128, 512], F32); sb("rstdL", [128, 512], F32)
        tl2 = sb("tl2", [128, 2, 512], F32)
        T.tl = [tl2[:, 0, :], tl2[:, 1, :]]
        T.prows = tl2[:].rearrange("p a b -> p (a b)")
        T.pbuf = tl2[:].rearrange("p a b -> p (a b)").rearrange("p (s n) -> p s n", s=4)
        sb("sT", [128, 8, 512], BF16); sb("mixT", [128, 8, 512], BF16)
        T.aoT = T.sT
        sb("hid", [128, 32, 512], BF16)
        sb("pbf", [128, 4, 256], BF16); sb("pT", [128, 2, 512], BF16)
        T.KTs = [T.pT[:, i, :].rearrange("p (h k) -> p h k", h=4) for i in range(2)]
        T.W = [sb("W%d" % i, [128, 4096], BF16) for i in range(NSLOT)]
        T.ps = [es.enter_context(nc.psum_tensor("psum%d" % i, [128, 512], F32)) for i in range(8)]
        hflat = T.hid[:].rearrange("p f t -> p (f t)")
        T.qT = hflat[:, 0:4096].rearrange("p (c t) -> p c t", c=8)
        T.qkf = hflat[:, 4096:6656].bitcast(F32)
        T.tmpA = hflat[:, 6656:7936].bitcast(F32).rearrange("p (h d) -> p h d", h=20)
        T.qrot = hflat[:, 8192:10752].bitcast(F32)
        T.tmpB = hflat[:, 10752:12032].bitcast(F32).rearrange("p (h d) -> p h d", h=20)
        T.aoTp = hflat[:, 12032:16128].rearrange("p (c t) -> p c t", c=8)
        T.Vs = hflat[:, 12032:12032 + 4160].rearrange("p (b h d) -> p b h d", b=16, h=4)
        T.histT = hflat[:, 0:3968].rearrange("p (j b k) -> p j b k", j=8, b=16)
        T.Kds = T.aT[:].rearrange("p j t -> p (j t)")[:, 0:4096].rearrange("p (b h d) -> p b h d", b=16, h=4)
        T.ybuf = [T.diag[i][:].rearrange("p k c -> p (k c)")[:, 0:2048].bitcast(F32) for i in range(2)]
        T.Pexp = T.ybf[:].rearrange("p j t -> p (j t)").rearrange("p (h b c) -> p h b c", h=16, b=16)

        P = Prog(nc)
        ng = len(set(k for k, _ in rec.groups))
        scr = nc.dram_tensor("wscr", [max(ng, 1), 128, 4096], BF16, kind="Internal").ap()
        Wl = WLoader(P, T, rec.groups, scr, ng)
        emit_all(P, Wl, T, D)
        P.emit()
    return nc


_CACHE = {}


def _rope_tables(start):
    half = 32
    inv = np.power(np.float32(10000.0), -np.arange(half, dtype=np.float32) / np.float32(half)).astype(np.float32)
    pos = (start - 128 + np.arange(33 * 128)).astype(np.float32)
    ang = (pos[:, None] * inv[None, :]).astype(np.float32)
    angs = (np.float32(PAST_LEN) * inv).astype(np.float32)[None, :].repeat(128, 0)
    ang = ang.reshape(33, 128, 32).transpose(1, 0, 2)
    ang = np.concatenate([ang, angs[:, None, :]], axis=1)
    return np.ascontiguousarray(np.cos(ang).astype(np.float32)), np.ascontiguousarray(np.sin(ang).astype(np.float32))


def kernel(**inputs):
    in_maps = _prep(**inputs)
    if 'nc' not in _CACHE:
        _CACHE['nc'] = build_program()
    nc = _CACHE['nc']
    res = run_bass_kernel_spmd(nc, in_maps, core_ids=list(range(8)))
    return _assemble(res.results)


def _prep(x_prompt, x_sample, cache_k, cache_v, state_conv, p_prompt, p_sample,
          ln1, w_in, b_glu, q_norm, k_norm, sinks, w_o_attn, conv_dw, conv_dw_b,
          conv_ln_g, conv_ln_b, w_conv_out, b_conv_out, w_out, ln2, w_ff1, w_ff2,
          ln_ple, w_ple_gate, w_ple):
    f = lambda a: np.ascontiguousarray(np.asarray(a, dtype=np.float32))
    x_prompt, x_sample, cache_k, cache_v, state_conv, p_prompt, p_sample = map(
        f, (x_prompt, x_sample, cache_k, cache_v, state_conv, p_prompt, p_sample))
    wts = dict(ln1=f(ln1)[0], w_in=f(w_in)[0], b_glu=f(b_glu)[0], q_norm=f(q_norm)[0], k_norm=f(k_norm)[0],
               sinks=f(sinks)[0], w_o_attn=f(w_o_attn)[0], conv_dw=f(conv_dw)[0], conv_dw_b=f(conv_dw_b)[0],
               conv_ln_g=f(conv_ln_g)[0], conv_ln_b=f(conv_ln_b)[0], w_conv_out=f(w_conv_out)[0],
               b_conv_out=f(b_conv_out)[0], w_out=f(w_out)[0], ln2=f(ln2)[0], w_ff1=f(w_ff1)[0], w_ff2=f(w_ff2)[0],
               ln_ple=f(ln_ple)[0], w_ple_gate=f(w_ple_gate)[0], w_ple=f(w_ple)[0])
    in_maps = []
    L = NT * TT
    for c in range(8):
        b, half = c // 2, c % 2
        start = half * L
        xc = np.zeros((128 + L, 1024), np.float32)
        if half == 1:
            xc[:] = x_prompt[b, start - 128:start + L]
        else:
            xc[128:] = x_prompt[b, 0:L]
        cs_, sn_ = _rope_tables(start)
        m = dict(wts)
        m.update(x=xc, p=np.ascontiguousarray(p_prompt[0, b, start:start + L]),
                 flag=np.full((128, 1), float(half), np.float32), cos=cs_, sin=sn_,
                 xs=np.ascontiguousarray(x_sample[16 * c:16 * c + 16, 0]),
                 psamp=np.ascontiguousarray(p_sample[0, 16 * c:16 * c + 16, 0]),
                 ck=np.ascontiguousarray(cache_k[0, 16 * c:16 * c + 16].reshape(16, 128, 256)),
                 cv=np.ascontiguousarray(cache_v[0, 16 * c:16 * c + 16].reshape(16, 128, 256)),
                 st=np.ascontiguousarray(state_conv[0, 16 * c:16 * c + 16]))
        in_maps.append(m)
    return in_maps


def _assemble(R):
    L = NT * TT
    y_prompt = np.zeros((4, 2 * L, 1024), np.float32)
    y_sample = np.zeros((128, 1, 1024), np.float32)
    nkp = np.zeros((1, 4, 128, 4, 64), np.float32)
    nvp = np.zeros((1, 4, 128, 4, 64), np.float32)
    ncp = np.zeros((1, 4, 30, 1024), np.float32)
    nks = np.zeros((1, 128, 128, 4, 64), np.float32)
    nvs = np.zeros((1, 128, 128, 4, 64), np.float32)
    ncs = np.zeros((1, 128, 30, 1024), np.float32)
    for c in range(8):
        b, half = c // 2, c % 2
        r = R[c]
        y_prompt[b, half * L:(half + 1) * L] = r["y"]
        y_sample[16 * c:16 * c + 16, 0] = r["ys"]
        if half == 1:
            nkp[0, b] = r["kp"].reshape(128, 4, 64)
            nvp[0, b] = r["vp"].reshape(128, 4, 64)
            ncp[0, b] = r["cp"]
        nks[0, 16 * c:16 * c + 16] = r["ks"].reshape(16, 128, 4, 64)
        nvs[0, 16 * c:16 * c + 16] = r["vs"].reshape(16, 128, 4, 64)
        ncs[0, 16 * c:16 * c + 16] = r["cs"]
    return (y_prompt, y_sample, nkp, nvp, ncp, nks, nvs, ncs)
```

```python
import numpy as np
from contextlib import ExitStack
import concourse.bass as bass
import concourse.mybir as mybir
from concourse.bass_utils import run_bass_kernel_spmd

F32 = mybir.dt.float32
BF16 = mybir.dt.bfloat16
AF = mybir.ActivationFunctionType
ALU = mybir.AluOpType
AX = mybir.AxisListType

ENG_ATTR = {'pe': 'tensor', 'act': 'scalar', 'dve': 'vector', 'pool': 'gpsimd', 'sp': 'sync'}
EPS = 1e-6
NT = 8
TT = 512
NSLOT = 6
PAST_LEN = 16384
NT_RUN = NT
STRICT = True
STOP_STAGE = 99
KC_PER = 8
USE_SCRATCH = True
DEFER_HALF = True
SPLIT_RMS = True
MAX_OPS = 10 ** 9
RUN_SAMPLE = True


LOCK_SHARED = {'qT', 'qkf', 'tmpA', 'qrot', 'tmpB', 'Vs', 'aoTp'}


class _Op:
    __slots__ = ('eng', 'fn', 'deps', 'dma', 'sig', 'val')

    def __init__(self, eng, fn, deps, dma):
        self.eng, self.fn, self.deps, self.dma = eng, fn, deps, dma
        self.sig = False
        self.val = 0


class Prog:
    def __init__(self, nc):
        self.nc = nc
        self.ops = []
        self.lastw = {}
        self.rds = {}

    def add(self, eng, fn, reads=(), writes=(), dma=None):
        idx = len(self.ops)
        if idx >= MAX_OPS:
            return idx
        reads = list(reads)
        writes = list(writes)
        alln = reads + writes
        if any(n in LOCK_SHARED for n in alln):
            reads.append('hidlock')
        if 'hid' in alln:
            writes.append('hidlock')
        deps = {}
        for b in reads:
            w = self.lastw.get(b)
            if w is not None:
                deps[w] = 'raw'
            if b.startswith('ps'):
                for r in self.rds.get(b, ()):
                    if self.ops[r].eng != eng and r not in deps:
                        deps[r] = 'psx'
        for b in writes:
            w = self.lastw.get(b)
            if w is not None and w not in deps:
                deps[w] = 'waw'
            for r in self.rds.get(b, ()):
                if r not in deps:
                    deps[r] = 'war'
        self.ops.append(_Op(eng, fn, deps, dma))
        for b in reads:
            self.rds.setdefault(b, []).append(idx)
        for b in writes:
            self.lastw[b] = idx
            self.rds[b] = []
        return idx

    def emit(self):
        nc = self.nc
        ops = self.ops
        for op in ops:
            keep = {}
            for d, kind in op.deps.items():
                D = ops[d]
                if D.dma is None and op.dma is None and D.eng == op.eng:
                    if op.eng == 'pe':
                        continue
                    if kind != 'raw' and not STRICT:
                        continue
                keep[d] = kind
                if D.dma is None:
                    D.sig = True
            op.deps = keep
        cnt = {}
        dcnt = {}
        dpos = {}
        for oi, op in enumerate(ops):
            if op.dma is not None:
                dcnt[op.dma] = dcnt.get(op.dma, 0) + 1
                dpos.setdefault(op.dma, []).append(oi)
                op.val = 16 * dcnt[op.dma]
            elif op.sig:
                cnt[op.eng] = cnt.get(op.eng, 0) + 1
                op.val = cnt[op.eng]
        engines = ['pe', 'act', 'dve', 'pool', 'sp']
        with ExitStack() as es:
            esem = {e: es.enter_context(nc.semaphore("s_" + e)) for e in engines}
            dsem = {k: es.enter_context(nc.semaphore("d_%d" % i)) for i, k in enumerate(sorted(dcnt))}
            block = es.enter_context(nc.Block())

            import bisect

            def body(e, eng):
                waited = {}
                for oi, op in enumerate(ops):
                    if op.eng != eng:
                        continue
                    need = {}
                    for d in op.deps:
                        D = ops[d]
                        key = ('d', D.dma) if D.dma is not None else ('e', D.eng)
                        v = D.val
                        if D.dma is not None:
                            v = 16 * bisect.bisect_left(dpos[D.dma], oi)
                        if v > need.get(key, 0):
                            need[key] = v
                    for key, v in need.items():
                        if waited.get(key, 0) >= v:
                            continue
                        waited[key] = v
                        e.wait_ge(dsem[key[1]] if key[0] == 'd' else esem[key[1]], v)
                    ins = op.fn(e)
                    if op.dma is not None:
                        ins.then_inc(dsem[op.dma], 16)
                    elif op.sig:
                        ins.then_inc(esem[eng], 1)
                if eng == 'sp':
                    for k, c in dcnt.items():
                        if waited.get(('d', k), 0) < 16 * c:
                            e.wait_ge(dsem[k], 16 * c)
                    for en, c in cnt.items():
                        if c:
                            e.wait_ge(esem[en], c)

            for eng in engines:
                getattr(block, ENG_ATTR[eng])(lambda e, eng=eng: body(e, eng))
        return nc


class DryProg:
    def add(self, *a, **k):
        return 0


class WRec:
    def __init__(self):
        self.groups = []

    def next(self, spec):
        self.groups.append(spec)
        return 0

    def release(self, slot):
        pass

    def prefetch(self):
        pass


class WLoader:
    def __init__(self, P, T, groups, scr, ng):
        self.P, self.T, self.groups = P, T, groups
        self.scr, self.ng = scr, ng
        self.use = 0
        self.load = 0
        self.free = list(range(NSLOT))
        self.slot_of = {}
        self.kidx = {}
        self.seen = {}

    def _issue(self, gi, slot):
        wt = self.T.W[slot]
        key, spec = self.groups[gi]
        if key in self.kidx and USE_SCRATCH:
            g0 = self.kidx[key]
            self.P.add('pool', lambda e, wt=wt, g0=g0: e.dma_start(out=wt[:, :], in_=self.scr[g0]),
                       reads=['wscr.%d' % g0], writes=wnames(slot), dma='w%d' % slot)
            return
        occ = self.seen.get(key, 0)
        self.seen[key] = occ + 1
        first = (occ >= 1) or (len(self.seen) % 2 == 0) or not DEFER_HALF
        if first:
            self.kidx[key] = len(self.kidx)
        gidx = self.kidx.get(key, -1)
        for pi, (src, d0, d1, a, b) in enumerate(spec):
            dst = wt[:, d0:d1].rearrange("p (a b) -> p a b", a=a)
            for k0 in range(0, a, KC_PER):
                k1 = min(a, k0 + KC_PER)
                self.P.add('pool', lambda e, dst=dst, src=src, k0=k0, k1=k1: e.dma_start(out=dst[:, k0:k1, :], in_=src[:, k0:k1, :]),
                           writes=['w%d.%d.%d' % (slot, pi, k0)], dma='w%d' % slot)
        if USE_SCRATCH and first:
            self.P.add('sp', lambda e, wt=wt, gidx=gidx: e.dma_start(out=self.scr[gidx], in_=wt[:, :]),
                       reads=wnames(slot), writes=['wscr.%d' % gidx], dma='wst%d' % slot)

    def _prefetch(self):
        while self.load < len(self.groups) and self.free:
            slot = self.free.pop(0)
            self.slot_of[self.load] = slot
            self._issue(self.load, slot)
            self.load += 1

    def next(self, spec):
        self._prefetch()
        assert self.use in self.slot_of, "no free weight slot (too many groups held)"
        slot = self.slot_of.pop(self.use)
        self.use += 1
        return slot

    def release(self, slot):
        self.free.append(slot)
        self._prefetch()

    def prefetch(self):
        self._prefetch()


def wnames(slot):
    return ['w%d.%d.%d' % (slot, pi, k0) for pi in range(2) for k0 in range(0, 8, KC_PER)]


class TT_:
    pass


def emit_all(P, W, T, D):
    bankctr = [0]
    held = set()

    def bank():
        while True:
            i = bankctr[0] % 8
            bankctr[0] += 1
            if i not in held:
                return T.ps[i], 'ps%d' % i, i

    def psb(bk):
        return bk[:].bitcast(BF16)

    rr = {'sg': 0, 'pt': 0, 'dg': 0, 'y2': 0, 'tl': 0, 'rl': 0, 'ev': 0}

    def rot(key, n=2):
        rr[key] = (rr[key] + 1) % n
        return rr[key]

    def cn(base, n=8):
        if base == 'xn':
            return cn('ybf', 2 * n)
        return ['%s.%d' % (base, i) for i in range(n)]

    def xnn(s):
        return ['ybf.%d' % (2 * s), 'ybf.%d' % (2 * s + 1)]

    def wv(ap2d, c0, ncol, kc=8):
        return ap2d.rearrange("(kc p) n -> p kc n", p=128)[:, :, c0:c0 + ncol]

    def g_cols(ap2d, c0):
        return (('c', ap2d.tensor.name, c0), [(wv(ap2d, c0, 512), 0, 4096, 8, 512)])

    def g_glu(gi):
        return (('glu', gi), [(wv(D.w_in, 1536 + gi * 256, 256), 0, 2048, 8, 256),
                              (wv(D.w_in, 2560 + gi * 256, 256), 2048, 4096, 8, 256)])

    def g_ff2(fg, nh):
        src = D.w_ff2.rearrange("(fc p) n -> p fc n", p=128)[:, fg * 8:(fg + 1) * 8, nh * 512:(nh + 1) * 512]
        return (('ff2', fg, nh), [(src, 0, 4096, 8, 512)])

    def g_ple():
        return (('ple',), [(D.w_ple.rearrange("(kc p) n -> p kc n", p=128), 0, 2048, 2, 1024)])

    def wslot(slot, a=8, b=512):
        return T.W[slot][:, 0:a * b].rearrange("p (a b) -> p a b", a=a)

    W.prefetch()
    P.add('pool', lambda e: e.memset(T.identf[:], 0.0), writes=['identf'])
    P.add('pool', lambda e: e.affine_select(out=T.identf[:], in_=T.identf[:], pattern=[[-1, 128]],
                                            compare_op=ALU.not_equal, fill=1.0, base=0, channel_multiplier=1),
          reads=['identf'], writes=['identf'])
    P.add('dve', lambda e: e.tensor_copy(out=T.ident[:], in_=T.identf[:]), reads=['identf'], writes=['ident'])
    P.add('pool', lambda e: e.memset(T.ones[:], 1.0), writes=['ones'])
    P.add('pool', lambda e: e.memset(T.neghalf[:], -0.5), writes=['neghalf'])
    P.add('pool', lambda e: e.memset(T.maskf[:], 1.0), writes=['maskf'])
    P.add('pool', lambda e: e.affine_select(out=T.maskf[:, 0, :], in_=T.maskf[:, 0, :], pattern=[[-1, 128]],
                                            compare_op=ALU.is_gt, fill=0.0, base=0, channel_multiplier=1),
          reads=['maskf'], writes=['maskf'])
    P.add('pool', lambda e: e.affine_select(out=T.maskf[:, 1, :], in_=T.maskf[:, 1, :], pattern=[[1, 128]],
                                            compare_op=ALU.is_ge, fill=0.0, base=0, channel_multiplier=-1),
          reads=['maskf'], writes=['maskf'])
    P.add('dve', lambda e: e.tensor_copy(out=T.mask[:], in_=T.maskf[:]), reads=['maskf'], writes=['mask'])
    P.add('pool', lambda e: e.memset(T.Vaug[:], 1.0), writes=cn('V', 5))
    P.add('sp', lambda e: e.dma_start(out=T.cos[:, 0, :], in_=D.cos[:, 0, :]), writes=['cos'], dma='cos')
    P.add('sp', lambda e: e.dma_start(out=T.sin[:, 0, :], in_=D.sin[:, 0, :]), writes=['sin'], dma='sin')
    P.add('sp', lambda e: e.dma_start(out=T.flag[:], in_=D.flag), writes=['flag'], dma='flag')
    P.add('sp', lambda e: e.dma_start(out=T.gq[:], in_=D.q_norm.partition_broadcast(128)), writes=['gq'], dma='gq')
    P.add('sp', lambda e: e.dma_start(out=T.gk[:], in_=D.k_norm.partition_broadcast(128)), writes=['gk'], dma='gk')
    P.add('sp', lambda e: e.dma_start(out=T.esink[:], in_=D.sinks.partition_broadcast(128)), writes=['esink'], dma='esink')
    rows = [D.ln1, D.ln2, D.ln_ple, D.b_glu[0:1024], D.b_glu[1024:2048], D.conv_dw_b, D.conv_ln_g, D.conv_ln_b,
            D.b_conv_out]
    for r, src in enumerate(rows):
        P.add('sp', lambda e, r=r, src=src: e.dma_start(out=T.prows[r:r + 1, :], in_=src.rearrange("(o n) -> o n", o=1)),
              writes=['pr%d' % r], dma='prm')
    P.add('sp', lambda e: e.dma_start(out=T.prows[9:40, :], in_=D.conv_dw), writes=['pr9'], dma='prm')
    if RUN_SAMPLE:
        P.add('sp', lambda e: e.dma_start(out=D.ks[:, 0:127, :], in_=D.ck[:, 1:128, :]), writes=['ks.A'], dma='ksA')
        P.add('sp', lambda e: e.dma_start(out=D.vs[:, 0:127, :], in_=D.cv[:, 1:128, :]), writes=['vs.A'], dma='vsA')
        P.add('sp', lambda e: e.dma_start(out=D.cs[:, 0:29, :], in_=D.st[:, 1:30, :]), writes=['cs.A'], dma='csA')
    bk, bkn, _ = bank()

    def ptr(e):
        for c in range(8):
            ins = e.transpose(out=bk[:, c * 40:(c + 1) * 40], in_=T.prows[0:40, c * 128:(c + 1) * 128],
                              identity=T.identf[0:40, 0:40])
        return ins
    P.add('pe', ptr, reads=['pr%d' % r for r in range(10)] + ['identf', 'tl0', 'tl1'], writes=[bkn])
    P.add('dve', lambda e: e.tensor_copy(out=T.prm[:].rearrange("p c r -> p (c r)"), in_=bk[:, 0:320]),
          reads=[bkn], writes=['prm'])
    P.add('dve', lambda e: e.tensor_copy(out=T.wdw[:], in_=T.prm[:, :, 9:40]), reads=['prm'], writes=['wdw'])
    P.add('dve', lambda e: e.tensor_scalar(out=T.gfull[:, 0:16, :], in0=T.gq[:].unsqueeze(1).to_broadcast([128, 16, 64]),
                                           scalar1=0.125, scalar2=None, op0=ALU.mult), reads=['gq'], writes=['gfull'])
    P.add('dve', lambda e: e.tensor_copy(out=T.gfull[:, 16:20, :], in_=T.gk[:].unsqueeze(1).to_broadcast([128, 4, 64])),
          reads=['gk'], writes=['gfull'])
    P.add('act', lambda e: e.activation(out=T.esink[:], in_=T.esink[:], func=AF.Exp), reads=['esink'], writes=['esink'])

    def rms_head_sub(src_fn, src_names, Pn, s):
        P.add('act', lambda e: e.activation(out=T.xn[:Pn, s, :], in_=src_fn(s), func=AF.Square,
                                            accum_out=T.ss[:Pn, s:s + 1]), reads=src_names, writes=xnn(s) + ['ss.%d' % s])
        P.add('dve', lambda e: e.tensor_scalar(out=T.ms[:Pn, s:s + 1], in0=T.ss[:Pn, s:s + 1], scalar1=1.0 / 1024,
                                               scalar2=EPS, op0=ALU.mult, op1=ALU.add), reads=['ss.%d' % s], writes=['ms.%d' % s])
        P.add('pool', lambda e: e.tensor_tensor(out=T.rstd[:Pn, s:s + 1], in0=T.ms[:Pn, s:s + 1],
                                                in1=T.neghalf[:Pn, s:s + 1], op=ALU.pow),
              reads=['ms.%d' % s, 'neghalf'], writes=['rstd.%d' % s])
        P.add('dve', lambda e: e.tensor_scalar(out=T.xn[:Pn, s, :], in0=src_fn(s), scalar1=T.rstd[:Pn, s:s + 1],
                                               scalar2=None, op0=ALU.mult), reads=src_names + ['rstd.%d' % s], writes=xnn(s))

    def rms_T(src_fn, src_names, Pn, nsub, gidx, dst, dnames, col0, heads_done=False):
        N = Pn * nsub
        src_names = [n if isinstance(n, list) else [n] for n in src_names]
        if heads_done:
            return rms_tail(Pn, nsub, gidx, dst, dnames, col0)
        for s in range(nsub):
            if True:
                P.add('act', lambda e, s=s: e.activation(out=T.xn[:Pn, s, :], in_=src_fn(s), func=AF.Square,
                                                          accum_out=T.ss[:Pn, s:s + 1]),
                      reads=src_names[s], writes=xnn(s) + ['ss.%d' % s])
            else:
                P.add('dve', lambda e, s=s: e.tensor_tensor_reduce(out=T.xn[:Pn, s, :], in0=src_fn(s), in1=src_fn(s),
                                                                   scale=1.0, scalar=0.0, op0=ALU.mult, op1=ALU.add,
                                                                   accum_out=T.ss[:Pn, s:s + 1]),
                      reads=src_names[s], writes=xnn(s) + ['ss.%d' % s])
        P.add('dve', lambda e: e.tensor_scalar(out=T.ms[:Pn, 0:nsub], in0=T.ss[:Pn, 0:nsub], scalar1=1.0 / 1024,
                                               scalar2=EPS, op0=ALU.mult, op1=ALU.add), reads=['ss.%d' % s for s in range(nsub)],
              writes=['ms.%d' % s for s in range(nsub)])
        P.add('pool', lambda e: e.tensor_tensor(out=T.rstd[:Pn, 0:nsub], in0=T.ms[:Pn, 0:nsub],
                                                in1=T.neghalf[:Pn, 0:nsub], op=ALU.pow),
              reads=['ms.%d' % s for s in range(nsub)] + ['neghalf'], writes=['rstd.%d' % s for s in range(nsub)])
        for s in range(nsub):
            if s % 2 == 0 or not SPLIT_RMS:
                P.add('dve', lambda e, s=s: e.tensor_scalar(out=T.xn[:Pn, s, :], in0=src_fn(s), scalar1=T.rstd[:Pn, s:s + 1],
                                                            scalar2=None, op0=ALU.mult),
                      reads=src_names[s] + ['rstd.%d' % s], writes=xnn(s))
            else:
                P.add('act', lambda e, s=s: e.activation(out=T.xn[:Pn, s, :], in_=src_fn(s), func=AF.Copy,
                                                          scale=T.rstd[:Pn, s:s + 1]),
                      reads=src_names[s] + ['rstd.%d' % s], writes=xnn(s))
        rms_tail(Pn, nsub, gidx, dst, dnames, col0)

    def rms_tail(Pn, nsub, gidx, dst, dnames, col0):
        N = Pn * nsub
        for c in range(8):
            bk, bkn, _ = bank()
            pb = psb(bk)

            def tr(e, c=c, pb=pb):
                for s in range(nsub):
                    ins = e.transpose(out=pb[:, s * Pn:(s + 1) * Pn], in_=T.xn[:Pn, s, c * 128:(c + 1) * 128],
                                      identity=T.ident[:Pn, :Pn])
                return ins
            P.add('pe', tr, reads=sum([xnn(s) for s in range(nsub)], []) + ['ident'], writes=[bkn])
            if c % 2 == 0:
                P.add('act', lambda e, c=c, pb=pb: e.activation(out=dst[:, c, col0:col0 + N], in_=pb[:, 0:N], func=AF.Copy,
                                                                scale=T.prm[:, c, gidx:gidx + 1]),
                      reads=[bkn, 'prm'], writes=[dnames[c]])
            else:
                P.add('dve', lambda e, c=c, pb=pb: e.tensor_scalar(out=dst[:, c, col0:col0 + N], in0=pb[:, 0:N],
                                                                   scalar1=T.prm[:, c, gidx:gidx + 1], scalar2=None,
                                                                   op0=ALU.mult),
                      reads=[bkn, 'prm'], writes=[dnames[c]])

    def qkv_sub(Pn, ucol, tblk, slots, h0, qcol, kname, kcol, vblk, kvout, part='ab'):
        if 'a' in part:
            qkv_a(Pn, ucol, tblk, slots, h0, vblk, kvout)
        if 'b' in part:
            qkv_b(Pn, h0, qcol, kname, kcol)

    def qkv_a(Pn, ucol, tblk, slots, h0, vblk, kvout):
        sq0, sq1, skv = slots
        e0 = h0 * 64
        banks = []
        grp = ([(sq0, 0), (sq1, 512)] if h0 == 0 else []) + [(skv, 1024)]
        for slot, qo in grp:
            bk, bkn, _ = bank()
            banks.append((bk, bkn, qo))

            def mm(e, slot=slot, bk=bk):
                w = wslot(slot)
                for kc in range(8):
                    ins = e.matmul(bk[:Pn, :], lhsT=T.uT[:, kc, ucol:ucol + Pn], rhs=w[:, kc, :], start=(kc == 0),
                                   stop=(kc == 7))
                return ins
            P.add('pe', mm, reads=cn('uT') + wnames(slot), writes=[bkn])
        for bk, bkn, qo in banks:
            ncol = 512 if qo < 1024 else 256
            P.add('act', lambda e, bk=bk, qo=qo, ncol=ncol: e.activation(out=T.qkf[:Pn, qo:qo + ncol], in_=bk[:Pn, 0:ncol],
                                                                          func=AF.Copy),
                  reads=[bkn], writes=['qkf'])
        bkv, bkvn, _ = banks[-1]
        P.add('dve', lambda e: e.tensor_copy(out=T.Vaug[:Pn, vblk, :, 0:64],
                                             in_=bkv[:Pn, 256:512].rearrange("p (h d) -> p h d", h=4)),
              reads=[bkvn], writes=['V.%d' % vblk])
        if kvout:
            P.add('act', lambda e: e.activation(out=T.kvout[:Pn, 256:512], in_=bkv[:Pn, 256:512], func=AF.Copy),
                  reads=[bkvn], writes=['kvout.v'])
        nh_ = 20 - h0
        qv = T.qkf[:Pn, e0:1280]
        rv = T.qrot[:Pn, e0:1280]
        P.add('act', lambda e: e.activation(out=rv, in_=qv, func=AF.Square), reads=['qkf'], writes=['qrot'])
        P.add('dve', lambda e: e.tensor_reduce(out=T.ssqk[:Pn, h0:20], in_=rv.rearrange("p (h d) -> p h d", d=64),
                                               axis=AX.X, op=ALU.add), reads=['qrot'], writes=['ssqk'])
        P.add('dve', lambda e: e.tensor_scalar(out=T.msqk[:Pn, h0:20], in0=T.ssqk[:Pn, h0:20], scalar1=1.0 / 64,
                                               scalar2=EPS, op0=ALU.mult, op1=ALU.add), reads=['ssqk'], writes=['msqk'])
        P.add('pool', lambda e: e.tensor_tensor(out=T.rqk[:Pn, h0:20], in0=T.msqk[:Pn, h0:20], in1=T.neghalf[:Pn, h0:20],
                                                op=ALU.pow), reads=['msqk', 'neghalf'], writes=['rqk'])
        P.add('dve', lambda e: e.tensor_tensor(out=qv, in0=qv, in1=T.gfull[:Pn, h0:20, :].rearrange("p h d -> p (h d)"),
                                               op=ALU.mult), reads=['qkf', 'gfull'], writes=['qkf'])
        g4 = qv.rearrange("p (h t d) -> p h t d", t=2, d=32)
        r4 = rv.rearrange("p (h t d) -> p h t d", t=2, d=32)
        g1, g2 = g4[:, :, 0, :], g4[:, :, 1, :]
        cb = T.cos[:Pn, tblk, :].unsqueeze(1).to_broadcast([Pn, nh_, 32])
        sb_ = T.sin[:Pn, tblk, :].unsqueeze(1).to_broadcast([Pn, nh_, 32])
        tA = T.tmpA[:Pn, 0:nh_, :]
        tB = T.tmpB[:Pn, 0:nh_, :]
        P.add('dve', lambda e: e.tensor_tensor(out=tA, in0=g1, in1=cb, op=ALU.mult), reads=['qkf', 'cos'], writes=['tmpA'])
        P.add('dve', lambda e: e.tensor_tensor(out=tB, in0=g2, in1=sb_, op=ALU.mult), reads=['qkf', 'sin'], writes=['tmpB'])
        P.add('dve', lambda e: e.tensor_tensor(out=r4[:, :, 0, :], in0=tA, in1=tB, op=ALU.subtract),
              reads=['tmpA', 'tmpB'], writes=['qrot'])
        P.add('dve', lambda e: e.tensor_tensor(out=tA, in0=g2, in1=cb, op=ALU.mult), reads=['qkf', 'cos'], writes=['tmpA'])
        P.add('dve', lambda e: e.tensor_tensor(out=tB, in0=g1, in1=sb_, op=ALU.mult), reads=['qkf', 'sin'], writes=['tmpB'])
        P.add('dve', lambda e: e.tensor_tensor(out=r4[:, :, 1, :], in0=tA, in1=tB, op=ALU.add),
              reads=['tmpA', 'tmpB'], writes=['qrot'])
        r3 = T.qrot[:Pn, :].rearrange("p (h d) -> p h d", d=64)
        if h0 == 0:
            P.add('dve', lambda e: e.tensor_tensor(out=T.qbf[:Pn, :, :], in0=r3[:, 0:16, :],
                                                   in1=T.rqk[:Pn, 0:16].unsqueeze(2).to_broadcast([Pn, 16, 64]),
                                                   op=ALU.mult), reads=['qrot', 'rqk'], writes=['qbf'])
        P.add('dve', lambda e: e.tensor_tensor(
            out=T.kdbf[:Pn, :, :, :], in0=r3[:, 16:20, :].unsqueeze(2).to_broadcast([Pn, 4, 2, 64]),
            in1=T.rqk[:Pn, 16:20].unsqueeze(2).unsqueeze(3).to_broadcast([Pn, 4, 2, 64]), op=ALU.mult),
            reads=['qrot', 'rqk'], writes=['kdbf'])
        if kvout:
            P.add('dve', lambda e: e.tensor_tensor(out=T.kvout[:Pn, 0:256].rearrange("p (h d) -> p h d", d=64),
                                                   in0=r3[:, 16:20, :],
                                                   in1=T.rqk[:Pn, 16:20].unsqueeze(2).to_broadcast([Pn, 4, 64]),
                                                   op=ALU.mult), reads=['qrot', 'rqk'], writes=['kvout.k'])

    def qkv_b(Pn, h0, qcol, kname, kcol):
        if h0 == 0:
            bk, bkn, _ = bank()
            pb = psb(bk)

            def trq(e, pb=pb):
                qf = T.qbf[:Pn, :, :].rearrange("p h d -> p (h d)")
                for c in range(8):
                    ins = e.transpose(out=pb[:, c * 128:c * 128 + Pn], in_=qf[:, c * 128:(c + 1) * 128],
                                      identity=T.ident[:Pn, :Pn])
                return ins
            P.add('pe', trq, reads=['qbf', 'ident'], writes=[bkn])
            P.add('act', lambda e, pb=pb: e.activation(out=T.qT[:, :, qcol:qcol + Pn],
                                                       in_=pb.rearrange("p (c t) -> p c t", c=8)[:, :, 0:Pn], func=AF.Copy),
                  reads=[bkn], writes=['qT'])
        if kname is not None:
            bk, bkn, _ = bank()
            pb = psb(bk)

            def trk(e, pb=pb):
                kf = T.kdbf[:Pn, :, :, :].rearrange("p h t d -> p (h t d)")
                for hk in range(4):
                    ins = e.transpose(out=pb[:, hk * 128:hk * 128 + Pn], in_=kf[:, hk * 128:(hk + 1) * 128],
                                      identity=T.ident[:Pn, :Pn])
                return ins
            P.add('pe', trk, reads=['kdbf', 'ident'], writes=[bkn])
            P.add('dve', lambda e, pb=pb: e.tensor_copy(out=T.kT[:, :, kcol:kcol + Pn],
                                                        in_=pb[:, 0:512].rearrange("p (c t) -> p c t", c=4)[:, :, 0:Pn]),
                  reads=[bkn], writes=[kname])

    def normalize_heads(bO, bOn, hk, Pn):
        ov = bO[:Pn, 0:260].rearrange("p (g d) -> p g d", g=4)
        P.add('dve', lambda e: e.tensor_tensor(out=T.den[:Pn, :], in0=ov[:, :, 64], in1=T.esink[:Pn, 4 * hk:4 * hk + 4],
                                               op=ALU.add), reads=[bOn, 'esink'], writes=['den'])
        P.add('dve', lambda e: e.reciprocal(out=T.rden[:Pn, :], in_=T.den[:Pn, :]), reads=['den'], writes=['rden'])
        P.add('dve', lambda e: e.tensor_tensor(out=T.ao[:Pn, 4 * hk:4 * hk + 4, :], in0=ov[:, :, 0:64],
                                               in1=T.rden[:Pn, :].unsqueeze(2).to_broadcast([Pn, 4, 64]), op=ALU.mult),
              reads=[bOn, 'rden'], writes=['ao'])

    def ao_transpose(Pn, col, dst=None, dnames=None):
        if dst is None:
            dst, dnames = T.aoT, cn('sT')
        bk, bkn, _ = bank()
        pb = psb(bk)

        def tr(e):
            af = T.ao[:Pn, :, :].rearrange("p h d -> p (h d)")
            for c in range(8):
                ins = e.transpose(out=pb[:, c * 128:c * 128 + Pn], in_=af[:, c * 128:(c + 1) * 128],
                                  identity=T.ident[:Pn, :Pn])
            return ins
        P.add('pe', tr, reads=['ao', 'ident'], writes=[bkn])
        P.add('act', lambda e: e.activation(out=dst[:, :, col:col + Pn],
                                            in_=pb.rearrange("p (c t) -> p c t", c=8)[:, :, 0:Pn], func=AF.Copy),
              reads=[bkn], writes=dnames)

    ast = {}

    def attn_sc(s, hk):
        qc = s * 128
        bp, bpn, _ = bank()
        bo, bon, _ = bank()

        def sc(e):
            banks2 = ((bp, s * 128), (bo, 128 + s * 128))
            for par in (0, 1):
                if par == 1:
                    e.matmul(bp[:, 128:129], lhsT=T.ident[:, :], rhs=T.ident[:, 0:1], start=True, stop=True)
                for bk, kcol in banks2:
                    v = bk[:].rearrange("p (g q) -> p g q", g=4)
                    for g in (par, par + 2):
                        ins = e.matmul(v[:, g, :], lhsT=T.kT[64 * par:64 * par + 64, hk, kcol:kcol + 128],
                                       rhs=T.qT[64 * par:64 * par + 64, 2 * hk + g // 2, qc:qc + 128],
                                       start=True, stop=True)
            return ins
        P.add('pe', sc, reads=['kT.%d' % s, 'kT.%d' % (s + 1), 'qT', 'ident'], writes=[bpn, bon])
        i = rot('pt')
        pt = T.PT[i]
        ptn = 'PT%d' % i
        P.add('act', lambda e: e.activation(out=pt[:, 0, :], in_=bp[:], func=AF.Exp), reads=[bpn], writes=[ptn])
        P.add('act', lambda e: e.activation(out=pt[:, 1, :], in_=bo[:], func=AF.Exp), reads=[bon], writes=[ptn])
        pv4 = pt[:].rearrange("p w (g q) -> p w g q", g=4)
        P.add('dve', lambda e: e.tensor_tensor(out=pv4, in0=pv4, in1=T.mask[:].unsqueeze(2).to_broadcast([128, 2, 4, 128]),
                                               op=ALU.mult), reads=[ptn, 'mask'], writes=[ptn])
        ast[(s, hk)] = (pt, ptn)

    def attn_pv(s, hk):
        pt, ptn = ast.pop((s, hk))
        bO, bOn, _ = bank()

        def pvm(e):
            for g in range(4):
                e.matmul(bO[:, g * 65:(g + 1) * 65], lhsT=pt[:, 0, g * 128:(g + 1) * 128], rhs=T.Vaug[:, s, hk, :],
                         start=True, stop=False)
                ins = e.matmul(bO[:, g * 65:(g + 1) * 65], lhsT=pt[:, 1, g * 128:(g + 1) * 128],
                               rhs=T.Vaug[:, s + 1, hk, :], start=False, stop=True)
            return ins
        P.add('pe', pvm, reads=[ptn, 'V.%d' % s, 'V.%d' % (s + 1)], writes=[bOn])
        normalize_heads(bO, bOn, hk, 128)

    def attn_fin(s):
        ao_transpose(128, s * 128, T.aoTp, ['aoTp'])

    def attn_block(s):
        attn_sc(s, 0)
        for hk in range(4):
            if hk + 1 < 4:
                attn_sc(s, hk + 1)
            attn_pv(s, hk)
        attn_fin(s)

    def glu_stage(N, ucol, mode, last):
        for gi in range(4):
            glu_group(gi, N, ucol, mode, last)

    def glu_group(gi, N, ucol, mode, last):
        if True:
            slot = W.next(g_glu(gi))
            wa = T.W[slot][:, 0:2048].rearrange("p (a b) -> p a b", a=8)
            wb = T.W[slot][:, 2048:4096].rearrange("p (a b) -> p a b", a=8)
            for jj in range(2):
                j = 2 * gi + jj
                calls = [(N, ucol, mode)]
                if mode == 'main0':
                    calls = [(32, 96, 'halo'), (N, ucol, 'main')]
                for (n_, uc, md) in calls:
                    b1, b1n, _ = bank()
                    b2, b2n, _ = bank()

                    def mm(e, b1=b1, b2=b2, n_=n_, uc=uc, jj=jj, wa=wa, wb=wb):
                        for kc in range(8):
                            e.matmul(b1[:, 0:n_], lhsT=wa[:, kc, jj * 128:(jj + 1) * 128], rhs=T.uT[:, kc, uc:uc + n_],
                                     start=(kc == 0), stop=(kc == 7))
                        for kc in range(8):
                            ins = e.matmul(b2[:, 0:n_], lhsT=wb[:, kc, jj * 128:(jj + 1) * 128],
                                           rhs=T.uT[:, kc, uc:uc + n_], start=(kc == 0), stop=(kc == 7))
                        return ins
                    P.add('pe', mm, reads=cn('uT') + wnames(slot), writes=[b1n, b2n])
                    i = rot('sg')
                    sg = T.sg[i]
                    sgn = 'sg%d' % i
                    P.add('act', lambda e, sg=sg, b2=b2, n_=n_, j=j: e.activation(out=sg[:, 0:n_], in_=b2[:, 0:n_],
                                                                                 func=AF.Sigmoid, bias=T.prm[:, j, 4:5]),
                          reads=[b2n, 'prm'], writes=[sgn])
                    if md == 'halo':
                        P.add('dve', lambda e, sg=sg, b1=b1, j=j: e.scalar_tensor_tensor(
                            out=T.aT[:, j, 0:30], in0=b1[:, 2:32], scalar=T.prm[:, j, 3:4], in1=sg[:, 2:32],
                            op0=ALU.add, op1=ALU.mult), reads=[b1n, sgn, 'prm'], writes=['aT.h'])
                        P.add('dve', lambda e, j=j: e.tensor_scalar(out=T.aT[:, j, 0:30], in0=T.aT[:, j, 0:30],
                                                                    scalar1=T.flag[:, 0:1], scalar2=None, op0=ALU.mult),
                              reads=['aT.h', 'flag'], writes=['aT.h'])
                    elif md == 'sample':
                        P.add('dve', lambda e, sg=sg, b1=b1, j=j: e.scalar_tensor_tensor(
                            out=T.a32[:, j, 0:16], in0=b1[:, 0:16], scalar=T.prm[:, j, 3:4], in1=sg[:, 0:16],
                            op0=ALU.add, op1=ALU.mult), reads=[b1n, sgn, 'prm'], writes=['a32'])
                        P.add('dve', lambda e, j=j: e.tensor_copy(out=T.histT[:, j, :, 30], in_=T.a32[:, j, 0:16]),
                              reads=['a32'], writes=['qT'])
                    else:
                        P.add('dve', lambda e, sg=sg, b1=b1, j=j, n_=n_: e.scalar_tensor_tensor(
                            out=T.aT[:, j, 30:30 + n_], in0=b1[:, 0:n_], scalar=T.prm[:, j, 3:4], in1=sg[:, 0:n_],
                            op0=ALU.add, op1=ALU.mult), reads=[b1n, sgn, 'prm'], writes=['aT.m%d' % j])
                        if last:
                            P.add('dve', lambda e, sg=sg, b1=b1, j=j, n_=n_: e.scalar_tensor_tensor(
                                out=T.a32[:, j, 0:32], in0=b1[:, n_ - 32:n_], scalar=T.prm[:, j, 3:4], in1=sg[:, n_ - 32:n_],
                                op0=ALU.add, op1=ALU.mult), reads=[b1n, sgn, 'prm'], writes=['a32'])

            W.release(slot)

    cst = {}

    def conv_ln_stage(N, sample):
        conv_begin()
        for j in range(8):
            conv_chunk(j, N, sample)
        conv_finish(N)

    def conv_begin():
        bsum, bsumn, isum = bank()
        held.add(isum)
        bsq, bsqn, isq = bank()
        held.add(isq)
        cst['b'] = (bsum, bsumn, isum, bsq, bsqn, isq)

    def conv_chunk(j, N, sample):
        if not sample:
            diag_build(j)
        conv_mm(j, N, sample)

    def diag_build(j):
        i = rot('dg')
        dg = T.diag[i]
        dgn = 'diag%d' % i
        cst['dg%d' % j] = (dg, dgn)
        P.add('dve', lambda e, dg=dg, j=j: e.tensor_tensor(
            out=dg[:], in0=T.ident[:].unsqueeze(1).to_broadcast([128, 31, 128]),
            in1=T.wdw[:, j, :].unsqueeze(2).to_broadcast([128, 31, 128]), op=ALU.mult),
            reads=['ident', 'wdw'], writes=[dgn])

    def conv_mm(j, N, sample):
        bsum, bsumn, isum, bsq, bsqn, isq = cst['b']
        if sample:
            i = rot('tl')
            tl = T.tl[i]
            tln = 'tl%d' % i
            t3 = tl[:, 0:N * 31].rearrange("p (b k) -> p b k", k=31)
            P.add('dve', lambda e: e.tensor_tensor(out=t3, in0=T.histT[:, j, :, :],
                                                   in1=T.wdw[:, j, :].unsqueeze(1).to_broadcast([128, N, 31]), op=ALU.mult),
                  reads=['qT', 'wdw'], writes=[tln])
            bc = T.ycs
            bcn = 'ycs'
            P.add('dve', lambda e: e.tensor_reduce(out=T.ycs[:, 0:N], in_=t3, axis=AX.X, op=ALU.add), reads=[tln],
                  writes=['ycs'])
        else:
            dg, dgn = cst['dg%d' % j]
            bc, bcn, _ = bank()

            def cm(e, dg=dg, bc=bc, j=j):
                for k in range(31):
                    ins = e.matmul(bc[:, 0:N], lhsT=dg[:, k, :], rhs=T.aT[:, j, k:k + N], start=(k == 0), stop=(k == 30))
                return ins
            P.add('pe', cm, reads=[dgn, 'aT.h', 'aT.m%d' % j], writes=[bcn])
        if True:
            P.add('act', lambda e, bc=bc, j=j: e.activation(out=T.ybf[:, j, 0:N], in_=bc[:, 0:N], func=AF.Identity,
                                                            bias=T.prm[:, j, 5:6]), reads=[bcn, 'prm'], writes=['ybf.%d' % j])
            i2 = rot('y2')
            y2 = T.y2[i2]
            y2n = 'y2%d' % i2
            P.add('act', lambda e, bc=bc, j=j, y2=y2: e.activation(out=y2[:, 0:N], in_=bc[:, 0:N], func=AF.Square,
                                                                   bias=T.prm[:, j, 5:6]), reads=[bcn, 'prm'], writes=[y2n])

            def st(e, j=j, y2=y2):
                e.matmul(bsum[:, 0:N], lhsT=T.ones[:], rhs=T.ybf[:, j, 0:N], start=(j == 0), stop=(j == 7))
                return e.matmul(bsq[:, 0:N], lhsT=T.ones[:], rhs=y2[:, 0:N], start=(j == 0), stop=(j == 7))
            P.add('pe', st, reads=['ones', 'ybf.%d' % j, y2n], writes=[bsumn, bsqn])

    def conv_finish(N):
        ln_head(N)
        for j in range(8):
            ln_chunk(j, N)

    def ln_head(N):
        bsum, bsumn, isum, bsq, bsqn, isq = cst['b']
        P.add('act', lambda e: e.activation(out=T.mean[:, 0:N], in_=bsum[:, 0:N], func=AF.Copy, scale=1.0 / 1024),
              reads=[bsumn], writes=['mean'])
        P.add('dve', lambda e: e.tensor_tensor(out=T.m2[:, 0:N], in0=T.mean[:, 0:N], in1=T.mean[:, 0:N], op=ALU.mult),
              reads=['mean'], writes=['m2'])
        P.add('dve', lambda e: e.scalar_tensor_tensor(out=T.m2[:, 0:N], in0=bsq[:, 0:N], scalar=1.0 / 1024, in1=T.m2[:, 0:N],
                                                      op0=ALU.mult, op1=ALU.subtract), reads=[bsqn, 'm2'], writes=['m2'])
        P.add('dve', lambda e: e.tensor_scalar(out=T.m2[:, 0:N], in0=T.m2[:, 0:N], scalar1=EPS, scalar2=None, op0=ALU.add),
              reads=['m2'], writes=['m2'])
        P.add('act', lambda e: e.activation(out=T.rstdL[:, 0:N], in_=T.m2[:, 0:N], func=AF.Ln), reads=['m2'], writes=['rstdL'])
        P.add('act', lambda e: e.activation(out=T.rstdL[:, 0:N], in_=T.rstdL[:, 0:N], func=AF.Exp, scale=-0.5),
              reads=['rstdL'], writes=['rstdL'])
        held.discard(isum)
        held.discard(isq)

    def ln_chunk(j, N):
        i = rot('tl')
        tl = T.tl[i]
        tln = 'tl%d' % i
        P.add('dve', lambda e: e.tensor_tensor(out=tl[:, 0:N], in0=T.ybf[:, j, 0:N], in1=T.mean[:, 0:N], op=ALU.subtract),
              reads=['ybf.%d' % j, 'mean'], writes=[tln])
        P.add('dve', lambda e: e.tensor_tensor(out=tl[:, 0:N], in0=tl[:, 0:N], in1=T.rstdL[:, 0:N], op=ALU.mult),
              reads=[tln, 'rstdL'], writes=[tln])
        P.add('dve', lambda e: e.tensor_scalar(out=tl[:, 0:N], in0=tl[:, 0:N], scalar1=T.prm[:, j, 6:7],
                                               scalar2=T.prm[:, j, 7:8], op0=ALU.mult, op1=ALU.add),
              reads=[tln, 'prm'], writes=[tln])
        P.add('act', lambda e: e.activation(out=T.m2[:, 0:N], in_=tl[:, 0:N], func=AF.Sigmoid), reads=[tln], writes=['m2'])
        P.add('dve', lambda e: e.tensor_tensor(out=T.sT[:, j, 0:N], in0=tl[:, 0:N], in1=T.m2[:, 0:N], op=ALU.mult),
              reads=[tln, 'm2'], writes=['sT.%d' % j])

    def mix_stage(N, ucol, which, src=None, srcn=None, pre=None):
        wsrc = D.w_o_attn if which == 'a' else D.w_conv_out
        gcol = 3584 if which == 'a' else 4608
        if src is None:
            src = T.aoT if which == 'a' else T.sT
            srcn = cn('sT')
        for half in range(2):
            s1 = W.next(g_cols(wsrc, half * 512))
            s2 = W.next(g_cols(D.w_in, gcol + half * 512))
            w1, w2 = wslot(s1), wslot(s2)
            for jj in range(4):
                j = half * 4 + jj
                if pre is not None:
                    pre(j)
                b1, b1n, _ = bank()
                b2, b2n, _ = bank()

                def mm(e, b1=b1, b2=b2, jj=jj, w1=w1, w2=w2):
                    for kc in range(8):
                        e.matmul(b1[:, 0:N], lhsT=w1[:, kc, jj * 128:(jj + 1) * 128], rhs=src[:, kc, 0:N], start=(kc == 0),
                                 stop=(kc == 7))
                    for kc in range(8):
                        ins = e.matmul(b2[:, 0:N], lhsT=w2[:, kc, jj * 128:(jj + 1) * 128], rhs=T.uT[:, kc, ucol:ucol + N],
                                       start=(kc == 0), stop=(kc == 7))
                    return ins
                P.add('pe', mm, reads=srcn + cn('uT') + wnames(s1) + wnames(s2), writes=[b1n, b2n])
                i = rot('sg')
                sg = T.sg[i]
                sgn = 'sg%d' % i
                P.add('act', lambda e, sg=sg, b2=b2: e.activation(out=sg[:, 0:N], in_=b2[:, 0:N], func=AF.Sigmoid),
                      reads=[b2n], writes=[sgn])
                if which == 'a':
                    P.add('dve', lambda e, sg=sg, b1=b1, j=j: e.tensor_tensor(out=T.mixT[:, j, 0:N], in0=b1[:, 0:N],
                                                                              in1=sg[:, 0:N], op=ALU.mult),
                          reads=[b1n, sgn], writes=['mixT.%d' % j])
                else:
                    P.add('dve', lambda e, sg=sg, b1=b1, j=j: e.scalar_tensor_tensor(
                        out=sg[:, 0:N], in0=b1[:, 0:N], scalar=T.prm[:, j, 8:9], in1=sg[:, 0:N], op0=ALU.add, op1=ALU.mult),
                        reads=[b1n, sgn, 'prm'], writes=[sgn])
                    P.add('dve', lambda e, sg=sg, j=j: e.tensor_tensor(out=T.mixT[:, j, 0:N], in0=sg[:, 0:N],
                                                                       in1=T.mixT[:, j, 0:N], op=ALU.add),
                          reads=[sgn, 'mixT.%d' % j], writes=['mixT.%d' % j])
            W.release(s1)
            W.release(s2)

    def wout_stage(Pn, nsub):
        for nh in range(2):
            slot = W.next(g_cols(D.w_out, nh * 512))
            w = wslot(slot)
            for s in range(nsub):
                bk, bkn, _ = bank()

                def mm(e, bk=bk, s=s, w=w):
                    for kc in range(8):
                        ins = e.matmul(bk[:Pn, :], lhsT=T.mixT[:, kc, s * Pn:(s + 1) * Pn], rhs=w[:, kc, :], start=(kc == 0),
                                       stop=(kc == 7))
                    return ins
                P.add('pe', mm, reads=cn('mixT') + wnames(slot), writes=[bkn])
                xv = T.xbuf[:Pn, s, nh * 512:(nh + 1) * 512]
                P.add('dve', lambda e, bk=bk, xv=xv: e.tensor_tensor(out=xv, in0=bk[:Pn, :], in1=xv, op=ALU.add),
                      reads=[bkn, 'x.%d' % s], writes=['x.%d' % s])
            W.release(slot)

    def ffn_stage(Pn, nsub, ucol):
        N = Pn * nsub
        for g in range(8):
            slot = W.next(g_cols(D.w_ff1, g * 512))
            w = wslot(slot)
            for jj in range(4):
                f = 4 * g + jj
                bk, bkn, _ = bank()

                def mm(e, bk=bk, jj=jj, w=w):
                    for kc in range(8):
                        ins = e.matmul(bk[:, 0:N], lhsT=w[:, kc, jj * 128:(jj + 1) * 128], rhs=T.uT[:, kc, ucol:ucol + N],
                                       start=(kc == 0), stop=(kc == 7))
                    return ins
                P.add('pe', mm, reads=cn('uT') + wnames(slot), writes=[bkn])
                i = rot('sg')
                rl = T.sg[i]
                rln = 'sg%d' % i
                P.add('act', lambda e, rl=rl, bk=bk: e.activation(out=rl[:, 0:N], in_=bk[:, 0:N], func=AF.Relu),
                      reads=[bkn], writes=[rln])
                P.add('dve', lambda e, rl=rl, f=f: e.tensor_tensor(out=T.hid[:, f, 0:N], in0=rl[:, 0:N], in1=rl[:, 0:N],
                                                                   op=ALU.mult), reads=[rln], writes=['hid'])
            W.release(slot)
        for nh in range(2):
            bks = [bank() for _ in range(nsub)]
            for fg in range(4):
                slot = W.next(g_ff2(fg, nh))
                w = wslot(slot)
                for s in range(nsub):
                    bk, bkn, _ = bks[s]

                    def mm(e, bk=bk, s=s, w=w, fg=fg):
                        for fc in range(8):
                            ins = e.matmul(bk[:Pn, :], lhsT=T.hid[:, fg * 8 + fc, s * Pn:(s + 1) * Pn], rhs=w[:, fc, :],
                                           start=(fg == 0 and fc == 0), stop=(fg == 3 and fc == 7))
                        return ins
                    P.add('pe', mm, reads=['hid'] + wnames(slot), writes=[bkn])
                W.release(slot)
            for s in range(nsub):
                bk, bkn, _ = bks[s]
                xv = T.xbuf[:Pn, s, nh * 512:(nh + 1) * 512]
                P.add('dve', lambda e, bk=bk, xv=xv: e.tensor_tensor(out=xv, in0=bk[:Pn, :], in1=xv, op=ALU.add),
                      reads=[bkn, 'x.%d' % s], writes=['x.%d' % s])

    def ple_stage(Pn, nsub, ucol, psrc, ydst_fn, xnext_fn):
        P.add('sp', lambda e: e.dma_start(out=T.pbuf[:Pn, 0:nsub, :], in_=psrc), writes=['tl0', 'tl1'], dma='pbuf')
        P.add('dve', lambda e: e.tensor_copy(out=T.pbf[:Pn, 0:nsub, :], in_=T.pbuf[:Pn, 0:nsub, :]), reads=['tl0', 'tl1'],
              writes=['pbf'])
        bk, bkn, _ = bank()
        pb = psb(bk)
        N = Pn * nsub

        def tr(e):
            for c in range(2):
                for s in range(nsub):
                    ins = e.transpose(out=pb[:, c * 512 + s * Pn:c * 512 + (s + 1) * Pn],
                                      in_=T.pbf[:Pn, s, c * 128:(c + 1) * 128], identity=T.ident[:Pn, :Pn])
            return ins
        P.add('pe', tr, reads=['pbf', 'ident'], writes=[bkn])
        P.add('act', lambda e: e.activation(out=T.pT[:, :, 0:N], in_=pb.rearrange("p (c t) -> p c t", c=2)[:, :, 0:N],
                                            func=AF.Copy), reads=[bkn], writes=['pT'])
        sp_ = W.next(g_ple())
        wp = wslot(sp_, 2, 1024)
        gs = [W.next(g_cols(D.w_ple_gate, nh * 512)) for nh in range(2)]
        for s in range(nsub):
            yb = T.ybuf[s % 2]
            ybn = 'diag%d' % (s % 2)
            for nh in range(2):
                slot = gs[nh]
                w = wslot(slot)
                bg, bgn, _ = bank()
                bp, bpn, _ = bank()

                def mm(e, bg=bg, bp=bp, s=s, w=w, nh=nh):
                    for kc in range(8):
                        e.matmul(bg[:Pn, :], lhsT=T.uT[:, kc, ucol + s * Pn:ucol + (s + 1) * Pn], rhs=w[:, kc, :],
                                 start=(kc == 0), stop=(kc == 7))
                    for c in range(2):
                        ins = e.matmul(bp[:Pn, :], lhsT=T.pT[:, c, s * Pn:(s + 1) * Pn], rhs=wp[:, c, nh * 512:(nh + 1) * 512],
                                       start=(c == 0), stop=(c == 1))
                    return ins
                P.add('pe', mm, reads=cn('uT') + ['pT'] + wnames(slot) + wnames(sp_), writes=[bgn, bpn])
                i = rot('sg')
                sg = T.sg[i]
                sgn = 'sg%d' % i
                P.add('act', lambda e, sg=sg, bg=bg: e.activation(out=sg[:Pn, :], in_=bg[:Pn, :], func=AF.Sigmoid),
                      reads=[bgn], writes=[sgn])
                P.add('dve', lambda e, sg=sg, bp=bp: e.tensor_tensor(out=sg[:Pn, :], in0=bp[:Pn, :], in1=sg[:Pn, :],
                                                                     op=ALU.mult), reads=[bpn, sgn], writes=[sgn])
                xv = T.xbuf[:Pn, s, nh * 512:(nh + 1) * 512]
                P.add('dve', lambda e, sg=sg, xv=xv, yb=yb, nh=nh: e.tensor_tensor(out=yb[:Pn, nh * 512:(nh + 1) * 512],
                                                                               in0=sg[:Pn, :], in1=xv, op=ALU.add),
                      reads=[sgn, 'x.%d' % s], writes=[ybn])
            P.add('sp', lambda e, s=s, yb=yb: e.dma_start(out=ydst_fn(s), in_=yb[:Pn, :]), reads=[ybn], dma='yst')
            if xnext_fn is not None:
                P.add('sp', lambda e, s=s: e.dma_start(out=T.xbuf[:, s, :], in_=xnext_fn(s)), writes=['x.%d' % s],
                      dma='x%d' % s)
                if s >= 1:
                    rms_head_sub(lambda q: T.xbuf[:, q, :], ['x.%d' % (s - 1)], 128, s - 1)
        if xnext_fn is not None:
            rms_head_sub(lambda q: T.xbuf[:, q, :], ['x.%d' % (nsub - 1)], 128, nsub - 1)
        for slot in gs:
            W.release(slot)
        W.release(sp_)

    def a32_out(nrows, dst, r0):
        for hh in range(2):
            bk, bkn, _ = bank()

            def tr(e, bk=bk, hh=hh):
                for jj in range(4):
                    ins = e.transpose(out=bk[:nrows, jj * 128:(jj + 1) * 128], in_=T.a32[:, hh * 4 + jj, 0:nrows],
                                      identity=T.identf[:, :])
                return ins
            P.add('pe', tr, reads=['a32', 'identf'], writes=[bkn])
            P.add('dve', lambda e, bk=bk, hh=hh: e.tensor_copy(out=T.qkf[:nrows, hh * 512:(hh + 1) * 512], in_=bk[:nrows, :]),
                  reads=[bkn], writes=['qkf'])
        P.add('sp', lambda e: e.dma_start(out=dst, in_=T.qkf[r0:nrows, 0:1024]), reads=['qkf'], dma='qkfo')

    for it in range(NT_RUN):
        last = (it == NT - 1)
        r0 = 128 + it * TT
        if it == 0:
            for s in range(4):
                P.add('sp', lambda e, s=s: e.dma_start(out=T.xbuf[:, s, :], in_=D.x[128 + s * 128:256 + s * 128, :]),
                      writes=['x.%d' % s], dma='x%d' % s)
        P.add('sp', lambda e, it=it: e.dma_start(out=T.cos[:, 1:5, :], in_=D.cos[:, 1 + 4 * it:5 + 4 * it, :]), writes=['cos'], dma='cos')
        P.add('sp', lambda e, it=it: e.dma_start(out=T.sin[:, 1:5, :], in_=D.sin[:, 1 + 4 * it:5 + 4 * it, :]), writes=['sin'], dma='sin')
        if it == 0:
            P.add('sp', lambda e: e.dma_start(out=T.xh[:, :], in_=D.x[0:128, :]), writes=['sg0', 'sg1'], dma='xh')
            rms_T(lambda s: T.xh[:, :], [['sg0', 'sg1']], 128, 1, 0, T.uT, cn('uT'), 0)
        rms_T(lambda s: T.xbuf[:, s, :], cn('x', 4), 128, 4, 0, T.uT, cn('uT'), 128, heads_done=(it > 0))
        if STOP_STAGE == 1:
            return
        sq0 = W.next(g_cols(D.w_in, 0))
        sq1 = W.next(g_cols(D.w_in, 512))
        skv = W.next(g_cols(D.w_in, 1024))
        slots = (sq0, sq1, skv)
        if it == 0:
            qkv_sub(128, 0, 0, slots, 16, None, 'kT.0', 0, 0, False)
            P.add('dve', lambda e: e.tensor_scalar(out=T.Vaug[:, 0, :, :], in0=T.Vaug[:, 0, :, :], scalar1=T.flag[:, 0:1],
                                                   scalar2=None, op0=ALU.mult), reads=['V.0', 'flag'], writes=['V.0'])
        if STOP_STAGE == 11:
            return
        gmode = 'main0' if it == 0 else 'main'
        conv_begin()
        for s in range(4):
            A = s - 1
            diag_build(2 * s)
            diag_build(2 * s + 1)
            glu_group(s, TT, 128, gmode, last)
            if A >= 0:
                attn_sc(A, 0)
            qkv_sub(128, 128 + s * 128, 1 + s, slots, 0, s * 128, 'kT.%d' % (s + 1), 128 + s * 128, s + 1,
                    last and s == 3, part='a')
            if s == 3:
                for sl_ in slots:
                    W.release(sl_)
            if A >= 0:
                attn_pv(A, 0)
                attn_sc(A, 1)
            conv_mm(2 * s, TT, False)
            if A >= 0:
                attn_pv(A, 1)
                attn_sc(A, 2)
            conv_mm(2 * s + 1, TT, False)
            if A >= 0:
                attn_pv(A, 2)
                attn_sc(A, 3)
            qkv_sub(128, 128 + s * 128, 1 + s, slots, 0, s * 128, 'kT.%d' % (s + 1), 128 + s * 128, s + 1,
                    last and s == 3, part='b')
            if A >= 0:
                attn_pv(A, 3)
                attn_fin(A)
        attn_block(3)
        if last:
            P.add('sp', lambda e: e.dma_start(out=D.kp, in_=T.kvout[:, 0:256]), reads=['kvout.k'], dma='kvo')
            P.add('sp', lambda e: e.dma_start(out=D.vp, in_=T.kvout[:, 256:512]), reads=['kvout.v'], dma='kvo')
        ln_head(TT)
        mix_stage(TT, 128, 'a', src=T.aoTp, srcn=['aoTp'], pre=lambda j: ln_chunk(j, TT))
        if last:
            a32_out(32, D.cp, 2)
        mix_stage(TT, 128, 'c')
        if STOP_STAGE == 6:
            return
        wout_stage(128, 4)
        if STOP_STAGE == 7:
            return
        if not last:
            P.add('dve', lambda e: e.tensor_copy(out=T.kT[:, :, 0:128], in_=T.kT[:, :, 512:640]), reads=['kT.4'], writes=['kT.0'])
            P.add('dve', lambda e: e.tensor_copy(out=T.Vaug[:, 0, :, :], in_=T.Vaug[:, 4, :, :]), reads=['V.4'], writes=['V.0'])
            P.add('dve', lambda e: e.tensor_copy(out=T.aT[:, :, 0:30], in_=T.aT[:, :, 512:542]), reads=['aT.m%d' % j for j in range(8)], writes=['aT.h'])
        rms_T(lambda s: T.xbuf[:, s, :], cn('x', 4), 128, 4, 1, T.uT, cn('uT'), 128)
        ffn_stage(128, 4, 128)
        if STOP_STAGE == 8:
            return
        rms_T(lambda s: T.xbuf[:, s, :], cn('x', 4), 128, 4, 2, T.uT, cn('uT'), 128)
        p0 = it * TT
        r1 = 128 + (it + 1) * TT
        ple_stage(128, 4, 128, D.p[p0:p0 + TT, :].rearrange("(s p) n -> p s n", p=128),
                  lambda s, p0=p0: D.y[p0 + s * 128:p0 + (s + 1) * 128, :],
                  (lambda s, r1=r1: D.x[r1 + s * 128:r1 + (s + 1) * 128, :]) if it + 1 < NT_RUN else None)

    if not RUN_SAMPLE:
        return
    NS = 16
    P.add('sp', lambda e: e.dma_start(out=T.xbuf[:NS, 0, :], in_=D.xs), writes=['x.0'], dma='x0')
    P.add('sp', lambda e: e.dma_start(out=T.cos[:, 0, :], in_=D.cos[:, 33, :]), writes=['cos'], dma='cos')
    P.add('sp', lambda e: e.dma_start(out=T.sin[:, 0, :], in_=D.sin[:, 33, :]), writes=['sin'], dma='sin')
    rms_T(lambda s: T.xbuf[:NS, 0, :], ['x.0'], NS, 1, 0, T.uT, cn('uT'), 128)
    sq0 = W.next(g_cols(D.w_in, 0))
    sq1 = W.next(g_cols(D.w_in, 512))
    skv = W.next(g_cols(D.w_in, 1024))
    qkv_sub(NS, 128, 0, (sq0, sq1, skv), 0, 0, None, 0, 0, True)
    for sl_ in (sq0, sq1, skv):
        W.release(sl_)
    P.add('sp', lambda e: e.dma_start(out=D.ks[:, 127, :], in_=T.kvout[:NS, 0:256]), reads=['kvout.k'], writes=['ks.B'], dma='ksB')
    P.add('sp', lambda e: e.dma_start(out=D.vs[:, 127, :], in_=T.kvout[:NS, 256:512]), reads=['kvout.v'], writes=['vs.B'], dma='vsB')
    ksv = D.ks.rearrange("b j (h d) -> j b h d", h=4)
    P.add('pool', lambda e: e.dma_start(out=T.Kds[:, :, :, :], in_=ksv), reads=['ks.A', 'ks.B'],
          writes=['aT.m%d' % j for j in range(8)] + ['aT.h'], dma='kds')
    P.add('pool', lambda e: e.memset(T.Vs[:, :, :, 64:65], 1.0), writes=['Vs'])
    vsv = D.vs.rearrange("b j (h d) -> j b h d", h=4)
    for hk in range(4):
        P.add('pool', lambda e, hk=hk: e.dma_start(out=T.Vs[:, :, hk, 0:64], in_=vsv[:, :, hk, :]),
              reads=['vs.A', 'vs.B'], writes=['Vs'], dma='vsr')
    P.add('pool', lambda e: e.memset(T.Pexp[:], 0.0), writes=cn('ybf'))
    bS, bSn, iS = bank()
    held.add(iS)
    bS4 = bS[:, 0:256].rearrange("p (b h g) -> p b h g", b=16, h=4)
    for b in range(NS):
        bk, bkn, _ = bank()
        pb = psb(bk)

        P.add('dve', lambda e, b=b: e.tensor_copy(out=T.kdbf[:, :, :, :],
                                                  in_=T.Kds[:, b, :, :].unsqueeze(2).to_broadcast([128, 4, 2, 64])),
              reads=['aT.m%d' % j for j in range(8)] + ['aT.h'], writes=['kdbf'])

        def trk(e, b=b, pb=pb):
            kf = T.kdbf[:, :, :, :].rearrange("p h t d -> p (h t d)")
            for hk in range(4):
                ins = e.transpose(out=pb[:, hk * 128:(hk + 1) * 128], in_=kf[:, hk * 128:(hk + 1) * 128],
                                  identity=T.ident[:, :])
            return ins
        P.add('pe', trk, reads=['kdbf', 'ident'], writes=[bkn])
        i = rot('ev')
        kt = T.KTs[i]
        ktn = 'pT'
        P.add('dve', lambda e, kt=kt, pb=pb: e.tensor_copy(out=kt[:].rearrange("p h k -> p (h k)"), in_=pb[:, 0:512]),
              reads=[bkn], writes=[ktn])

        def sc(e, b=b, kt=kt):
            for par in (0, 1):
                if par == 1:
                    e.matmul(bS4[:, b, 0, 1:2], lhsT=T.ident[:, :], rhs=T.ident[:, 0:1], start=True, stop=True)
                for hk in range(4):
                    ins = e.matmul(bS4[:, b, hk, par::2], lhsT=kt[64 * par:64 * par + 64, hk, :],
                                   rhs=T.qT[64 * par:64 * par + 64, 2 * hk:2 * hk + 2, b], start=True, stop=True)
            return ins
        P.add('pe', sc, reads=[ktn, 'qT', 'ident'], writes=[bSn])
    P.add('act', lambda e: e.activation(out=T.Pexp[:].rearrange("p h b c -> p h (b c)")[:, :, ::17],
                                        in_=bS[:, 0:256].rearrange("p (b h) -> p h b", b=16), func=AF.Exp),
          reads=[bSn], writes=cn('ybf'))
    held.discard(iS)
    for hk in range(4):
        bO, bOn, _ = bank()

        def pvm(e, hk=hk, bO=bO):
            for g in range(4):
                for b in range(NS):
                    ins = e.matmul(bO[:NS, g * 65:(g + 1) * 65], lhsT=T.Pexp[:, 4 * hk + g, b, :], rhs=T.Vs[:, b, hk, :],
                                   start=(b == 0), stop=(b == NS - 1))
            return ins
        P.add('pe', pvm, reads=cn('ybf') + ['Vs'], writes=[bOn])
        normalize_heads(bO, bOn, hk, NS)
    ao_transpose(NS, 0)
    mix_stage(NS, 128, 'a')
    stv = D.st.rearrange("(i bb) k c -> i (bb k) c", i=4)
    for i4 in range(4):
        P.add('sp', lambda e, i4=i4: e.dma_start(out=T.xh[:120, :], in_=stv[i4]), writes=['sg0', 'sg1'], dma='xh')
        P.add('dve', lambda e, i4=i4: e.tensor_copy(out=T.xn[:120, i4, :], in_=T.xh[:120, :]), reads=['sg0', 'sg1'],
              writes=xnn(i4))
    for i4 in range(4):
        bk, bkn, _ = bank()
        pb = psb(bk)

        def trs(e, i4=i4, pb=pb):
            for c in range(8):
                ins = e.transpose(out=pb[:, c * 120:(c + 1) * 120], in_=T.xn[:120, i4, c * 128:(c + 1) * 128],
                                  identity=T.ident[:120, :120])
            return ins
        P.add('pe', trs, reads=cn('xn', 4) + ['ident'], writes=[bkn])
        P.add('dve', lambda e, i4=i4, pb=pb: e.tensor_copy(
            out=T.histT[:, :, 4 * i4:4 * i4 + 4, 0:30], in_=pb[:, 0:960].rearrange("p (c b k) -> p c b k", c=8, b=4)),
            reads=[bkn], writes=['qT'])
    glu_stage(NS, 128, 'sample', False)
    conv_ln_stage(NS, True)
    a32_out(NS, D.cs[:, 29, :], 0)
    mix_stage(NS, 128, 'c')
    wout_stage(NS, 1)
    rms_T(lambda s: T.xbuf[:NS, 0, :], ['x.0'], NS, 1, 1, T.uT, cn('uT'), 128)
    ffn_stage(NS, 1, 128)
    rms_T(lambda s: T.xbuf[:NS, 0, :], ['x.0'], NS, 1, 2, T.uT, cn('uT'), 128)
    ple_stage(NS, 1, 128, D.psamp.rearrange("(s p) n -> p s n", s=1), lambda s: D.ys, None)


def build_program():
    nc = bass.Bass("TRN2", target_bir_lowering=False)
    D = TT_()

    def din(name, shape):
        setattr(D, name, nc.dram_tensor(name, shape, F32, kind="ExternalInput").ap())

    def dout(name, shape):
        setattr(D, name, nc.dram_tensor(name, shape, F32, kind="ExternalOutput").ap())
    din("x", [128 + NT * TT, 1024]); din("p", [NT * TT, 256]); din("flag", [128, 1])
    din("cos", [128, 34, 32]); din("sin", [128, 34, 32])
    din("xs", [16, 1024]); din("psamp", [16, 256]); din("ck", [16, 128, 256]); din("cv", [16, 128, 256])
    din("st", [16, 30, 1024])
    for n, sh in [("ln1", [1024]), ("w_in", [1024, 5632]), ("b_glu", [2048]), ("q_norm", [64]), ("k_norm", [64]),
                  ("sinks", [16]), ("w_o_attn", [1024, 1024]), ("conv_dw", [31, 1024]), ("conv_dw_b", [1024]),
                  ("conv_ln_g", [1024]), ("conv_ln_b", [1024]), ("w_conv_out", [1024, 1024]), ("b_conv_out", [1024]),
                  ("w_out", [1024, 1024]), ("ln2", [1024]), ("w_ff1", [1024, 4096]), ("w_ff2", [4096, 1024]),
                  ("ln_ple", [1024]), ("w_ple_gate", [1024, 1024]), ("w_ple", [256, 1024])]:
        din(n, sh)
    dout("y", [NT * TT, 1024]); dout("ys", [16, 1024]); dout("kp", [128, 256]); dout("vp", [128, 256])
    dout("cp", [30, 1024]); dout("ks", [16, 128, 256]); dout("vs", [16, 128, 256]); dout("cs", [16, 30, 1024])

    rec = WRec()
    T0 = TT_()

    class _Any:
        def __getattr__(self, k):
            return _Any()

        def __getitem__(self, k):
            return _Any()

        def __call__(self, *a, **k):
            return _Any()
    for nme in ['ps', 'W', 'sg', 'PT', 'diag', 'y2', 'tl', 'rl', 'KTs']:
        setattr(T0, nme, [_Any() for _ in range(8)])

    class _T0(TT_):
        def __getattr__(self, k):
            return _Any()
    T0d = _T0()
    T0d.ps = T0.ps; T0d.W = T0.W; T0d.sg = T0.sg; T0d.PT = T0.PT; T0d.diag = T0.diag; T0d.y2 = T0.y2
    T0d.tl = T0.tl; T0d.rl = T0.rl; T0d.KTs = T0.KTs
    emit_all(DryProg(), rec, T0d, D)

    with ExitStack() as es:
        T = TT_()

        def sb(name, shape, dt):
            t = es.enter_context(nc.sbuf_tensor("sb_" + name, shape, dt))
            setattr(T, name, t)
            return t
        sb("identf", [128, 128], F32); sb("ident", [128, 128], BF16); sb("ones", [128, 128], BF16)
        sb("neghalf", [128, 32], F32); sb("maskf", [128, 2, 128], F32); sb("mask", [128, 2, 128], BF16)
        sb("cos", [128, 5, 32], F32); sb("sin", [128, 5, 32], F32); sb("flag", [128, 1], F32)
        sb("gq", [128, 64], F32); sb("gk", [128, 64], F32); sb("esink", [128, 16], F32)
        sb("prm", [128, 8, 40], F32); sb("wdw", [128, 8, 31], BF16)
        sb("gfull", [128, 20, 64], F32)
        sb("xbuf", [128, 4, 1024], F32)
        sb("ss", [128, 4], F32); sb("ms", [128, 4], F32); sb("rstd", [128, 4], F32)
        sb("uT", [128, 8, 640], BF16)
        sb("ssqk", [128, 20], F32); sb("msqk", [128, 20], F32); sb("rqk", [128, 20], F32)
        sb("qbf", [128, 16, 64], BF16); sb("kdbf", [128, 4, 2, 64], BF16); sb("kvout", [128, 512], F32)
        sb("kT", [128, 4, 640], BF16); sb("Vaug", [128, 5, 4, 65], BF16)
        T.PT = [sb("PT%d" % i, [128, 2, 512], BF16) for i in range(2)]
        sb("den", [128, 4], F32); sb("rden", [128, 4], F32); sb("ao", [128, 16, 64], BF16)
        sb("aT", [128, 8, 542], BF16); sb("a32", [128, 8, 32], F32)
        sg2 = sb("sg2", [128, 2, 512], F32)
        T.sg = [sg2[:, 0, :], sg2[:, 1, :]]
        T.xh = sg2[:].rearrange("p a b -> p (a b)")
        T.diag = [sb("diag%d" % i, [128, 31, 128], BF16) for i in range(2)]
        sb("ybf", [128, 8, 512], BF16)
        T.xn = T.ybf[:].rearrange("p j t -> p (j t)").rearrange("p (s n) -> p s n", s=4)
        y22 = sb("y22", [128, 2, 512], BF16)
        T.y2 = [y22[:, 0, :], y22[:, 1, :]]
        T.junk = y22[:].rearrange("p a b -> p (a b)")
        sb("ycs", [128, 16], F32)
        sb("mean", [128, 512], F32); sb("m2", [128, 512], F32); sb("rstdL", [128, 512], F32)
        tl2 = sb("tl2", [128, 2, 512], F32)
        T.tl = [tl2[:, 0, :], tl2[:, 1, :]]
        T.prows = tl2[:].rearrange("p a b -> p (a b)")
        T.pbuf = tl2[:].rearrange("p a b -> p (a b)").rearrange("p (s n) -> p s n", s=4)
        sb("sT", [128, 8, 512], BF16); sb("mixT", [128, 8, 512], BF16)
        T.aoT = T.sT
        sb("hid", [128, 32, 512], BF16)
        sb("pbf", [128, 4, 256], BF16); sb("pT", [128, 2, 512], BF16)
        T.KTs = [T.pT[:, i, :].rearrange("p (h k) -> p h k", h=4) for i in range(2)]
        T.W = [sb("W%d" % i, [128, 4096], BF16) for i in range(NSLOT)]
        T.ps = [es.enter_context(nc.psum_tensor("psum%d" % i, [128, 512], F32)) for i in range(8)]
        hflat = T.hid[:].rearrange("p f t -> p (f t)")
        T.qT = hflat[:, 0:4096].rearrange("p (c t) -> p c t", c=8)
        T.qkf = hflat[:, 4096:6656].bitcast(F32)
        T.tmpA = hflat[:, 6656:7936].bitcast(F32).rearrange("p (h d) -> p h d", h=20)
        T.qrot = hflat[:, 8192:10752].bitcast(F32)
        T.tmpB = hflat[:, 10752:12032].bitcast(F32).rearrange("p (h d) -> p h d", h=20)
        T.aoTp = hflat[:, 12032:16128].rearrange("p (c t) -> p c t", c=8)
        T.Vs = hflat[:, 12032:12032 + 4160].rearrange("p (b h d) -> p b h d", b=16, h=4)
        T.histT = hflat[:, 0:3968].rearrange("p (j b k) -> p j b k", j=8, b=16)
        T.Kds = T.aT[:].rearrange("p j t -> p (j t)")[:, 0:4096].rearrange("p (b h d) -> p b h d", b=16, h=4)
        T.ybuf = [T.diag[i][:].rearrange("p k c -> p (k c)")[:, 0:2048].bitcast(F32) for i in range(2)]
        T.Pexp = T.ybf[:].rearrange("p j t -> p (j t)").rearrange("p (h b c) -> p h b c", h=16, b=16)

        P = Prog(nc)
        ng = len(set(k for k, _ in rec.groups))
        scr = nc.dram_tensor("wscr", [max(ng, 1), 128, 4096], BF16, kind="Internal").ap()
        Wl = WLoader(P, T, rec.groups, scr, ng)
        emit_all(P, Wl, T, D)
        P.emit()
    return nc


_CACHE = {}


def _rope_tables(start):
    half = 32
    inv = np.power(10000.0, -np.arange(half, dtype=np.float64) / half)
    pos = (start - 128 + np.arange(33 * 128)).astype(np.float64)
    ang = pos[:, None] * inv[None, :]
    angs = (float(PAST_LEN) * inv)[None, :].repeat(128, 0)
    ang = ang.reshape(33, 128, 32).transpose(1, 0, 2)
    ang = np.concatenate([ang, angs[:, None, :]], axis=1)
    return np.ascontiguousarray(np.cos(ang).astype(np.float32)), np.ascontiguousarray(np.sin(ang).astype(np.float32))


def kernel(**inputs):
    in_maps = _prep(**inputs)
    if 'nc' not in _CACHE:
        _CACHE['nc'] = build_program()
    nc = _CACHE['nc']
    res = run_bass_kernel_spmd(nc, in_maps, core_ids=list(range(8)))
    return _assemble(res.results)


def _prep(x_prompt, x_sample, cache_k, cache_v, state_conv, p_prompt, p_sample,
          ln1, w_in, b_glu, q_norm, k_norm, sinks, w_o_attn, conv_dw, conv_dw_b,
          conv_ln_g, conv_ln_b, w_conv_out, b_conv_out, w_out, ln2, w_ff1, w_ff2,
          ln_ple, w_ple_gate, w_ple):
    f = lambda a: np.ascontiguousarray(np.asarray(a, dtype=np.float32))
    x_prompt, x_sample, cache_k, cache_v, state_conv, p_prompt, p_sample = map(
        f, (x_prompt, x_sample, cache_k, cache_v, state_conv, p_prompt, p_sample))
    wts = dict(ln1=f(ln1)[0], w_in=f(w_in)[0], b_glu=f(b_glu)[0], q_norm=f(q_norm)[0], k_norm=f(k_norm)[0],
               sinks=f(sinks)[0], w_o_attn=f(w_o_attn)[0], conv_dw=f(conv_dw)[0], conv_dw_b=f(conv_dw_b)[0],
               conv_ln_g=f(conv_ln_g)[0], conv_ln_b=f(conv_ln_b)[0], w_conv_out=f(w_conv_out)[0],
               b_conv_out=f(b_conv_out)[0], w_out=f(w_out)[0], ln2=f(ln2)[0], w_ff1=f(w_ff1)[0], w_ff2=f(w_ff2)[0],
               ln_ple=f(ln_ple)[0], w_ple_gate=f(w_ple_gate)[0], w_ple=f(w_ple)[0])
    in_maps = []
    L = NT * TT
    for c in range(8):
        b, half = c // 2, c % 2
        start = half * L
        xc = np.zeros((128 + L, 1024), np.float32)
        if half == 1:
            xc[:] = x_prompt[b, start - 128:start + L]
        else:
            xc[128:] = x_prompt[b, 0:L]
        cs_, sn_ = _rope_tables(start)
        m = dict(wts)
        m.update(x=xc, p=np.ascontiguousarray(p_prompt[0, b, start:start + L]),
                 flag=np.full((128, 1), float(half), np.float32), cos=cs_, sin=sn_,
                 xs=np.ascontiguousarray(x_sample[16 * c:16 * c + 16, 0]),
                 psamp=np.ascontiguousarray(p_sample[0, 16 * c:16 * c + 16, 0]),
                 ck=np.ascontiguousarray(cache_k[0, 16 * c:16 * c + 16].reshape(16, 128, 256)),
                 cv=np.ascontiguousarray(cache_v[0, 16 * c:16 * c + 16].reshape(16, 128, 256)),
                 st=np.ascontiguousarray(state_conv[0, 16 * c:16 * c + 16]))
        in_maps.append(m)
    return in_maps


def _assemble(R):
    L = NT * TT
    y_prompt = np.zeros((4, 2 * L, 1024), np.float32)
    y_sample = np.zeros((128, 1, 1024), np.float32)
    nkp = np.zeros((1, 4, 128, 4, 64), np.float32)
    nvp = np.zeros((1, 4, 128, 4, 64), np.float32)
    ncp = np.zeros((1, 4, 30, 1024), np.float32)
    nks = np.zeros((1, 128, 128, 4, 64), np.float32)
    nvs = np.zeros((1, 128, 128, 4, 64), np.float32)
    ncs = np.zeros((1, 128, 30, 1024), np.float32)
    for c in range(8):
        b, half = c // 2, c % 2
        r = R[c]
        y_prompt[b, half * L:(half + 1) * L] = r["y"]
        y_sample[16 * c:16 * c + 16, 0] = r["ys"]
        if half == 1:
            nkp[0, b] = r["kp"].reshape(128, 4, 64)
            nvp[0, b] = r["vp"].reshape(128, 4, 64)
            ncp[0, b] = r["cp"]
        nks[0, 16 * c:16 * c + 16] = r["ks"].reshape(16, 128, 4, 64)
        nvs[0, 16 * c:16 * c + 16] = r["vs"].reshape(16, 128, 4, 64)
        ncs[0, 16 * c:16 * c + 16] = r["cs"]
    return (y_prompt, y_sample, nkp, nvp, ncp, nks, nvs, ncs)
```

```python
import numpy as np
from contextlib import ExitStack
import concourse.bass as bass
import concourse.mybir as mybir
from concourse.bass_utils import run_bass_kernel_spmd

F32 = mybir.dt.float32
BF16 = mybir.dt.bfloat16
AF = mybir.ActivationFunctionType
ALU = mybir.AluOpType
AX = mybir.AxisListType

ENG_ATTR = {'pe': 'tensor', 'act': 'scalar', 'dve': 'vector', 'pool': 'gpsimd', 'sp': 'sync'}
EPS = 1e-6
NT = 8
TT = 512
NSLOT = 6
PAST_LEN = 16384
NT_RUN = NT
STRICT = True
STOP_STAGE = 99
KC_PER = 8
USE_SCRATCH = True
DEFER_HALF = True
SPLIT_RMS = True
MAX_OPS = 10 ** 9
RUN_SAMPLE = True


LOCK_SHARED = {'qT', 'qkf', 'tmpA', 'qrot', 'tmpB', 'Vs', 'aoTp'}


class _Op:
    __slots__ = ('eng', 'fn', 'deps', 'dma', 'sig', 'val')

    def __init__(self, eng, fn, deps, dma):
        self.eng, self.fn, self.deps, self.dma = eng, fn, deps, dma
        self.sig = False
        self.val = 0


class Prog:
    def __init__(self, nc):
        self.nc = nc
        self.ops = []
        self.lastw = {}
        self.rds = {}

    def add(self, eng, fn, reads=(), writes=(), dma=None):
        idx = len(self.ops)
        if idx >= MAX_OPS:
            return idx
        reads = list(reads)
        writes = list(writes)
        alln = reads + writes
        if any(n in LOCK_SHARED for n in alln):
            reads.append('hidlock')
        if 'hid' in alln:
            writes.append('hidlock')
        deps = {}
        for b in reads:
            w = self.lastw.get(b)
            if w is not None:
                deps[w] = 'raw'
            if b.startswith('ps'):
                for r in self.rds.get(b, ()):
                    if self.ops[r].eng != eng and r not in deps:
                        deps[r] = 'psx'
        for b in writes:
            w = self.lastw.get(b)
            if w is not None and w not in deps:
                deps[w] = 'waw'
            for r in self.rds.get(b, ()):
                if r not in deps:
                    deps[r] = 'war'
        self.ops.append(_Op(eng, fn, deps, dma))
        for b in reads:
            self.rds.setdefault(b, []).append(idx)
        for b in writes:
            self.lastw[b] = idx
            self.rds[b] = []
        return idx

    def emit(self):
        nc = self.nc
        ops = self.ops
        for op in ops:
            keep = {}
            for d, kind in op.deps.items():
                D = ops[d]
                if D.dma is None and op.dma is None and D.eng == op.eng:
                    if op.eng == 'pe':
                        continue
                    if kind != 'raw' and not STRICT:
                        continue
                keep[d] = kind
                if D.dma is None:
                    D.sig = True
            op.deps = keep
        cnt = {}
        dcnt = {}
        dpos = {}
        for oi, op in enumerate(ops):
            if op.dma is not None:
                dcnt[op.dma] = dcnt.get(op.dma, 0) + 1
                dpos.setdefault(op.dma, []).append(oi)
                op.val = 16 * dcnt[op.dma]
            elif op.sig:
                cnt[op.eng] = cnt.get(op.eng, 0) + 1
                op.val = cnt[op.eng]
        engines = ['pe', 'act', 'dve', 'pool', 'sp']
        with ExitStack() as es:
            esem = {e: es.enter_context(nc.semaphore("s_" + e)) for e in engines}
            dsem = {k: es.enter_context(nc.semaphore("d_%d" % i)) for i, k in enumerate(sorted(dcnt))}
            block = es.enter_context(nc.Block())

            import bisect

            def body(e, eng):
                waited = {}
                for oi, op in enumerate(ops):
                    if op.eng != eng:
                        continue
                    need = {}
                    for d in op.deps:
                        D = ops[d]
                        key = ('d', D.dma) if D.dma is not None else ('e', D.eng)
                        v = D.val
                        if D.dma is not None:
                            v = 16 * bisect.bisect_left(dpos[D.dma], oi)
                        if v > need.get(key, 0):
                            need[key] = v
                    for key, v in need.items():
                        if waited.get(key, 0) >= v:
                            continue
                        waited[key] = v
                        e.wait_ge(dsem[key[1]] if key[0] == 'd' else esem[key[1]], v)
                    ins = op.fn(e)
                    if op.dma is not None:
                        ins.then_inc(dsem[op.dma], 16)
                    elif op.sig:
                        ins.then_inc(esem[eng], 1)
                if eng == 'sp':
                    for k, c in dcnt.items():
                        if waited.get(('d', k), 0) < 16 * c:
                            e.wait_ge(dsem[k], 16 * c)
                    for en, c in cnt.items():
                        if c:
                            e.wait_ge(esem[en], c)

            for eng in engines:
                getattr(block, ENG_ATTR[eng])(lambda e, eng=eng: body(e, eng))
        return nc


class DryProg:
    def add(self, *a, **k):
        return 0


class WRec:
    def __init__(self):
        self.groups = []

    def next(self, spec):
        self.groups.append(spec)
        return 0

    def release(self, slot):
        pass

    def prefetch(self):
        pass


class WLoader:
    def __init__(self, P, T, groups, scr, ng):
        self.P, self.T, self.groups = P, T, groups
        self.scr, self.ng = scr, ng
        self.use = 0
        self.load = 0
        self.free = list(range(NSLOT))
        self.slot_of = {}
        self.kidx = {}
        self.seen = {}

    def _issue(self, gi, slot):
        wt = self.T.W[slot]
        key, spec = self.groups[gi]
        if key in self.kidx and USE_SCRATCH:
            g0 = self.kidx[key]
            self.P.add('pool', lambda e, wt=wt, g0=g0: e.dma_start(out=wt[:, :], in_=self.scr[g0]),
                       reads=['wscr.%d' % g0], writes=wnames(slot), dma='w%d' % slot)
            return
        occ = self.seen.get(key, 0)
        self.seen[key] = occ + 1
        first = (occ >= 1) or (len(self.seen) % 2 == 0) or not DEFER_HALF
        if first:
            self.kidx[key] = len(self.kidx)
        gidx = self.kidx.get(key, -1)
        for pi, (src, d0, d1, a, b) in enumerate(spec):
            dst = wt[:, d0:d1].rearrange("p (a b) -> p a b", a=a)
            for k0 in range(0, a, KC_PER):
                k1 = min(a, k0 + KC_PER)
                self.P.add('pool', lambda e, dst=dst, src=src, k0=k0, k1=k1: e.dma_start(out=dst[:, k0:k1, :], in_=src[:, k0:k1, :]),
                           writes=['w%d.%d.%d' % (slot, pi, k0)], dma='w%d' % slot)
        if USE_SCRATCH and first:
            self.P.add('sp', lambda e, wt=wt, gidx=gidx: e.dma_start(out=self.scr[gidx], in_=wt[:, :]),
                       reads=wnames(slot), writes=['wscr.%d' % gidx], dma='wst%d' % slot)

    def _prefetch(self):
        while self.load < len(self.groups) and self.free:
            slot = self.free.pop(0)
            self.slot_of[self.load] = slot
            self._issue(self.load, slot)
            self.load += 1

    def next(self, spec):
        self._prefetch()
        assert self.use in self.slot_of, "no free weight slot (too many groups held)"
        slot = self.slot_of.pop(self.use)
        self.use += 1
        return slot

    def release(self, slot):
        self.free.append(slot)
        self._prefetch()

    def prefetch(self):
        self._prefetch()


def wnames(slot):
    return ['w%d.%d.%d' % (slot, pi, k0) for pi in range(2) for k0 in range(0, 8, KC_PER)]


class TT_:
    pass


def emit_all(P, W, T, D):
    bankctr = [0]
    held = set()

    def bank():
        while True:
            i = bankctr[0] % 8
            bankctr[0] += 1
            if i not in held:
                return T.ps[i], 'ps%d' % i, i

    def psb(bk):
        return bk[:].bitcast(BF16)

    rr = {'sg': 0, 'pt': 0, 'dg': 0, 'y2': 0, 'tl': 0, 'rl': 0, 'ev': 0}

    def rot(key, n=2):
        rr[key] = (rr[key] + 1) % n
        return rr[key]

    def cn(base, n=8):
        if base == 'xn':
            return cn('ybf', 2 * n)
        return ['%s.%d' % (base, i) for i in range(n)]

    def xnn(s):
        return ['ybf.%d' % (2 * s), 'ybf.%d' % (2 * s + 1)]

    def wv(ap2d, c0, ncol, kc=8):
        return ap2d.rearrange("(kc p) n -> p kc n", p=128)[:, :, c0:c0 + ncol]

    def g_cols(ap2d, c0):
        return (('c', ap2d.tensor.name, c0), [(wv(ap2d, c0, 512), 0, 4096, 8, 512)])

    def g_glu(gi):
        return (('glu', gi), [(wv(D.w_in, 1536 + gi * 256, 256), 0, 2048, 8, 256),
                              (wv(D.w_in, 2560 + gi * 256, 256), 2048, 4096, 8, 256)])

    def g_ff2(fg, nh):
        src = D.w_ff2.rearrange("(fc p) n -> p fc n", p=128)[:, fg * 8:(fg + 1) * 8, nh * 512:(nh + 1) * 512]
        return (('ff2', fg, nh), [(src, 0, 4096, 8, 512)])

    def g_ple():
        return (('ple',), [(D.w_ple.rearrange("(kc p) n -> p kc n", p=128), 0, 2048, 2, 1024)])

    def wslot(slot, a=8, b=512):
        return T.W[slot][:, 0:a * b].rearrange("p (a b) -> p a b", a=a)

    W.prefetch()
    P.add('pool', lambda e: e.memset(T.identf[:], 0.0), writes=['identf'])
    P.add('pool', lambda e: e.affine_select(out=T.identf[:], in_=T.identf[:], pattern=[[-1, 128]],
                                            compare_op=ALU.not_equal, fill=1.0, base=0, channel_multiplier=1),
          reads=['identf'], writes=['identf'])
    P.add('dve', lambda e: e.tensor_copy(out=T.ident[:], in_=T.identf[:]), reads=['identf'], writes=['ident'])
    P.add('pool', lambda e: e.memset(T.ones[:], 1.0), writes=['ones'])
    P.add('pool', lambda e: e.memset(T.neghalf[:], -0.5), writes=['neghalf'])
    P.add('pool', lambda e: e.memset(T.maskf[:], 1.0), writes=['maskf'])
    P.add('pool', lambda e: e.affine_select(out=T.maskf[:, 0, :], in_=T.maskf[:, 0, :], pattern=[[-1, 128]],
                                            compare_op=ALU.is_gt, fill=0.0, base=0, channel_multiplier=1),
          reads=['maskf'], writes=['maskf'])
    P.add('pool', lambda e: e.affine_select(out=T.maskf[:, 1, :], in_=T.maskf[:, 1, :], pattern=[[1, 128]],
                                            compare_op=ALU.is_ge, fill=0.0, base=0, channel_multiplier=-1),
          reads=['maskf'], writes=['maskf'])
    P.add('dve', lambda e: e.tensor_copy(out=T.mask[:], in_=T.maskf[:]), reads=['maskf'], writes=['mask'])
    P.add('pool', lambda e: e.memset(T.Vaug[:], 1.0), writes=cn('V', 5))
    P.add('sp', lambda e: e.dma_start(out=T.cos[:, 0, :], in_=D.cos[:, 0, :]), writes=['cos'], dma='cos')
    P.add('sp', lambda e: e.dma_start(out=T.sin[:, 0, :], in_=D.sin[:, 0, :]), writes=['sin'], dma='sin')
    P.add('sp', lambda e: e.dma_start(out=T.flag[:], in_=D.flag), writes=['flag'], dma='flag')
    P.add('sp', lambda e: e.dma_start(out=T.gq[:], in_=D.q_norm.partition_broadcast(128)), writes=['gq'], dma='gq')
    P.add('sp', lambda e: e.dma_start(out=T.gk[:], in_=D.k_norm.partition_broadcast(128)), writes=['gk'], dma='gk')
    P.add('sp', lambda e: e.dma_start(out=T.esink[:], in_=D.sinks.partition_broadcast(128)), writes=['esink'], dma='esink')
    rows = [D.ln1, D.ln2, D.ln_ple, D.b_glu[0:1024], D.b_glu[1024:2048], D.conv_dw_b, D.conv_ln_g, D.conv_ln_b,
            D.b_conv_out]
    for r, src in enumerate(rows):
        P.add('sp', lambda e, r=r, src=src: e.dma_start(out=T.prows[r:r + 1, :], in_=src.rearrange("(o n) -> o n", o=1)),
              writes=['pr%d' % r], dma='prm')
    P.add('sp', lambda e: e.dma_start(out=T.prows[9:40, :], in_=D.conv_dw), writes=['pr9'], dma='prm')
    if RUN_SAMPLE:
        P.add('sp', lambda e: e.dma_start(out=D.ks[:, 0:127, :], in_=D.ck[:, 1:128, :]), writes=['ks.A'], dma='ksA')
        P.add('sp', lambda e: e.dma_start(out=D.vs[:, 0:127, :], in_=D.cv[:, 1:128, :]), writes=['vs.A'], dma='vsA')
        P.add('sp', lambda e: e.dma_start(out=D.cs[:, 0:29, :], in_=D.st[:, 1:30, :]), writes=['cs.A'], dma='csA')
    bk, bkn, _ = bank()

    def ptr(e):
        for c in range(8):
            ins = e.transpose(out=bk[:, c * 40:(c + 1) * 40], in_=T.prows[0:40, c * 128:(c + 1) * 128],
                              identity=T.identf[0:40, 0:40])
        return ins
    P.add('pe', ptr, reads=['pr%d' % r for r in range(10)] + ['identf', 'tl0', 'tl1'], writes=[bkn])
    P.add('dve', lambda e: e.tensor_copy(out=T.prm[:].rearrange("p c r -> p (c r)"), in_=bk[:, 0:320]),
          reads=[bkn], writes=['prm'])
    P.add('dve', lambda e: e.tensor_copy(out=T.wdw[:], in_=T.prm[:, :, 9:40]), reads=['prm'], writes=['wdw'])
    P.add('dve', lambda e: e.tensor_scalar(out=T.gfull[:, 0:16, :], in0=T.gq[:].unsqueeze(1).to_broadcast([128, 16, 64]),
                                           scalar1=0.125, scalar2=None, op0=ALU.mult), reads=['gq'], writes=['gfull'])
    P.add('dve', lambda e: e.tensor_copy(out=T.gfull[:, 16:20, :], in_=T.gk[:].unsqueeze(1).to_broadcast([128, 4, 64])),
          reads=['gk'], writes=['gfull'])
    P.add('act', lambda e: e.activation(out=T.esink[:], in_=T.esink[:], func=AF.Exp), reads=['esink'], writes=['esink'])

    def rms_head_sub(src_fn, src_names, Pn, s):
        P.add('act', lambda e: e.activation(out=T.xn[:Pn, s, :], in_=src_fn(s), func=AF.Square,
                                            accum_out=T.ss[:Pn, s:s + 1]), reads=src_names, writes=xnn(s) + ['ss.%d' % s])
        P.add('dve', lambda e: e.tensor_scalar(out=T.ms[:Pn, s:s + 1], in0=T.ss[:Pn, s:s + 1], scalar1=1.0 / 1024,
                                               scalar2=EPS, op0=ALU.mult, op1=ALU.add), reads=['ss.%d' % s], writes=['ms.%d' % s])
        P.add('pool', lambda e: e.tensor_tensor(out=T.rstd[:Pn, s:s + 1], in0=T.ms[:Pn, s:s + 1],
                                                in1=T.neghalf[:Pn, s:s + 1], op=ALU.pow),
              reads=['ms.%d' % s, 'neghalf'], writes=['rstd.%d' % s])
        P.add('dve', lambda e: e.tensor_scalar(out=T.xn[:Pn, s, :], in0=src_fn(s), scalar1=T.rstd[:Pn, s:s + 1],
                                               scalar2=None, op0=ALU.mult), reads=src_names + ['rstd.%d' % s], writes=xnn(s))

    def rms_T(src_fn, src_names, Pn, nsub, gidx, dst, dnames, col0, heads_done=False):
        N = Pn * nsub
        src_names = [n if isinstance(n, list) else [n] for n in src_names]
        if heads_done:
            return rms_tail(Pn, nsub, gidx, dst, dnames, col0)
        for s in range(nsub):
            if True:
                P.add('act', lambda e, s=s: e.activation(out=T.xn[:Pn, s, :], in_=src_fn(s), func=AF.Square,
                                                          accum_out=T.ss[:Pn, s:s + 1]),
                      reads=src_names[s], writes=xnn(s) + ['ss.%d' % s])
            else:
                P.add('dve', lambda e, s=s: e.tensor_tensor_reduce(out=T.xn[:Pn, s, :], in0=src_fn(s), in1=src_fn(s),
                                                                   scale=1.0, scalar=0.0, op0=ALU.mult, op1=ALU.add,
                                                                   accum_out=T.ss[:Pn, s:s + 1]),
                      reads=src_names[s], writes=xnn(s) + ['ss.%d' % s])
        P.add('dve', lambda e: e.tensor_scalar(out=T.ms[:Pn, 0:nsub], in0=T.ss[:Pn, 0:nsub], scalar1=1.0 / 1024,
                                               scalar2=EPS, op0=ALU.mult, op1=ALU.add), reads=['ss.%d' % s for s in range(nsub)],
              writes=['ms.%d' % s for s in range(nsub)])
        P.add('pool', lambda e: e.tensor_tensor(out=T.rstd[:Pn, 0:nsub], in0=T.ms[:Pn, 0:nsub],
                                                in1=T.neghalf[:Pn, 0:nsub], op=ALU.pow),
              reads=['ms.%d' % s for s in range(nsub)] + ['neghalf'], writes=['rstd.%d' % s for s in range(nsub)])
        for s in range(nsub):
            if s % 2 == 0 or not SPLIT_RMS:
                P.add('dve', lambda e, s=s: e.tensor_scalar(out=T.xn[:Pn, s, :], in0=src_fn(s), scalar1=T.rstd[:Pn, s:s + 1],
                                                            scalar2=None, op0=ALU.mult),
                      reads=src_names[s] + ['rstd.%d' % s], writes=xnn(s))
            else:
                P.add('act', lambda e, s=s: e.activation(out=T.xn[:Pn, s, :], in_=src_fn(s), func=AF.Copy,
                                                          scale=T.rstd[:Pn, s:s + 1]),
                      reads=src_names[s] + ['rstd.%d' % s], writes=xnn(s))
        rms_tail(Pn, nsub, gidx, dst, dnames, col0)

    def rms_tail(Pn, nsub, gidx, dst, dnames, col0):
        N = Pn * nsub
        for c in range(8):
            bk, bkn, _ = bank()
            pb = psb(bk)

            def tr(e, c=c, pb=pb):
                for s in range(nsub):
                    ins = e.transpose(out=pb[:, s * Pn:(s + 1) * Pn], in_=T.xn[:Pn, s, c * 128:(c + 1) * 128],
                                      identity=T.ident[:Pn, :Pn])
                return ins
            P.add('pe', tr, reads=sum([xnn(s) for s in range(nsub)], []) + ['ident'], writes=[bkn])
            if c % 2 == 0:
                P.add('act', lambda e, c=c, pb=pb: e.activation(out=dst[:, c, col0:col0 + N], in_=pb[:, 0:N], func=AF.Copy,
                                                                scale=T.prm[:, c, gidx:gidx + 1]),
                      reads=[bkn, 'prm'], writes=[dnames[c]])
            else:
                P.add('dve', lambda e, c=c, pb=pb: e.tensor_scalar(out=dst[:, c, col0:col0 + N], in0=pb[:, 0:N],
                                                                   scalar1=T.prm[:, c, gidx:gidx + 1], scalar2=None,
                                                                   op0=ALU.mult),
                      reads=[bkn, 'prm'], writes=[dnames[c]])

    def qkv_sub(Pn, ucol, tblk, slots, h0, qcol, kname, kcol, vblk, kvout, part='ab'):
        if 'a' in part:
            qkv_a(Pn, ucol, tblk, slots, h0, vblk, kvout)
        if 'b' in part:
            qkv_b(Pn, h0, qcol, kname, kcol)

    def qkv_a(Pn, ucol, tblk, slots, h0, vblk, kvout):
        sq0, sq1, skv = slots
        e0 = h0 * 64
        banks = []
        grp = ([(sq0, 0), (sq1, 512)] if h0 == 0 else []) + [(skv, 1024)]
        for slot, qo in grp:
            bk, bkn, _ = bank()
            banks.append((bk, bkn, qo))

            def mm(e, slot=slot, bk=bk):
                w = wslot(slot)
                for kc in range(8):
                    ins = e.matmul(bk[:Pn, :], lhsT=T.uT[:, kc, ucol:ucol + Pn], rhs=w[:, kc, :], start=(kc == 0),
                                   stop=(kc == 7))
                return ins
            P.add('pe', mm, reads=cn('uT') + wnames(slot), writes=[bkn])
        for bk, bkn, qo in banks:
            ncol = 512 if qo < 1024 else 256
            P.add('act', lambda e, bk=bk, qo=qo, ncol=ncol: e.activation(out=T.qkf[:Pn, qo:qo + ncol], in_=bk[:Pn, 0:ncol],
                                                                          func=AF.Copy),
                  reads=[bkn], writes=['qkf'])
        bkv, bkvn, _ = banks[-1]
        P.add('dve', lambda e: e.tensor_copy(out=T.Vaug[:Pn, vblk, :, 0:64],
                                             in_=bkv[:Pn, 256:512].rearrange("p (h d) -> p h d", h=4)),
              reads=[bkvn], writes=['V.%d' % vblk])
        if kvout:
            P.add('act', lambda e: e.activation(out=T.kvout[:Pn, 256:512], in_=bkv[:Pn, 256:512], func=AF.Copy),
                  reads=[bkvn], writes=['kvout.v'])
        nh_ = 20 - h0
        qv = T.qkf[:Pn, e0:1280]
        rv = T.qrot[:Pn, e0:1280]
        P.add('act', lambda e: e.activation(out=rv, in_=qv, func=AF.Square), reads=['qkf'], writes=['qrot'])
        P.add('dve', lambda e: e.tensor_reduce(out=T.ssqk[:Pn, h0:20], in_=rv.rearrange("p (h d) -> p h d", d=64),
                                               axis=AX.X, op=ALU.add), reads=['qrot'], writes=['ssqk'])
        P.add('dve', lambda e: e.tensor_scalar(out=T.msqk[:Pn, h0:20], in0=T.ssqk[:Pn, h0:20], scalar1=1.0 / 64,
                                               scalar2=EPS, op0=ALU.mult, op1=ALU.add), reads=['ssqk'], writes=['msqk'])
        P.add('act', lambda e: e.activation(out=T.rqk[:Pn, h0:20], in_=T.msqk[:Pn, h0:20], func=AF.Ln),
              reads=['msqk'], writes=['rqk'])
        P.add('act', lambda e: e.activation(out=T.rqk[:Pn, h0:20], in_=T.rqk[:Pn, h0:20], func=AF.Exp, scale=-0.5),
              reads=['rqk'], writes=['rqk'])
        P.add('dve', lambda e: e.tensor_tensor(out=qv, in0=qv, in1=T.gfull[:Pn, h0:20, :].rearrange("p h d -> p (h d)"),
                                               op=ALU.mult), reads=['qkf', 'gfull'], writes=['qkf'])
        g4 = qv.rearrange("p (h t d) -> p h t d", t=2, d=32)
        r4 = rv.rearrange("p (h t d) -> p h t d", t=2, d=32)
        g1, g2 = g4[:, :, 0, :], g4[:, :, 1, :]
        cb = T.cos[:Pn, tblk, :].unsqueeze(1).to_broadcast([Pn, nh_, 32])
        sb_ = T.sin[:Pn, tblk, :].unsqueeze(1).to_broadcast([Pn, nh_, 32])
        tA = T.tmpA[:Pn, 0:nh_, :]
        tB = T.tmpB[:Pn, 0:nh_, :]
        P.add('dve', lambda e: e.tensor_tensor(out=tA, in0=g1, in1=cb, op=ALU.mult), reads=['qkf', 'cos'], writes=['tmpA'])
        P.add('dve', lambda e: e.tensor_tensor(out=tB, in0=g2, in1=sb_, op=ALU.mult), reads=['qkf', 'sin'], writes=['tmpB'])
        P.add('dve', lambda e: e.tensor_tensor(out=r4[:, :, 0, :], in0=tA, in1=tB, op=ALU.subtract),
              reads=['tmpA', 'tmpB'], writes=['qrot'])
        P.add('dve', lambda e: e.tensor_tensor(out=tA, in0=g2, in1=cb, op=ALU.mult), reads=['qkf', 'cos'], writes=['tmpA'])
        P.add('dve', lambda e: e.tensor_tensor(out=tB, in0=g1, in1=sb_, op=ALU.mult), reads=['qkf', 'sin'], writes=['tmpB'])
        P.add('dve', lambda e: e.tensor_tensor(out=r4[:, :, 1, :], in0=tA, in1=tB, op=ALU.add),
              reads=['tmpA', 'tmpB'], writes=['qrot'])
        r3 = T.qrot[:Pn, :].rearrange("p (h d) -> p h d", d=64)
        if h0 == 0:
            P.add('dve', lambda e: e.tensor_tensor(out=T.qbf[:Pn, :, :], in0=r3[:, 0:16, :],
                                                   in1=T.rqk[:Pn, 0:16].unsqueeze(2).to_broadcast([Pn, 16, 64]),
                                                   op=ALU.mult), reads=['qrot', 'rqk'], writes=['qbf'])
        P.add('dve', lambda e: e.tensor_tensor(
            out=T.kdbf[:Pn, :, :, :], in0=r3[:, 16:20, :].unsqueeze(2).to_broadcast([Pn, 4, 2, 64]),
            in1=T.rqk[:Pn, 16:20].unsqueeze(2).unsqueeze(3).to_broadcast([Pn, 4, 2, 64]), op=ALU.mult),
            reads=['qrot', 'rqk'], writes=['kdbf'])
        if kvout:
            P.add('dve', lambda e: e.tensor_tensor(out=T.kvout[:Pn, 0:256].rearrange("p (h d) -> p h d", d=64),
                                                   in0=r3[:, 16:20, :],
                                                   in1=T.rqk[:Pn, 16:20].unsqueeze(2).to_broadcast([Pn, 4, 64]),
                                                   op=ALU.mult), reads=['qrot', 'rqk'], writes=['kvout.k'])

    def qkv_b(Pn, h0, qcol, kname, kcol):
        if h0 == 0:
            bk, bkn, _ = bank()
            pb = psb(bk)

            def trq(e, pb=pb):
                qf = T.qbf[:Pn, :, :].rearrange("p h d -> p (h d)")
                for c in range(8):
                    ins = e.transpose(out=pb[:, c * 128:c * 128 + Pn], in_=qf[:, c * 128:(c + 1) * 128],
                                      identity=T.ident[:Pn, :Pn])
                return ins
            P.add('pe', trq, reads=['qbf', 'ident'], writes=[bkn])
            P.add('act', lambda e, pb=pb: e.activation(out=T.qT[:, :, qcol:qcol + Pn],
                                                       in_=pb.rearrange("p (c t) -> p c t", c=8)[:, :, 0:Pn], func=AF.Copy),
                  reads=[bkn], writes=['qT'])
        if kname is not None:
            bk, bkn, _ = bank()
            pb = psb(bk)

            def trk(e, pb=pb):
                kf = T.kdbf[:Pn, :, :, :].rearrange("p h t d -> p (h t d)")
                for hk in range(4):
                    ins = e.transpose(out=pb[:, hk * 128:hk * 128 + Pn], in_=kf[:, hk * 128:(hk + 1) * 128],
                                      identity=T.ident[:Pn, :Pn])
                return ins
            P.add('pe', trk, reads=['kdbf', 'ident'], writes=[bkn])
            P.add('dve', lambda e, pb=pb: e.tensor_copy(out=T.kT[:, :, kcol:kcol + Pn],
                                                        in_=pb[:, 0:512].rearrange("p (c t) -> p c t", c=4)[:, :, 0:Pn]),
                  reads=[bkn], writes=[kname])

    def normalize_heads(bO, bOn, hk, Pn):
        ov = bO[:Pn, 0:260].rearrange("p (g d) -> p g d", g=4)
        P.add('dve', lambda e: e.tensor_tensor(out=T.den[:Pn, :], in0=ov[:, :, 64], in1=T.esink[:Pn, 4 * hk:4 * hk + 4],
                                               op=ALU.add), reads=[bOn, 'esink'], writes=['den'])
        P.add('dve', lambda e: e.reciprocal(out=T.rden[:Pn, :], in_=T.den[:Pn, :]), reads=['den'], writes=['rden'])
        P.add('dve', lambda e: e.tensor_tensor(out=T.ao[:Pn, 4 * hk:4 * hk + 4, :], in0=ov[:, :, 0:64],
                                               in1=T.rden[:Pn, :].unsqueeze(2).to_broadcast([Pn, 4, 64]), op=ALU.mult),
              reads=[bOn, 'rden'], writes=['ao'])

    def ao_transpose(Pn, col, dst=None, dnames=None):
        if dst is None:
            dst, dnames = T.aoT, cn('sT')
        bk, bkn, _ = bank()
        pb = psb(bk)

        def tr(e):
            af = T.ao[:Pn, :, :].rearrange("p h d -> p (h d)")
            for c in range(8):
                ins = e.transpose(out=pb[:, c * 128:c * 128 + Pn], in_=af[:, c * 128:(c + 1) * 128],
                                  identity=T.ident[:Pn, :Pn])
            return ins
        P.add('pe', tr, reads=['ao', 'ident'], writes=[bkn])
        P.add('act', lambda e: e.activation(out=dst[:, :, col:col + Pn],
                                            in_=pb.rearrange("p (c t) -> p c t", c=8)[:, :, 0:Pn], func=AF.Copy),
              reads=[bkn], writes=dnames)

    ast = {}

    def attn_sc(s, hk):
        qc = s * 128
        bp, bpn, _ = bank()
        bo, bon, _ = bank()

        def sc(e):
            banks2 = ((bp, s * 128), (bo, 128 + s * 128))
            for par in (0, 1):
                if par == 1:
                    e.matmul(bp[:, 128:129], lhsT=T.ident[:, :], rhs=T.ident[:, 0:1], start=True, stop=True)
                for bk, kcol in banks2:
                    v = bk[:].rearrange("p (g q) -> p g q", g=4)
                    for g in (par, par + 2):
                        ins = e.matmul(v[:, g, :], lhsT=T.kT[64 * par:64 * par + 64, hk, kcol:kcol + 128],
                                       rhs=T.qT[64 * par:64 * par + 64, 2 * hk + g // 2, qc:qc + 128],
                                       start=True, stop=True)
            return ins
        P.add('pe', sc, reads=['kT.%d' % s, 'kT.%d' % (s + 1), 'qT', 'ident'], writes=[bpn, bon])
        i = rot('pt')
        pt = T.PT[i]
        ptn = 'PT%d' % i
        P.add('act', lambda e: e.activation(out=pt[:, 0, :], in_=bp[:], func=AF.Exp), reads=[bpn], writes=[ptn])
        P.add('act', lambda e: e.activation(out=pt[:, 1, :], in_=bo[:], func=AF.Exp), reads=[bon], writes=[ptn])
        pv4 = pt[:].rearrange("p w (g q) -> p w g q", g=4)
        P.add('dve', lambda e: e.tensor_tensor(out=pv4, in0=pv4, in1=T.mask[:].unsqueeze(2).to_broadcast([128, 2, 4, 128]),
                                               op=ALU.mult), reads=[ptn, 'mask'], writes=[ptn])
        ast[(s, hk)] = (pt, ptn)

    def attn_pv(s, hk):
        pt, ptn = ast.pop((s, hk))
        bO, bOn, _ = bank()

        def pvm(e):
            for g in range(4):
                e.matmul(bO[:, g * 65:(g + 1) * 65], lhsT=pt[:, 0, g * 128:(g + 1) * 128], rhs=T.Vaug[:, s, hk, :],
                         start=True, stop=False)
                ins = e.matmul(bO[:, g * 65:(g + 1) * 65], lhsT=pt[:, 1, g * 128:(g + 1) * 128],
                               rhs=T.Vaug[:, s + 1, hk, :], start=False, stop=True)
            return ins
        P.add('pe', pvm, reads=[ptn, 'V.%d' % s, 'V.%d' % (s + 1)], writes=[bOn])
        normalize_heads(bO, bOn, hk, 128)

    def attn_fin(s):
        ao_transpose(128, s * 128, T.aoTp, ['aoTp'])

    def attn_block(s):
        attn_sc(s, 0)
        for hk in range(4):
            if hk + 1 < 4:
                attn_sc(s, hk + 1)
            attn_pv(s, hk)
        attn_fin(s)

    def glu_stage(N, ucol, mode, last):
        for gi in range(4):
            glu_group(gi, N, ucol, mode, last)

    def glu_group(gi, N, ucol, mode, last):
        if True:
            slot = W.next(g_glu(gi))
            wa = T.W[slot][:, 0:2048].rearrange("p (a b) -> p a b", a=8)
            wb = T.W[slot][:, 2048:4096].rearrange("p (a b) -> p a b", a=8)
            for jj in range(2):
                j = 2 * gi + jj
                calls = [(N, ucol, mode)]
                if mode == 'main0':
                    calls = [(32, 96, 'halo'), (N, ucol, 'main')]
                for (n_, uc, md) in calls:
                    b1, b1n, _ = bank()
                    b2, b2n, _ = bank()

                    def mm(e, b1=b1, b2=b2, n_=n_, uc=uc, jj=jj, wa=wa, wb=wb):
                        for kc in range(8):
                            e.matmul(b1[:, 0:n_], lhsT=wa[:, kc, jj * 128:(jj + 1) * 128], rhs=T.uT[:, kc, uc:uc + n_],
                                     start=(kc == 0), stop=(kc == 7))
                        for kc in range(8):
                            ins = e.matmul(b2[:, 0:n_], lhsT=wb[:, kc, jj * 128:(jj + 1) * 128],
                                           rhs=T.uT[:, kc, uc:uc + n_], start=(kc == 0), stop=(kc == 7))
                        return ins
                    P.add('pe', mm, reads=cn('uT') + wnames(slot), writes=[b1n, b2n])
                    i = rot('sg')
                    sg = T.sg[i]
                    sgn = 'sg%d' % i
                    P.add('act', lambda e, sg=sg, b2=b2, n_=n_, j=j: e.activation(out=sg[:, 0:n_], in_=b2[:, 0:n_],
                                                                                 func=AF.Sigmoid, bias=T.prm[:, j, 4:5]),
                          reads=[b2n, 'prm'], writes=[sgn])
                    if md == 'halo':
                        P.add('dve', lambda e, sg=sg, b1=b1, j=j: e.scalar_tensor_tensor(
                            out=T.aT[:, j, 0:30], in0=b1[:, 2:32], scalar=T.prm[:, j, 3:4], in1=sg[:, 2:32],
                            op0=ALU.add, op1=ALU.mult), reads=[b1n, sgn, 'prm'], writes=['aT.h'])
                        P.add('dve', lambda e, j=j: e.tensor_scalar(out=T.aT[:, j, 0:30], in0=T.aT[:, j, 0:30],
                                                                    scalar1=T.flag[:, 0:1], scalar2=None, op0=ALU.mult),
                              reads=['aT.h', 'flag'], writes=['aT.h'])
                    elif md == 'sample':
                        P.add('dve', lambda e, sg=sg, b1=b1, j=j: e.scalar_tensor_tensor(
                            out=T.a32[:, j, 0:16], in0=b1[:, 0:16], scalar=T.prm[:, j, 3:4], in1=sg[:, 0:16],
                            op0=ALU.add, op1=ALU.mult), reads=[b1n, sgn, 'prm'], writes=['a32'])
                        P.add('dve', lambda e, j=j: e.tensor_copy(out=T.histT[:, j, :, 30], in_=T.a32[:, j, 0:16]),
                              reads=['a32'], writes=['qT'])
                    else:
                        P.add('dve', lambda e, sg=sg, b1=b1, j=j, n_=n_: e.scalar_tensor_tensor(
                            out=T.aT[:, j, 30:30 + n_], in0=b1[:, 0:n_], scalar=T.prm[:, j, 3:4], in1=sg[:, 0:n_],
                            op0=ALU.add, op1=ALU.mult), reads=[b1n, sgn, 'prm'], writes=['aT.m%d' % j])
                        if last:
                            P.add('dve', lambda e, sg=sg, b1=b1, j=j, n_=n_: e.scalar_tensor_tensor(
                                out=T.a32[:, j, 0:32], in0=b1[:, n_ - 32:n_], scalar=T.prm[:, j, 3:4], in1=sg[:, n_ - 32:n_],
                                op0=ALU.add, op1=ALU.mult), reads=[b1n, sgn, 'prm'], writes=['a32'])

            W.release(slot)

    cst = {}

    def conv_ln_stage(N, sample):
        conv_begin()
        for j in range(8):
            conv_chunk(j, N, sample)
        conv_finish(N)

    def conv_begin():
        bsum, bsumn, isum = bank()
        held.add(isum)
        bsq, bsqn, isq = bank()
        held.add(isq)
        cst['b'] = (bsum, bsumn, isum, bsq, bsqn, isq)

    def conv_chunk(j, N, sample):
        if not sample:
            diag_build(j)
        conv_mm(j, N, sample)

    def diag_build(j):
        i = rot('dg')
        dg = T.diag[i]
        dgn = 'diag%d' % i
        cst['dg%d' % j] = (dg, dgn)
        P.add('dve', lambda e, dg=dg, j=j: e.tensor_tensor(
            out=dg[:], in0=T.ident[:].unsqueeze(1).to_broadcast([128, 31, 128]),
            in1=T.wdw[:, j, :].unsqueeze(2).to_broadcast([128, 31, 128]), op=ALU.mult),
            reads=['ident', 'wdw'], writes=[dgn])

    def conv_mm(j, N, sample):
        bsum, bsumn, isum, bsq, bsqn, isq = cst['b']
        if sample:
            i = rot('tl')
            tl = T.tl[i]
            tln = 'tl%d' % i
            t3 = tl[:, 0:N * 31].rearrange("p (b k) -> p b k", k=31)
            P.add('dve', lambda e: e.tensor_tensor(out=t3, in0=T.histT[:, j, :, :],
                                                   in1=T.wdw[:, j, :].unsqueeze(1).to_broadcast([128, N, 31]), op=ALU.mult),
                  reads=['qT', 'wdw'], writes=[tln])
            bc = T.ycs
            bcn = 'ycs'
            P.add('dve', lambda e: e.tensor_reduce(out=T.ycs[:, 0:N], in_=t3, axis=AX.X, op=ALU.add), reads=[tln],
                  writes=['ycs'])
        else:
            dg, dgn = cst['dg%d' % j]
            bc, bcn, _ = bank()

            def cm(e, dg=dg, bc=bc, j=j):
                for k in range(31):
                    ins = e.matmul(bc[:, 0:N], lhsT=dg[:, k, :], rhs=T.aT[:, j, k:k + N], start=(k == 0), stop=(k == 30))
                return ins
            P.add('pe', cm, reads=[dgn, 'aT.h', 'aT.m%d' % j], writes=[bcn])
        if True:
            P.add('act', lambda e, bc=bc, j=j: e.activation(out=T.ybf[:, j, 0:N], in_=bc[:, 0:N], func=AF.Identity,
                                                            bias=T.prm[:, j, 5:6]), reads=[bcn, 'prm'], writes=['ybf.%d' % j])
            i2 = rot('y2')
            y2 = T.y2[i2]
            y2n = 'y2%d' % i2
            P.add('act', lambda e, bc=bc, j=j, y2=y2: e.activation(out=y2[:, 0:N], in_=bc[:, 0:N], func=AF.Square,
                                                                   bias=T.prm[:, j, 5:6]), reads=[bcn, 'prm'], writes=[y2n])

            def st(e, j=j, y2=y2):
                e.matmul(bsum[:, 0:N], lhsT=T.ones[:], rhs=T.ybf[:, j, 0:N], start=(j == 0), stop=(j == 7))
                return e.matmul(bsq[:, 0:N], lhsT=T.ones[:], rhs=y2[:, 0:N], start=(j == 0), stop=(j == 7))
            P.add('pe', st, reads=['ones', 'ybf.%d' % j, y2n], writes=[bsumn, bsqn])

    def conv_finish(N):
        ln_head(N)
        for j in range(8):
            ln_chunk(j, N)

    def ln_head(N):
        bsum, bsumn, isum, bsq, bsqn, isq = cst['b']
        P.add('act', lambda e: e.activation(out=T.mean[:, 0:N], in_=bsum[:, 0:N], func=AF.Copy, scale=1.0 / 1024),
              reads=[bsumn], writes=['mean'])
        P.add('dve', lambda e: e.tensor_tensor(out=T.m2[:, 0:N], in0=T.mean[:, 0:N], in1=T.mean[:, 0:N], op=ALU.mult),
              reads=['mean'], writes=['m2'])
        P.add('dve', lambda e: e.scalar_tensor_tensor(out=T.m2[:, 0:N], in0=bsq[:, 0:N], scalar=1.0 / 1024, in1=T.m2[:, 0:N],
                                                      op0=ALU.mult, op1=ALU.subtract), reads=[bsqn, 'm2'], writes=['m2'])
        P.add('dve', lambda e: e.tensor_scalar(out=T.m2[:, 0:N], in0=T.m2[:, 0:N], scalar1=EPS, scalar2=None, op0=ALU.add),
              reads=['m2'], writes=['m2'])
        P.add('act', lambda e: e.activation(out=T.rstdL[:, 0:N], in_=T.m2[:, 0:N], func=AF.Ln), reads=['m2'], writes=['rstdL'])
        P.add('act', lambda e: e.activation(out=T.rstdL[:, 0:N], in_=T.rstdL[:, 0:N], func=AF.Exp, scale=-0.5),
              reads=['rstdL'], writes=['rstdL'])
        held.discard(isum)
        held.discard(isq)

    def ln_chunk(j, N):
        i = rot('tl')
        tl = T.tl[i]
        tln = 'tl%d' % i
        P.add('dve', lambda e: e.tensor_tensor(out=tl[:, 0:N], in0=T.ybf[:, j, 0:N], in1=T.mean[:, 0:N], op=ALU.subtract),
              reads=['ybf.%d' % j, 'mean'], writes=[tln])
        P.add('dve', lambda e: e.tensor_tensor(out=tl[:, 0:N], in0=tl[:, 0:N], in1=T.rstdL[:, 0:N], op=ALU.mult),
              reads=[tln, 'rstdL'], writes=[tln])
        P.add('dve', lambda e: e.tensor_scalar(out=tl[:, 0:N], in0=tl[:, 0:N], scalar1=T.prm[:, j, 6:7],
                                               scalar2=T.prm[:, j, 7:8], op0=ALU.mult, op1=ALU.add),
              reads=[tln, 'prm'], writes=[tln])
        P.add('act', lambda e: e.activation(out=T.m2[:, 0:N], in_=tl[:, 0:N], func=AF.Sigmoid), reads=[tln], writes=['m2'])
        P.add('dve', lambda e: e.tensor_tensor(out=T.sT[:, j, 0:N], in0=tl[:, 0:N], in1=T.m2[:, 0:N], op=ALU.mult),
              reads=[tln, 'm2'], writes=['sT.%d' % j])

    def mix_stage(N, ucol, which, src=None, srcn=None, pre=None):
        wsrc = D.w_o_attn if which == 'a' else D.w_conv_out
        gcol = 3584 if which == 'a' else 4608
        if src is None:
            src = T.aoT if which == 'a' else T.sT
            srcn = cn('sT')
        for half in range(2):
            s1 = W.next(g_cols(wsrc, half * 512))
            s2 = W.next(g_cols(D.w_in, gcol + half * 512))
            w1, w2 = wslot(s1), wslot(s2)
            for jj in range(4):
                j = half * 4 + jj
                if pre is not None:
                    pre(j)
                b1, b1n, _ = bank()
                b2, b2n, _ = bank()

                def mm(e, b1=b1, b2=b2, jj=jj, w1=w1, w2=w2):
                    for kc in range(8):
                        e.matmul(b1[:, 0:N], lhsT=w1[:, kc, jj * 128:(jj + 1) * 128], rhs=src[:, kc, 0:N], start=(kc == 0),
                                 stop=(kc == 7))
                    for kc in range(8):
                        ins = e.matmul(b2[:, 0:N], lhsT=w2[:, kc, jj * 128:(jj + 1) * 128], rhs=T.uT[:, kc, ucol:ucol + N],
                                       start=(kc == 0), stop=(kc == 7))
                    return ins
                P.add('pe', mm, reads=srcn + cn('uT') + wnames(s1) + wnames(s2), writes=[b1n, b2n])
                i = rot('sg')
                sg = T.sg[i]
                sgn = 'sg%d' % i
                P.add('act', lambda e, sg=sg, b2=b2: e.activation(out=sg[:, 0:N], in_=b2[:, 0:N], func=AF.Sigmoid),
                      reads=[b2n], writes=[sgn])
                if which == 'a':
                    P.add('dve', lambda e, sg=sg, b1=b1, j=j: e.tensor_tensor(out=T.mixT[:, j, 0:N], in0=b1[:, 0:N],
                                                                              in1=sg[:, 0:N], op=ALU.mult),
                          reads=[b1n, sgn], writes=['mixT.%d' % j])
                else:
                    P.add('dve', lambda e, sg=sg, b1=b1, j=j: e.scalar_tensor_tensor(
                        out=sg[:, 0:N], in0=b1[:, 0:N], scalar=T.prm[:, j, 8:9], in1=sg[:, 0:N], op0=ALU.add, op1=ALU.mult),
                        reads=[b1n, sgn, 'prm'], writes=[sgn])
                    P.add('dve', lambda e, sg=sg, j=j: e.tensor_tensor(out=T.mixT[:, j, 0:N], in0=sg[:, 0:N],
                                                                       in1=T.mixT[:, j, 0:N], op=ALU.add),
                          reads=[sgn, 'mixT.%d' % j], writes=['mixT.%d' % j])
            W.release(s1)
            W.release(s2)

    def wout_stage(Pn, nsub):
        for nh in range(2):
            slot = W.next(g_cols(D.w_out, nh * 512))
            w = wslot(slot)
            for s in range(nsub):
                bk, bkn, _ = bank()

                def mm(e, bk=bk, s=s, w=w):
                    for kc in range(8):
                        ins = e.matmul(bk[:Pn, :], lhsT=T.mixT[:, kc, s * Pn:(s + 1) * Pn], rhs=w[:, kc, :], start=(kc == 0),
                                       stop=(kc == 7))
                    return ins
                P.add('pe', mm, reads=cn('mixT') + wnames(slot), writes=[bkn])
                xv = T.xbuf[:Pn, s, nh * 512:(nh + 1) * 512]
                P.add('dve', lambda e, bk=bk, xv=xv: e.tensor_tensor(out=xv, in0=bk[:Pn, :], in1=xv, op=ALU.add),
                      reads=[bkn, 'x.%d' % s], writes=['x.%d' % s])
            W.release(slot)

    def ffn_stage(Pn, nsub, ucol):
        N = Pn * nsub
        for g in range(8):
            slot = W.next(g_cols(D.w_ff1, g * 512))
            w = wslot(slot)
            for jj in range(4):
                f = 4 * g + jj
                bk, bkn, _ = bank()

                def mm(e, bk=bk, jj=jj, w=w):
                    for kc in range(8):
                        ins = e.matmul(bk[:, 0:N], lhsT=w[:, kc, jj * 128:(jj + 1) * 128], rhs=T.uT[:, kc, ucol:ucol + N],
                                       start=(kc == 0), stop=(kc == 7))
                    return ins
                P.add('pe', mm, reads=cn('uT') + wnames(slot), writes=[bkn])
                i = rot('sg')
                rl = T.sg[i]
                rln = 'sg%d' % i
                P.add('act', lambda e, rl=rl, bk=bk: e.activation(out=rl[:, 0:N], in_=bk[:, 0:N], func=AF.Relu),
                      reads=[bkn], writes=[rln])
                P.add('dve', lambda e, rl=rl, f=f: e.tensor_tensor(out=T.hid[:, f, 0:N], in0=rl[:, 0:N], in1=rl[:, 0:N],
                                                                   op=ALU.mult), reads=[rln], writes=['hid'])
            W.release(slot)
        for nh in range(2):
            bks = [bank() for _ in range(nsub)]
            for fg in range(4):
                slot = W.next(g_ff2(fg, nh))
                w = wslot(slot)
                for s in range(nsub):
                    bk, bkn, _ = bks[s]

                    def mm(e, bk=bk, s=s, w=w, fg=fg):
                        for fc in range(8):
                            ins = e.matmul(bk[:Pn, :], lhsT=T.hid[:, fg * 8 + fc, s * Pn:(s + 1) * Pn], rhs=w[:, fc, :],
                                           start=(fg == 0 and fc == 0), stop=(fg == 3 and fc == 7))
                        return ins
                    P.add('pe', mm, reads=['hid'] + wnames(slot), writes=[bkn])
                W.release(slot)
            for s in range(nsub):
                bk, bkn, _ = bks[s]
                xv = T.xbuf[:Pn, s, nh * 512:(nh + 1) * 512]
                P.add('dve', lambda e, bk=bk, xv=xv: e.tensor_tensor(out=xv, in0=bk[:Pn, :], in1=xv, op=ALU.add),
                      reads=[bkn, 'x.%d' % s], writes=['x.%d' % s])

    def ple_stage(Pn, nsub, ucol, psrc, ydst_fn, xnext_fn):
        P.add('sp', lambda e: e.dma_start(out=T.pbuf[:Pn, 0:nsub, :], in_=psrc), writes=['tl0', 'tl1'], dma='pbuf')
        P.add('dve', lambda e: e.tensor_copy(out=T.pbf[:Pn, 0:nsub, :], in_=T.pbuf[:Pn, 0:nsub, :]), reads=['tl0', 'tl1'],
              writes=['pbf'])
        bk, bkn, _ = bank()
        pb = psb(bk)
        N = Pn * nsub

        def tr(e):
            for c in range(2):
                for s in range(nsub):
                    ins = e.transpose(out=pb[:, c * 512 + s * Pn:c * 512 + (s + 1) * Pn],
                                      in_=T.pbf[:Pn, s, c * 128:(c + 1) * 128], identity=T.ident[:Pn, :Pn])
            return ins
        P.add('pe', tr, reads=['pbf', 'ident'], writes=[bkn])
        P.add('act', lambda e: e.activation(out=T.pT[:, :, 0:N], in_=pb.rearrange("p (c t) -> p c t", c=2)[:, :, 0:N],
                                            func=AF.Copy), reads=[bkn], writes=['pT'])
        sp_ = W.next(g_ple())
        wp = wslot(sp_, 2, 1024)
        gs = [W.next(g_cols(D.w_ple_gate, nh * 512)) for nh in range(2)]
        for s in range(nsub):
            yb = T.ybuf[s % 2]
            ybn = 'diag%d' % (s % 2)
            for nh in range(2):
                slot = gs[nh]
                w = wslot(slot)
                bg, bgn, _ = bank()
                bp, bpn, _ = bank()

                def mm(e, bg=bg, bp=bp, s=s, w=w, nh=nh):
                    for kc in range(8):
                        e.matmul(bg[:Pn, :], lhsT=T.uT[:, kc, ucol + s * Pn:ucol + (s + 1) * Pn], rhs=w[:, kc, :],
                                 start=(kc == 0), stop=(kc == 7))
                    for c in range(2):
                        ins = e.matmul(bp[:Pn, :], lhsT=T.pT[:, c, s * Pn:(s + 1) * Pn], rhs=wp[:, c, nh * 512:(nh + 1) * 512],
                                       start=(c == 0), stop=(c == 1))
                    return ins
                P.add('pe', mm, reads=cn('uT') + ['pT'] + wnames(slot) + wnames(sp_), writes=[bgn, bpn])
                i = rot('sg')
                sg = T.sg[i]
                sgn = 'sg%d' % i
                P.add('act', lambda e, sg=sg, bg=bg: e.activation(out=sg[:Pn, :], in_=bg[:Pn, :], func=AF.Sigmoid),
                      reads=[bgn], writes=[sgn])
                P.add('dve', lambda e, sg=sg, bp=bp: e.tensor_tensor(out=sg[:Pn, :], in0=bp[:Pn, :], in1=sg[:Pn, :],
                                                                     op=ALU.mult), reads=[bpn, sgn], writes=[sgn])
                xv = T.xbuf[:Pn, s, nh * 512:(nh + 1) * 512]
                P.add('dve', lambda e, sg=sg, xv=xv, yb=yb, nh=nh: e.tensor_tensor(out=yb[:Pn, nh * 512:(nh + 1) * 512],
                                                                               in0=sg[:Pn, :], in1=xv, op=ALU.add),
                      reads=[sgn, 'x.%d' % s], writes=[ybn])
            P.add('sp', lambda e, s=s, yb=yb: e.dma_start(out=ydst_fn(s), in_=yb[:Pn, :]), reads=[ybn], dma='yst')
            if xnext_fn is not None:
                P.add('sp', lambda e, s=s: e.dma_start(out=T.xbuf[:, s, :], in_=xnext_fn(s)), writes=['x.%d' % s],
                      dma='x%d' % s)
                if s >= 1:
                    rms_head_sub(lambda q: T.xbuf[:, q, :], ['x.%d' % (s - 1)], 128, s - 1)
        if xnext_fn is not None:
            rms_head_sub(lambda q: T.xbuf[:, q, :], ['x.%d' % (nsub - 1)], 128, nsub - 1)
        for slot in gs:
            W.release(slot)
        W.release(sp_)

    def a32_out(nrows, dst, r0):
        for hh in range(2):
            bk, bkn, _ = bank()

            def tr(e, bk=bk, hh=hh):
                for jj in range(4):
                    ins = e.transpose(out=bk[:nrows, jj * 128:(jj + 1) * 128], in_=T.a32[:, hh * 4 + jj, 0:nrows],
                                      identity=T.identf[:, :])
                return ins
            P.add('pe', tr, reads=['a32', 'identf'], writes=[bkn])
            P.add('dve', lambda e, bk=bk, hh=hh: e.tensor_copy(out=T.qkf[:nrows, hh * 512:(hh + 1) * 512], in_=bk[:nrows, :]),
                  reads=[bkn], writes=['qkf'])
        P.add('sp', lambda e: e.dma_start(out=dst, in_=T.qkf[r0:nrows, 0:1024]), reads=['qkf'], dma='qkfo')

    for it in range(NT_RUN):
        last = (it == NT - 1)
        r0 = 128 + it * TT
        if it == 0:
            for s in range(4):
                P.add('sp', lambda e, s=s: e.dma_start(out=T.xbuf[:, s, :], in_=D.x[128 + s * 128:256 + s * 128, :]),
                      writes=['x.%d' % s], dma='x%d' % s)
        P.add('sp', lambda e, it=it: e.dma_start(out=T.cos[:, 1:5, :], in_=D.cos[:, 1 + 4 * it:5 + 4 * it, :]), writes=['cos'], dma='cos')
        P.add('sp', lambda e, it=it: e.dma_start(out=T.sin[:, 1:5, :], in_=D.sin[:, 1 + 4 * it:5 + 4 * it, :]), writes=['sin'], dma='sin')
        if it == 0:
            P.add('sp', lambda e: e.dma_start(out=T.xh[:, :], in_=D.x[0:128, :]), writes=['sg0', 'sg1'], dma='xh')
            rms_T(lambda s: T.xh[:, :], [['sg0', 'sg1']], 128, 1, 0, T.uT, cn('uT'), 0)
        rms_T(lambda s: T.xbuf[:, s, :], cn('x', 4), 128, 4, 0, T.uT, cn('uT'), 128, heads_done=(it > 0))
        if STOP_STAGE == 1:
            return
        sq0 = W.next(g_cols(D.w_in, 0))
        sq1 = W.next(g_cols(D.w_in, 512))
        skv = W.next(g_cols(D.w_in, 1024))
        slots = (sq0, sq1, skv)
        if it == 0:
            qkv_sub(128, 0, 0, slots, 16, None, 'kT.0', 0, 0, False)
            P.add('dve', lambda e: e.tensor_scalar(out=T.Vaug[:, 0, :, :], in0=T.Vaug[:, 0, :, :], scalar1=T.flag[:, 0:1],
                                                   scalar2=None, op0=ALU.mult), reads=['V.0', 'flag'], writes=['V.0'])
        if STOP_STAGE == 11:
            return
        gmode = 'main0' if it == 0 else 'main'
        conv_begin()
        for s in range(4):
            A = s - 1
            diag_build(2 * s)
            diag_build(2 * s + 1)
            glu_group(s, TT, 128, gmode, last)
            if A >= 0:
                attn_sc(A, 0)
            qkv_sub(128, 128 + s * 128, 1 + s, slots, 0, s * 128, 'kT.%d' % (s + 1), 128 + s * 128, s + 1,
                    last and s == 3, part='a')
            if s == 3:
                for sl_ in slots:
                    W.release(sl_)
            if A >= 0:
                attn_pv(A, 0)
                attn_sc(A, 1)
            conv_mm(2 * s, TT, False)
            if A >= 0:
                attn_pv(A, 1)
                attn_sc(A, 2)
            conv_mm(2 * s + 1, TT, False)
            if A >= 0:
                attn_pv(A, 2)
                attn_sc(A, 3)
            qkv_sub(128, 128 + s * 128, 1 + s, slots, 0, s * 128, 'kT.%d' % (s + 1), 128 + s * 128, s + 1,
                    last and s == 3, part='b')
            if A >= 0:
                attn_pv(A, 3)
                attn_fin(A)
        attn_block(3)
        if last:
            P.add('sp', lambda e: e.dma_start(out=D.kp, in_=T.kvout[:, 0:256]), reads=['kvout.k'], dma='kvo')
            P.add('sp', lambda e: e.dma_start(out=D.vp, in_=T.kvout[:, 256:512]), reads=['kvout.v'], dma='kvo')
        ln_head(TT)
        mix_stage(TT, 128, 'a', src=T.aoTp, srcn=['aoTp'], pre=lambda j: ln_chunk(j, TT))
        if last:
            a32_out(32, D.cp, 2)
        mix_stage(TT, 128, 'c')
        if STOP_STAGE == 6:
            return
        wout_stage(128, 4)
        if STOP_STAGE == 7:
            return
        if not last:
            P.add('dve', lambda e: e.tensor_copy(out=T.kT[:, :, 0:128], in_=T.kT[:, :, 512:640]), reads=['kT.4'], writes=['kT.0'])
            P.add('dve', lambda e: e.tensor_copy(out=T.Vaug[:, 0, :, :], in_=T.Vaug[:, 4, :, :]), reads=['V.4'], writes=['V.0'])
            P.add('dve', lambda e: e.tensor_copy(out=T.aT[:, :, 0:30], in_=T.aT[:, :, 512:542]), reads=['aT.m%d' % j for j in range(8)], writes=['aT.h'])
        rms_T(lambda s: T.xbuf[:, s, :], cn('x', 4), 128, 4, 1, T.uT, cn('uT'), 128)
        ffn_stage(128, 4, 128)
        if STOP_STAGE == 8:
            return
        rms_T(lambda s: T.xbuf[:, s, :], cn('x', 4), 128, 4, 2, T.uT, cn('uT'), 128)
        p0 = it * TT
        r1 = 128 + (it + 1) * TT
        ple_stage(128, 4, 128, D.p[p0:p0 + TT, :].rearrange("(s p) n -> p s n", p=128),
                  lambda s, p0=p0: D.y[p0 + s * 128:p0 + (s + 1) * 128, :],
                  (lambda s, r1=r1: D.x[r1 + s * 128:r1 + (s + 1) * 128, :]) if it + 1 < NT_RUN else None)

    if not RUN_SAMPLE:
        return
    NS = 16
    P.add('sp', lambda e: e.dma_start(out=T.xbuf[:NS, 0, :], in_=D.xs), writes=['x.0'], dma='x0')
    P.add('sp', lambda e: e.dma_start(out=T.cos[:, 0, :], in_=D.cos[:, 33, :]), writes=['cos'], dma='cos')
    P.add('sp', lambda e: e.dma_start(out=T.sin[:, 0, :], in_=D.sin[:, 33, :]), writes=['sin'], dma='sin')
    rms_T(lambda s: T.xbuf[:NS, 0, :], ['x.0'], NS, 1, 0, T.uT, cn('uT'), 128)
    sq0 = W.next(g_cols(D.w_in, 0))
    sq1 = W.next(g_cols(D.w_in, 512))
    skv = W.next(g_cols(D.w_in, 1024))
    qkv_sub(NS, 128, 0, (sq0, sq1, skv), 0, 0, None, 0, 0, True)
    for sl_ in (sq0, sq1, skv):
        W.release(sl_)
    P.add('sp', lambda e: e.dma_start(out=D.ks[:, 127, :], in_=T.kvout[:NS, 0:256]), reads=['kvout.k'], writes=['ks.B'], dma='ksB')
    P.add('sp', lambda e: e.dma_start(out=D.vs[:, 127, :], in_=T.kvout[:NS, 256:512]), reads=['kvout.v'], writes=['vs.B'], dma='vsB')
    ksv = D.ks.rearrange("b j (h d) -> j b h d", h=4)
    P.add('pool', lambda e: e.dma_start(out=T.Kds[:, :, :, :], in_=ksv), reads=['ks.A', 'ks.B'],
          writes=['aT.m%d' % j for j in range(8)] + ['aT.h'], dma='kds')
    P.add('pool', lambda e: e.memset(T.Vs[:, :, :, 64:65], 1.0), writes=['Vs'])
    vsv = D.vs.rearrange("b j (h d) -> j b h d", h=4)
    for hk in range(4):
        P.add('pool', lambda e, hk=hk: e.dma_start(out=T.Vs[:, :, hk, 0:64], in_=vsv[:, :, hk, :]),
              reads=['vs.A', 'vs.B'], writes=['Vs'], dma='vsr')
    P.add('pool', lambda e: e.memset(T.Pexp[:], 0.0), writes=cn('ybf'))
    bS, bSn, iS = bank()
    held.add(iS)
    bS4 = bS[:, 0:256].rearrange("p (b h g) -> p b h g", b=16, h=4)
    for b in range(NS):
        bk, bkn, _ = bank()
        pb = psb(bk)

        P.add('dve', lambda e, b=b: e.tensor_copy(out=T.kdbf[:, :, :, :],
                                                  in_=T.Kds[:, b, :, :].unsqueeze(2).to_broadcast([128, 4, 2, 64])),
              reads=['aT.m%d' % j for j in range(8)] + ['aT.h'], writes=['kdbf'])

        def trk(e, b=b, pb=pb):
            kf = T.kdbf[:, :, :, :].rearrange("p h t d -> p (h t d)")
            for hk in range(4):
                ins = e.transpose(out=pb[:, hk * 128:(hk + 1) * 128], in_=kf[:, hk * 128:(hk + 1) * 128],
                                  identity=T.ident[:, :])
            return ins
        P.add('pe', trk, reads=['kdbf', 'ident'], writes=[bkn])
        i = rot('ev')
        kt = T.KTs[i]
        ktn = 'pT'
        P.add('dve', lambda e, kt=kt, pb=pb: e.tensor_copy(out=kt[:].rearrange("p h k -> p (h k)"), in_=pb[:, 0:512]),
              reads=[bkn], writes=[ktn])

        def sc(e, b=b, kt=kt):
            for par in (0, 1):
                if par == 1:
                    e.matmul(bS4[:, b, 0, 1:2], lhsT=T.ident[:, :], rhs=T.ident[:, 0:1], start=True, stop=True)
                for hk in range(4):
                    ins = e.matmul(bS4[:, b, hk, par::2], lhsT=kt[64 * par:64 * par + 64, hk, :],
                                   rhs=T.qT[64 * par:64 * par + 64, 2 * hk:2 * hk + 2, b], start=True, stop=True)
            return ins
        P.add('pe', sc, reads=[ktn, 'qT', 'ident'], writes=[bSn])
    P.add('act', lambda e: e.activation(out=T.Pexp[:].rearrange("p h b c -> p h (b c)")[:, :, ::17],
                                        in_=bS[:, 0:256].rearrange("p (b h) -> p h b", b=16), func=AF.Exp),
          reads=[bSn], writes=cn('ybf'))
    held.discard(iS)
    for hk in range(4):
        bO, bOn, _ = bank()

        def pvm(e, hk=hk, bO=bO):
            for g in range(4):
                for b in range(NS):
                    ins = e.matmul(bO[:NS, g * 65:(g + 1) * 65], lhsT=T.Pexp[:, 4 * hk + g, b, :], rhs=T.Vs[:, b, hk, :],
                                   start=(b == 0), stop=(b == NS - 1))
            return ins
        P.add('pe', pvm, reads=cn('ybf') + ['Vs'], writes=[bOn])
        normalize_heads(bO, bOn, hk, NS)
    ao_transpose(NS, 0)
    mix_stage(NS, 128, 'a')
    stv = D.st.rearrange("(i bb) k c -> i (bb k) c", i=4)
    for i4 in range(4):
        P.add('sp', lambda e, i4=i4: e.dma_start(out=T.xh[:120, :], in_=stv[i4]), writes=['sg0', 'sg1'], dma='xh')
        P.add('dve', lambda e, i4=i4: e.tensor_copy(out=T.xn[:120, i4, :], in_=T.xh[:120, :]), reads=['sg0', 'sg1'],
              writes=xnn(i4))
    for i4 in range(4):
        bk, bkn, _ = bank()
        pb = psb(bk)

        def trs(e, i4=i4, pb=pb):
            for c in range(8):
                ins = e.transpose(out=pb[:, c * 120:(c + 1) * 120], in_=T.xn[:120, i4, c * 128:(c + 1) * 128],
                                  identity=T.ident[:120, :120])
            return ins
        P.add('pe', trs, reads=cn('xn', 4) + ['ident'], writes=[bkn])
        P.add('dve', lambda e, i4=i4, pb=pb: e.tensor_copy(
            out=T.histT[:, :, 4 * i4:4 * i4 + 4, 0:30], in_=pb[:, 0:960].rearrange("p (c b k) -> p c b k", c=8, b=4)),
            reads=[bkn], writes=['qT'])
    glu_stage(NS, 128, 'sample', False)
    conv_ln_stage(NS, True)
    a32_out(NS, D.cs[:, 29, :], 0)
    mix_stage(NS, 128, 'c')
    wout_stage(NS, 1)
    rms_T(lambda s: T.xbuf[:NS, 0, :], ['x.0'], NS, 1, 1, T.uT, cn('uT'), 128)
    ffn_stage(NS, 1, 128)
    rms_T(lambda s: T.xbuf[:NS, 0, :], ['x.0'], NS, 1, 2, T.uT, cn('uT'), 128)
    ple_stage(NS, 1, 128, D.psamp.rearrange("(s p) n -> p s n", s=1), lambda s: D.ys, None)


def build_program():
    nc = bass.Bass("TRN2", target_bir_lowering=False)
    D = TT_()

    def din(name, shape):
        setattr(D, name, nc.dram_tensor(name, shape, F32, kind="ExternalInput").ap())

    def dout(name, shape):
        setattr(D, name, nc.dram_tensor(name, shape, F32, kind="ExternalOutput").ap())
    din("x", [128 + NT * TT, 1024]); din("p", [NT * TT, 256]); din("flag", [128, 1])
    din("cos", [128, 34, 32]); din("sin", [128, 34, 32])
    din("xs", [16, 1024]); din("psamp", [16, 256]); din("ck", [16, 128, 256]); din("cv", [16, 128, 256])
    din("st", [16, 30, 1024])
    for n, sh in [("ln1", [1024]), ("w_in", [1024, 5632]), ("b_glu", [2048]), ("q_norm", [64]), ("k_norm", [64]),
                  ("sinks", [16]), ("w_o_attn", [1024, 1024]), ("conv_dw", [31, 1024]), ("conv_dw_b", [1024]),
                  ("conv_ln_g", [1024]), ("conv_ln_b", [1024]), ("w_conv_out", [1024, 1024]), ("b_conv_out", [1024]),
                  ("w_out", [1024, 1024]), ("ln2", [1024]), ("w_ff1", [1024, 4096]), ("w_ff2", [4096, 1024]),
                  ("ln_ple", [1024]), ("w_ple_gate", [1024, 1024]), ("w_ple", [256, 1024])]:
        din(n, sh)
    dout("y", [NT * TT, 1024]); dout("ys", [16, 1024]); dout("kp", [128, 256]); dout("vp", [128, 256])
    dout("cp", [30, 1024]); dout("ks", [16, 128, 256]); dout("vs", [16, 128, 256]); dout("cs", [16, 30, 1024])

    rec = WRec()
    T0 = TT_()

    class _Any:
        def __getattr__(self, k):
            return _Any()

        def __getitem__(self, k):
            return _Any()

        def __call__(self, *a, **k):
            return _Any()
    for nme in ['ps', 'W', 'sg', 'PT', 'diag', 'y2', 'tl', 'rl', 'KTs']:
        setattr(T0, nme, [_Any() for _ in range(8)])

    class _T0(TT_):
        def __getattr__(self, k):
            return _Any()
    T0d = _T0()
    T0d.ps = T0.ps; T0d.W = T0.W; T0d.sg = T0.sg; T0d.PT = T0.PT; T0d.diag = T0.diag; T0d.y2 = T0.y2
    T0d.tl = T0.tl; T0d.rl = T0.rl; T0d.KTs = T0.KTs
    emit_all(DryProg(), rec, T0d, D)

    with ExitStack() as es:
        T = TT_()

        def sb(name, shape, dt):
            t = es.enter_context(nc.sbuf_tensor("sb_" + name, shape, dt))
            setattr(T, name, t)
            return t
        sb("identf", [128, 128], F32); sb("ident", [128, 128], BF16); sb("ones", [128, 128], BF16)
        sb("neghalf", [128, 32], F32); sb("maskf", [128, 2, 128], F32); sb("mask", [128, 2, 128], BF16)
        sb("cos", [128, 5, 32], F32); sb("sin", [128, 5, 32], F32); sb("flag", [128, 1], F32)
        sb("gq", [128, 64], F32); sb("gk", [128, 64], F32); sb("esink", [128, 16], F32)
        sb("prm", [128, 8, 40], F32); sb("wdw", [128, 8, 31], BF16)
        sb("gfull", [128, 20, 64], F32)
        sb("xbuf", [128, 4, 1024], F32)
        sb("ss", [128, 4], F32); sb("ms", [128, 4], F32); sb("rstd", [128, 4], F32)
        sb("uT", [128, 8, 640], BF16)
        sb("ssqk", [128, 20], F32); sb("msqk", [128, 20], F32); sb("rqk", [128, 20], F32)
        sb("qbf", [128, 16, 64], BF16); sb("kdbf", [128, 4, 2, 64], BF16); sb("kvout", [128, 512], F32)
        sb("kT", [128, 4, 640], BF16); sb("Vaug", [128, 5, 4, 65], BF16)
        T.PT = [sb("PT%d" % i, [128, 2, 512], BF16) for i in range(2)]
        sb("den", [128, 4], F32); sb("rden", [128, 4], F32); sb("ao", [128, 16, 64], BF16)
        sb("aT", [128, 8, 542], BF16); sb("a32", [128, 8, 32], F32)
        sg2 = sb("sg2", [128, 2, 512], F32)
        T.sg = [sg2[:, 0, :], sg2[:, 1, :]]
        T.xh = sg2[:].rearrange("p a b -> p (a b)")
        T.diag = [sb("diag%d" % i, [128, 31, 128], BF16) for i in range(2)]
        sb("ybf", [128, 8, 512], BF16)
        T.xn = T.ybf[:].rearrange("p j t -> p (j t)").rearrange("p (s n) -> p s n", s=4)
        y22 = sb("y22", [128, 2, 512], BF16)
        T.y2 = [y22[:, 0, :], y22[:, 1, :]]
        T.junk = y22[:].rearrange("p a b -> p (a b)")
        sb("ycs", [128, 16], F32)
        sb("mean", [128, 512], F32); sb("m2", [128, 512], F32); sb("rstdL", [128, 512], F32)
        tl2 = sb("tl2", [128, 2, 512], F32)
        T.tl = [tl2[:, 0, :], tl2[:, 1, :]]
        T.prows = tl2[:].rearrange("p a b -> p (a b)")
        T.pbuf = tl2[:].rearrange("p a b -> p (a b)").rearrange("p (s n) -> p s n", s=4)
        sb("sT", [128, 8, 512], BF16); sb("mixT", [128, 8, 512], BF16)
        T.aoT = T.sT
        sb("hid", [128, 32, 512], BF16)
        sb("pbf", [128, 4, 256], BF16); sb("pT", [128, 2, 512], BF16)
        T.KTs = [T.pT[:, i, :].rearrange("p (h k) -> p h k", h=4) for i in range(2)]
        T.W = [sb("W%d" % i, [128, 4096], BF16) for i in range(NSLOT)]
        T.ps = [es.enter_context(nc.psum_tensor("psum%d" % i, [128, 512], F32)) for i in range(8)]
        hflat = T.hid[:].rearrange("p f t -> p (f t)")
        T.qT = hflat[:, 0:4096].rearrange("p (c t) -> p c t", c=8)
        T.qkf = hflat[:, 4096:6656].bitcast(F32)
        T.tmpA = hflat[:, 6656:7936].bitcast(F32).rearrange("p (h d) -> p h d", h=20)
        T.qrot = hflat[:, 8192:10752].bitcast(F32)
        T.tmpB = hflat[:, 10752:12032].bitcast(F32).rearrange("p (h d) -> p h d", h=20)
        T.aoTp = hflat[:, 12032:16128].rearrange("p (c t) -> p c t", c=8)
        T.Vs = hflat[:, 12032:12032 + 4160].rearrange("p (b h d) -> p b h d", b=16, h=4)
        T.histT = hflat[:, 0:3968].rearrange("p (j b k) -> p j b k", j=8, b=16)
        T.Kds = T.aT[:].rearrange("p j t -> p (j t)")[:, 0:4096].rearrange("p (b h d) -> p b h d", b=16, h=4)
        T.ybuf = [T.diag[i][:].rearrange("p k c -> p (k c)")[:, 0:2048].bitcast(F32) for i in range(2)]
        T.Pexp = T.ybf[:].rearrange("p j t -> p (j t)").rearrange("p (h b c) -> p h b c", h=16, b=16)

        P = Prog(nc)
        ng = len(set(k for k, _ in rec.groups))
        scr = nc.dram_tensor("wscr", [max(ng, 1), 128, 4096], BF16, kind="Internal").ap()
        Wl = WLoader(P, T, rec.groups, scr, ng)
        emit_all(P, Wl, T, D)
        P.emit()
    return nc


_CACHE = {}


def _rope_tables(start):
    half = 32
    inv = np.power(10000.0, -np.arange(half, dtype=np.float64) / half)
    pos = (start - 128 + np.arange(33 * 128)).astype(np.float64)
    ang = pos[:, None] * inv[None, :]
    angs = (float(PAST_LEN) * inv)[None, :].repeat(128, 0)
    ang = ang.reshape(33, 128, 32).transpose(1, 0, 2)
    ang = np.concatenate([ang, angs[:, None, :]], axis=1)
    return np.ascontiguousarray(np.cos(ang).astype(np.float32)), np.ascontiguousarray(np.sin(ang).astype(np.float32))


def kernel(**inputs):
    in_maps = _prep(**inputs)
    if 'nc' not in _CACHE:
        _CACHE['nc'] = build_program()
    nc = _CACHE['nc']
    res = run_bass_kernel_spmd(nc, in_maps, core_ids=list(range(8)))
    return _assemble(res.results)


def _prep(x_prompt, x_sample, cache_k, cache_v, state_conv, p_prompt, p_sample,
          ln1, w_in, b_glu, q_norm, k_norm, sinks, w_o_attn, conv_dw, conv_dw_b,
          conv_ln_g, conv_ln_b, w_conv_out, b_conv_out, w_out, ln2, w_ff1, w_ff2,
          ln_ple, w_ple_gate, w_ple):
    f = lambda a: np.ascontiguousarray(np.asarray(a, dtype=np.float32))
    x_prompt, x_sample, cache_k, cache_v, state_conv, p_prompt, p_sample = map(
        f, (x_prompt, x_sample, cache_k, cache_v, state_conv, p_prompt, p_sample))
    wts = dict(ln1=f(ln1)[0], w_in=f(w_in)[0], b_glu=f(b_glu)[0], q_norm=f(q_norm)[0], k_norm=f(k_norm)[0],
               sinks=f(sinks)[0], w_o_attn=f(w_o_attn)[0], conv_dw=f(conv_dw)[0], conv_dw_b=f(conv_dw_b)[0],
               conv_ln_g=f(conv_ln_g)[0], conv_ln_b=f(conv_ln_b)[0], w_conv_out=f(w_conv_out)[0],
               b_conv_out=f(b_conv_out)[0], w_out=f(w_out)[0], ln2=f(ln2)[0], w_ff1=f(w_ff1)[0], w_ff2=f(w_ff2)[0],
               ln_ple=f(ln_ple)[0], w_ple_gate=f(w_ple_gate)[0], w_ple=f(w_ple)[0])
    in_maps = []
    L = NT * TT
    for c in range(8):
        b, half = c // 2, c % 2
        start = half * L
        xc = np.zeros((128 + L, 1024), np.float32)
        if half == 1:
            xc[:] = x_prompt[b, start - 128:start + L]
        else:
            xc[128:] = x_prompt[b, 0:L]
        cs_, sn_ = _rope_tables(start)
        m = dict(wts)
        m.update(x=xc, p=np.ascontiguousarray(p_prompt[0, b, start:start + L]),
                 flag=np.full((128, 1), float(half), np.float32), cos=cs_, sin=sn_,
                 xs=np.ascontiguousarray(x_sample[16 * c:16 * c + 16, 0]),
                 psamp=np.ascontiguousarray(p_sample[0, 16 * c:16 * c + 16, 0]),
                 ck=np.ascontiguousarray(cache_k[0, 16 * c:16 * c + 16].reshape(16, 128, 256)),
                 cv=np.ascontiguousarray(cache_v[0, 16 * c:16 * c + 16].reshape(16, 128, 256)),
                 st=np.ascontiguousarray(state_conv[0, 16 * c:16 * c + 16]))
        in_maps.append(m)
    return in_maps


def _assemble(R):
    L = NT * TT
    y_prompt = np.zeros((4, 2 * L, 1024), np.float32)
    y_sample = np.zeros((128, 1, 1024), np.float32)
    nkp = np.zeros((1, 4, 128, 4, 64), np.float32)
    nvp = np.zeros((1, 4, 128, 4, 64), np.float32)
    ncp = np.zeros((1, 4, 30, 1024), np.float32)
    nks = np.zeros((1, 128, 128, 4, 64), np.float32)
    nvs = np.zeros((1, 128, 128, 4, 64), np.float32)
    ncs = np.zeros((1, 128, 30, 1024), np.float32)
    for c in range(8):
        b, half = c // 2, c % 2
        r = R[c]
        y_prompt[b, half * L:(half + 1) * L] = r["y"]
        y_sample[16 * c:16 * c + 16, 0] = r["ys"]
        if half == 1:
            nkp[0, b] = r["kp"].reshape(128, 4, 64)
            nvp[0, b] = r["vp"].reshape(128, 4, 64)
            ncp[0, b] = r["cp"]
        nks[0, 16 * c:16 * c + 16] = r["ks"].reshape(16, 128, 4, 64)
        nvs[0, 16 * c:16 * c + 16] = r["vs"].reshape(16, 128, 4, 64)
        ncs[0, 16 * c:16 * c + 16] = r["cs"]
    return (y_prompt, y_sample, nkp, nvp, ncp, nks, nvs, ncs)
```
